# Optimizing a Trainium2 kernel written in Bass

```python
import math
import jax, jax.numpy as jnp
from jax import lax
import numpy as np

D_MODEL = 1024
BATCH = 8
SEQ = 2048
DEPTH = 1
DEC_BATCH = 8
DEC_SEQ = 64
PAST_LEN = 4096

CHUNK = 64
N_META = 16
SSD_HEADS = 16
SSD_HEAD_DIM = 64
SSD_INNER = SSD_HEADS * SSD_HEAD_DIM
SSD_GROUPS = 4
SSD_STATE = 128
SSD_BLOCK = 64
CONV_WIDTH = 4
CONV_DIM = SSD_INNER + 2 * SSD_GROUPS * SSD_STATE
N_HEADS = 8
N_KV_HEADS = 2
HEAD_DIM = 128
ATT_INNER = N_HEADS * HEAD_DIM
KV_DIM = N_KV_HEADS * HEAD_DIM
IDX_HEADS = 8
IDX_DIM = 64
TOPK_MAX = 256
Q_BLOCK = 128
REL_BUCKETS = 32
REL_MAX_DIST = 128
D_FF = -(-8 * D_MODEL // (3 * 256)) * 256
IN_WIDTHS = (SSD_INNER, CONV_DIM, SSD_HEADS, ATT_INNER, KV_DIM, KV_DIM, IDX_HEADS * IDX_DIM, IDX_DIM, IDX_HEADS, D_MODEL, D_MODEL)
IN_DIM = sum(IN_WIDTHS)
EPS = 1e-6

kernel_name = 'hybrid_ssd_dsa_streaming_step'


def rmsnorm(x, w):
    xf = x.astype(jnp.float32)
    y = xf * lax.rsqrt(jnp.mean(xf * xf, axis=-1, keepdims=True) + EPS)
    return (y * w.astype(jnp.float32)).astype(x.dtype)


def split_columns(h):
    outs = []
    o = 0
    for w in IN_WIDTHS:
        outs.append(h[..., o:o + w])
        o += w
    return outs


def t5_bucket(rel):
    nb = REL_BUCKETS // 2
    max_exact = nb // 2
    ret = jnp.where(rel > 0, nb, 0)
    n = jnp.abs(rel)
    nf = jnp.maximum(n, 1).astype(jnp.float32)
    large = max_exact + (jnp.log(nf / max_exact) / math.log(REL_MAX_DIST / max_exact) * (nb - max_exact)).astype(jnp.int32)
    large = jnp.minimum(large, nb - 1)
    return ret + jnp.where(n < max_exact, n, large)


def causal_conv(xbc, prev, w, b):
    T = xbc.shape[1]
    xpad = jnp.concatenate([prev.astype(xbc.dtype), xbc], axis=1)
    y = b + sum(xpad[:, i:i + T] * w[i] for i in range(CONV_WIDTH))
    return jax.nn.silu(y), xpad[:, xpad.shape[1] - (CONV_WIDTH - 1):]


def ssd_scan(x, dt, a, bm, cm, s0):
    Bsz, T = x.shape[:2]
    pad = (-T) % SSD_BLOCK
    nc = (T + pad) // SSD_BLOCK
    R = SSD_HEADS // SSD_GROUPS

    def blocks(u, tail):
        u = jnp.pad(u.astype(jnp.float32), [(0, 0), (0, pad)] + [(0, 0)] * (u.ndim - 2))
        return u.reshape((Bsz, nc, SSD_BLOCK) + tail)

    x = blocks(x, (SSD_GROUPS, R, SSD_HEAD_DIM))
    dt = blocks(dt, (SSD_GROUPS, R))
    bm = blocks(bm, (SSD_GROUPS, SSD_STATE))
    cm = blocks(cm, (SSD_GROUPS, SSD_STATE))
    acum = jnp.cumsum(dt * a.astype(jnp.float32).reshape(SSD_GROUPS, R), axis=2)
    causal = jnp.tril(jnp.ones((SSD_BLOCK, SSD_BLOCK), bool))[:, :, None, None]
    seg = acum[:, :, :, None] - acum[:, :, None, :]
    lmat = jnp.exp(jnp.where(causal, seg, -jnp.inf))
    cb = jnp.einsum('bcign,bcjgn->bcijg', cm, bm)
    wmat = cb[..., None] * lmat * dt[:, :, None]
    y_diag = jnp.einsum('bcijgr,bcjgrp->bcigrp', wmat, x)
    decay_end = jnp.exp(acum[:, :, -1:] - acum)
    st = jnp.einsum('bcjgn,bcjgrp->bcgrpn', bm, x * (decay_end * dt)[..., None])
    blk_decay = jnp.exp(acum[:, :, -1])

    def step(s, inp):
        st_c, dec_c = inp
        return s * dec_c[..., None, None] + st_c, s

    s0 = s0.astype(jnp.float32).reshape(Bsz, SSD_GROUPS, R, SSD_HEAD_DIM, SSD_STATE)
    s_fin, s_in = lax.scan(step, s0, (jnp.moveaxis(st, 1, 0), jnp.moveaxis(blk_decay, 1, 0)))
    s_in = jnp.moveaxis(s_in, 0, 1)
    y_off = jnp.einsum('bcign,bcgrpn->bcigrp', cm, s_in) * jnp.exp(acum)[..., None]
    y = (y_diag + y_off).reshape(Bsz, nc * SSD_BLOCK, SSD_HEADS, SSD_HEAD_DIM)[:, :T]
    return y, s_fin.reshape(Bsz, SSD_HEADS, SSD_HEAD_DIM, SSD_STATE)


def ssd_mixer(z, xbc, dt_raw, conv_prev, ssm_prev, conv_w, conv_b, dt_bias, a_log, d_skip, norm_w):
    Bsz, T = z.shape[:2]
    xbc, conv_new = causal_conv(xbc, conv_prev, conv_w, conv_b)
    gn = SSD_GROUPS * SSD_STATE
    xs = xbc[..., :SSD_INNER].reshape(Bsz, T, SSD_HEADS, SSD_HEAD_DIM)
    bm = xbc[..., SSD_INNER:SSD_INNER + gn].reshape(Bsz, T, SSD_GROUPS, SSD_STATE)
    cm = xbc[..., SSD_INNER + gn:].reshape(Bsz, T, SSD_GROUPS, SSD_STATE)
    dt = jax.nn.softplus(dt_raw.astype(jnp.float32) + dt_bias.astype(jnp.float32))
    a = -jnp.exp(a_log.astype(jnp.float32))
    y, s_new = ssd_scan(xs, dt, a, bm, cm, ssm_prev)
    y = y + xs.astype(jnp.float32) * d_skip.astype(jnp.float32)[:, None]
    y = y.reshape(Bsz, T, SSD_INNER) * jax.nn.silu(z.astype(jnp.float32))
    return rmsnorm(y, norm_w).astype(z.dtype), conv_new, s_new.astype(ssm_prev.dtype)


def sparse_attn_block(q, qi, wi, qpos, qchunk, k_all, v_all, ki_all, kpos, kchunk, n_sel, rel_bias):
    Bsz, Tq = q.shape[:2]
    R = N_HEADS // N_KV_HEADS
    s_idx = jnp.einsum('bthd,bsd->bths', qi.astype(jnp.float32), ki_all.astype(jnp.float32)) * IDX_DIM ** -0.5
    score = jnp.einsum('bth,bths->bts', wi.astype(jnp.float32), jax.nn.relu(s_idx))
    admissible = kchunk[None, :] <= qchunk[:, None]
    score = jnp.where(admissible[None], score, -jnp.inf)
    _, idx = lax.top_k(score, n_sel)
    take = jax.vmap(lambda rows, ids: rows[ids])
    k_sel = take(k_all, idx)
    v_sel = take(v_all, idx)
    valid = kchunk[idx] <= qchunk[None, :, None]
    bias = rel_bias[t5_bucket(kpos[idx] - qpos[None, :, None])]
    bias = bias.reshape(Bsz, Tq, n_sel, N_KV_HEADS, R).transpose(0, 1, 3, 4, 2)
    qg = q.reshape(Bsz, Tq, N_KV_HEADS, R, HEAD_DIM)
    logits = jnp.einsum('btgrd,btngd->btgrn', qg, k_sel, preferred_element_type=jnp.float32) * HEAD_DIM ** -0.5 + bias.astype(jnp.float32)
    logits = jnp.where(valid[:, :, None, None, :], logits, -1e30)
    p = jax.nn.softmax(logits, axis=-1)
    out = jnp.einsum('btgrn,btngd->btgrd', p.astype(v_sel.dtype), v_sel)
    return out.reshape(Bsz, Tq, ATT_INNER)


def attend(q, qi, wi, qpos, qchunk, k_all, v_all, ki_all, kpos, kchunk, n_sel, rel_bias):
    Bsz, T = q.shape[:2]
    if T <= Q_BLOCK:
        return sparse_attn_block(q, qi, wi, qpos, qchunk, k_all, v_all, ki_all, kpos, kchunk, n_sel, rel_bias)
    pad = (-T) % Q_BLOCK
    nb = (T + pad) // Q_BLOCK

    def blk(u):
        u = jnp.pad(u, [(0, 0), (0, pad)] + [(0, 0)] * (u.ndim - 2))
        return jnp.moveaxis(u.reshape((Bsz, nb, Q_BLOCK) + u.shape[2:]), 1, 0)

    qpos_b = jnp.pad(qpos, (0, pad), mode='edge').reshape(nb, Q_BLOCK)
    qchunk_b = jnp.pad(qchunk, (0, pad), mode='edge').reshape(nb, Q_BLOCK)
    out = lax.map(lambda u: sparse_attn_block(u[0], u[1], u[2], u[3], u[4], k_all, v_all, ki_all, kpos, kchunk, n_sel, rel_bias),
                  (blk(q), blk(qi), blk(wi), qpos_b, qchunk_b))
    return jnp.moveaxis(out, 0, 1).reshape(Bsz, nb * Q_BLOCK, ATT_INNER)[:, :T]


def trunk_layer(x, conv_prev, ssm_prev, k_past, v_past, ki_past, qpos, qchunk, kpos, kchunk, n_sel, rel_bias,
                norm1_w, w_in, conv_w, conv_b, dt_bias, a_log, d_skip, ssd_norm_w, q_norm_w, k_norm_w, idx_k_norm_w,
                w_br_ssd, w_br_att, w_out, norm2_w, w_gate, w_up, w_down):
    Bsz, T, _ = x.shape
    hn = rmsnorm(x, norm1_w)
    z, xbc, dt_raw, q, k, v, qi, ki, wi, g_ssd, g_att = split_columns(hn @ w_in)
    y_ssd, conv_new, ssm_new = ssd_mixer(z, xbc, dt_raw, conv_prev, ssm_prev, conv_w, conv_b, dt_bias, a_log, d_skip, ssd_norm_w)
    q = rmsnorm(q.reshape(Bsz, T, N_HEADS, HEAD_DIM), q_norm_w)
    k = rmsnorm(k.reshape(Bsz, T, N_KV_HEADS, HEAD_DIM), k_norm_w)
    v = v.reshape(Bsz, T, N_KV_HEADS, HEAD_DIM)
    qi = qi.reshape(Bsz, T, IDX_HEADS, IDX_DIM)
    ki = rmsnorm(ki, idx_k_norm_w)
    wi = wi * IDX_HEADS ** -0.5
    if k_past is None:
        k_all, v_all, ki_all = k, v, ki
    else:
        k_all = jnp.concatenate([k_past.astype(k.dtype), k], axis=1)
        v_all = jnp.concatenate([v_past.astype(v.dtype), v], axis=1)
        ki_all = jnp.concatenate([ki_past.astype(ki.dtype), ki], axis=1)
    y_att = attend(q, qi, wi, qpos, qchunk, k_all, v_all, ki_all, kpos, kchunk, n_sel, rel_bias)
    merged = jax.nn.sigmoid(g_ssd) * (y_ssd @ w_br_ssd) + jax.nn.sigmoid(g_att) * (y_att @ w_br_att)
    h = x + merged @ w_out
    hn2 = rmsnorm(h, norm2_w)
    y = h + (jax.nn.silu(hn2 @ w_gate) * (hn2 @ w_up)) @ w_down
    return y, k, v, ki, ssm_new, conv_new


def setup_inputs(seed: int = 0) -> dict:
    key = jax.random.key(seed)
    ks = jax.random.split(key, 32)
    f32 = jnp.float32

    def nrm(k, shape, scale):
        return jax.random.normal(k, shape, f32) * scale

    dt0 = jnp.exp(jax.random.uniform(ks[13], (DEPTH, SSD_HEADS), f32, math.log(1e-3), math.log(1e-1)))
    return {
        'x_prompt': nrm(ks[0], (BATCH, SEQ, D_MODEL), 1.0),
        'x_sample': nrm(ks[1], (DEC_BATCH, DEC_SEQ, D_MODEL), 1.0),
        'cache_k': nrm(ks[2], (DEPTH, DEC_BATCH, PAST_LEN, N_KV_HEADS, HEAD_DIM), 1.0),
        'cache_v': nrm(ks[3], (DEPTH, DEC_BATCH, PAST_LEN, N_KV_HEADS, HEAD_DIM), 1.0),
        'cache_kidx': nrm(ks[4], (DEPTH, DEC_BATCH, PAST_LEN, IDX_DIM), 1.0),
        'state_ssm': nrm(ks[5], (DEPTH, DEC_BATCH, SSD_HEADS, SSD_HEAD_DIM, SSD_STATE), 0.2),
        'state_conv': nrm(ks[6], (DEPTH, DEC_BATCH, CONV_WIDTH - 1, CONV_DIM), 1.0),
        'meta_tokens': nrm(ks[7], (N_META, D_MODEL), 1.0),
        'rel_bias': nrm(ks[8], (REL_BUCKETS, N_HEADS), 0.5),
        'norm1_w': 1.0 + nrm(ks[9], (DEPTH, D_MODEL), 0.02),
        'w_in': nrm(ks[10], (DEPTH, D_MODEL, IN_DIM), D_MODEL ** -0.5),
        'conv_w': nrm(ks[11], (DEPTH, CONV_WIDTH, CONV_DIM), CONV_WIDTH ** -0.5),
        'conv_b': nrm(ks[12], (DEPTH, CONV_DIM), 0.02),
        'dt_bias': dt0 + jnp.log(-jnp.expm1(-dt0)),
        'a_log': jnp.log(jax.random.uniform(ks[14], (DEPTH, SSD_HEADS), f32, 1.0, 16.0)),
        'd_skip': 1.0 + nrm(ks[15], (DEPTH, SSD_HEADS), 0.1),
        'ssd_norm_w': 1.0 + nrm(ks[16], (DEPTH, SSD_INNER), 0.02),
        'q_norm_w': 1.0 + nrm(ks[17], (DEPTH, HEAD_DIM), 0.02),
        'k_norm_w': 1.0 + nrm(ks[18], (DEPTH, HEAD_DIM), 0.02),
        'idx_k_norm_w': 1.0 + nrm(ks[19], (DEPTH, IDX_DIM), 0.02),
        'w_br_ssd': nrm(ks[20], (DEPTH, SSD_INNER, D_MODEL), SSD_INNER ** -0.5),
        'w_br_att': nrm(ks[21], (DEPTH, ATT_INNER, D_MODEL), ATT_INNER ** -0.5),
        'w_out': nrm(ks[22], (DEPTH, D_MODEL, D_MODEL), D_MODEL ** -0.5),
        'norm2_w': 1.0 + nrm(ks[23], (DEPTH, D_MODEL), 0.02),
        'w_gate': nrm(ks[24], (DEPTH, D_MODEL, D_FF), D_MODEL ** -0.5),
        'w_up': nrm(ks[25], (DEPTH, D_MODEL, D_FF), D_MODEL ** -0.5),
        'w_down': nrm(ks[26], (DEPTH, D_FF, D_MODEL), D_FF ** -0.5),
    }


def reference(x_prompt, x_sample, cache_k, cache_v, cache_kidx, state_ssm, state_conv, meta_tokens, rel_bias,
              norm1_w, w_in, conv_w, conv_b, dt_bias, a_log, d_skip, ssd_norm_w, q_norm_w, k_norm_w, idx_k_norm_w,
              w_br_ssd, w_br_att, w_out, norm2_w, w_gate, w_up, w_down):
    bp, sp = x_prompt.shape[:2]
    hp = jnp.concatenate([jnp.broadcast_to(meta_tokens.astype(x_prompt.dtype)[None], (bp, N_META, D_MODEL)), x_prompt], axis=1)
    pos_p = jnp.arange(N_META + sp)
    chunk_p = jnp.where(pos_p < N_META, 0, (pos_p - N_META) // CHUNK + 1)
    n_sel_p = min(TOPK_MAX, sp // 4)
    conv0 = jnp.zeros((bp, CONV_WIDTH - 1, CONV_DIM), x_prompt.dtype)
    ssm0 = jnp.zeros((bp, SSD_HEADS, SSD_HEAD_DIM, SSD_STATE), state_ssm.dtype)
    past = cache_k.shape[2]
    ts = x_sample.shape[1]
    kpos_s = jnp.arange(past + ts)
    kchunk_s = kpos_s // CHUNK
    n_sel_s = min(TOPK_MAX, (past + ts) // 4)
    hs = x_sample
    kp_l, vp_l, kip_l, sp_l, cp_l = [], [], [], [], []
    ks_l, vs_l, kis_l, ss_l, cs_l = [], [], [], [], []
    for l in range(DEPTH):
        lw = (norm1_w[l], w_in[l], conv_w[l], conv_b[l], dt_bias[l], a_log[l], d_skip[l], ssd_norm_w[l],
              q_norm_w[l], k_norm_w[l], idx_k_norm_w[l], w_br_ssd[l], w_br_att[l], w_out[l], norm2_w[l],
              w_gate[l], w_up[l], w_down[l])
        hp, kp, vp, kip, ssp, cvp = trunk_layer(hp, conv0, ssm0, None, None, None, pos_p, chunk_p, pos_p, chunk_p,
                                               n_sel_p, rel_bias, *lw)
        hs, kss, vss, kis, sss, cvs = trunk_layer(hs, state_conv[l], state_ssm[l], cache_k[l], cache_v[l], cache_kidx[l],
                                                 kpos_s[past:], kchunk_s[past:], kpos_s, kchunk_s, n_sel_s, rel_bias, *lw)
        kp_l.append(kp); vp_l.append(vp); kip_l.append(kip); sp_l.append(ssp); cp_l.append(cvp)
        ks_l.append(kss); vs_l.append(vss); kis_l.append(kis); ss_l.append(sss); cs_l.append(cvs)
    y_prompt = hp[:, N_META:]
    y_sample = hs
    k_prompt = jnp.stack(kp_l, 0)
    v_prompt = jnp.stack(vp_l, 0)
    kidx_prompt = jnp.stack(kip_l, 0)
    ssm_prompt = jnp.stack(sp_l, 0)
    conv_prompt = jnp.stack(cp_l, 0)
    k_sample = jnp.stack(ks_l, 0)
    v_sample = jnp.stack(vs_l, 0)
    kidx_sample = jnp.stack(kis_l, 0)
    ssm_sample = jnp.stack(ss_l, 0)
    conv_sample = jnp.stack(cs_l, 0)
    return (y_prompt, y_sample, k_prompt, v_prompt, kidx_prompt, ssm_prompt, conv_prompt, k_sample, v_sample, kidx_sample, ssm_sample, conv_sample)
```

```python
import math
from contextlib import ExitStack

import numpy as np
import concourse.bass as bass
import concourse.mybir as mybir
from concourse.bass_utils import run_bass_kernel_spmd

F32 = mybir.dt.float32
BF16 = mybir.dt.bfloat16
ALU = mybir.AluOpType
AF = mybir.ActivationFunctionType
AX = mybir.AxisListType

ENGS = ("pe", "act", "dve", "pool", "sp")

D = 1024
SEQ = 2048
NMETA = 16
TP = NMETA + SEQ
TS = 64
PAST = 4096
NCOL = TP + TS
KC = 8
IN_DIM = 7256
C_Z, C_XBC, C_DT, C_Q, C_K, C_V, C_QI, C_KI, C_WI, C_GS, C_GA = (
    0, 1024, 3072, 3088, 4112, 4368, 4624, 5136, 5200, 5208, 6232)
DFF = 2816
NFF = 22
EPS = 1e-6
TOPK = 256
NBIS = 18
NEG = -30000.0
WI_SCALE = (8 ** -0.5) * (64 ** -0.5)


class Res:
    __slots__ = ("name", "w", "r", "dsem", "dcnt", "excl")

    def __init__(self, name, excl=False):
        self.name = name
        self.w = {}
        self.r = {}
        self.dsem = None
        self.dcnt = 0
        self.excl = excl


class Ins:
    __slots__ = ("eng", "fn", "waits", "signal", "ticket", "dma")

    def __init__(self, eng, fn):
        self.eng = eng
        self.fn = fn
        self.waits = []
        self.signal = False
        self.ticket = None
        self.dma = None


class Prog:
    def __init__(self, nc):
        self.nc = nc
        self.q = {e: [] for e in ENGS}
        self.dma_res = []
        self.last = {e: None for e in ENGS}
        self.pending = {e: [] for e in ENGS}
        self.final = []

    def _add(self, ins, waits):
        eng = ins.eng
        if self.pending[eng]:
            waits = waits + self.pending[eng]
            self.pending[eng] = []
        ins.waits = waits
        for ev in waits:
            if ev[0] == 'c':
                ev[1].signal = True
        self.q[eng].append(ins)

    def op(self, eng, fn, reads=(), writes=()):
        ins = Ins(eng, fn)
        waits = []
        for R in reads:
            for ev in R.w.values():
                if ev[0] == 'c' and ev[1].eng == eng and eng == "pe":
                    continue
                waits.append(ev)
            if R.excl:
                for ev in R.r.values():
                    if ev[0] == 'c' and ev[1].eng == eng:
                        continue
                    waits.append(ev)
        for R in writes:
            for ev in R.w.values():
                if ev[0] == 'c' and ev[1].eng == eng:
                    continue
                waits.append(ev)
            for ev in R.r.values():
                if ev[0] == 'c' and ev[1].eng == eng:
                    continue
                waits.append(ev)
        self._add(ins, waits)
        me = ('c', ins)
        for R in reads:
            R.r[eng] = me
        for R in writes:
            R.w = {eng: me}
            R.r = {}
        self.last[eng] = ins
        return ins

    def dma(self, eng, out, in_, reads=(), writes=(), sem_res=None, **kw):
        if sem_res is None:
            sem_res = writes[0] if writes else reads[0]
        if sem_res.dsem is None:
            sem_res.dsem = True
            self.dma_res.append(sem_res)

        def fn(e, out=out, in_=in_, kw=kw):
            return e.dma_start(out=out, in_=in_, **kw)
        ins = Ins(eng, fn)
        waits = []
        for R in reads:
            waits.extend(R.w.values())
        for R in writes:
            waits.extend(R.w.values())
            waits.extend(R.r.values())
        self._add(ins, waits)
        sem_res.dcnt += 16
        ins.dma = (sem_res, sem_res.dcnt)
        me = ('d', sem_res, sem_res.dcnt)
        key = ('d', id(sem_res))
        for R in reads:
            R.r[key] = me
        for R in writes:
            R.w = {key: me}
            R.r = {}
        return ins

    def dma_more(self, eng, out, in_, R, **kw):
        def fn(e, out=out, in_=in_, kw=kw):
            return e.dma_start(out=out, in_=in_, **kw)
        ins = Ins(eng, fn)
        self._add(ins, [])
        R.dcnt += 16
        ins.dma = (R, R.dcnt)
        R.w = {('d', id(R)): ('d', R, R.dcnt)}
        return ins

    def barrier(self):
        evs = []
        for e in ENGS:
            if self.last[e] is not None:
                evs.append(('c', self.last[e]))
        for R in self.dma_res:
            evs.append(('d', R, R.dcnt))
        for e in ENGS:
            self.pending[e] = self.pending[e] + [
                ev for ev in evs if not (ev[0] == 'c' and ev[1].eng == e)]

    def emit(self, stack):
        nc = self.nc
        esem = {e: stack.enter_context(nc.semaphore("s_" + e)) for e in ENGS}
        for i, R in enumerate(self.dma_res):
            R.dsem = stack.enter_context(nc.semaphore("d%d" % i))
        for e in ENGS:
            t = 0
            for ins in self.q[e]:
                if ins.signal:
                    t += 1
                    ins.ticket = t
        block = stack.enter_context(nc.Block())
        final = self.final

        def run(e, eo):
            seen = {}

            def wait(ev):
                if ev[0] == 'c':
                    sem, val, key = esem[ev[1].eng], ev[1].ticket, ev[1].eng
                else:
                    sem, val, key = ev[1].dsem, ev[2], id(ev[1])
                if seen.get(key, 0) >= val:
                    return
                seen[key] = val
                eo.wait_ge(sem, val)
            for ins in self.q[e]:
                for ev in ins.waits:
                    wait(ev)
                bi = ins.fn(eo)
                if ins.dma is not None:
                    bi.then_inc(ins.dma[0].dsem, 16)
                elif ins.signal:
                    bi.then_inc(esem[e], 1)
            if e == "sp":
                for R in final:
                    for ev in R.w.values():
                        wait(ev)

        @block.tensor
        def _(eo):
            run("pe", eo)

        @block.scalar
        def _(eo):
            run("act", eo)

        @block.vector
        def _(eo):
            run("dve", eo)

        @block.gpsimd
        def _(eo):
            run("pool", eo)

        @block.sync
        def _(eo):
            run("sp", eo)


class Arena:
    def __init__(self, base, nbytes):
        self.base = base
        self.nbytes = nbytes
        self.off = 0
        self.peak = 0
        self.peak_since_release = 0

    def alloc(self, n, dt):
        esz = 4 if dt == F32 else 2
        nb = (n * esz + 63) // 64 * 64
        assert self.off + nb <= self.nbytes, ("arena overflow", self.off, nb, self.nbytes)
        a = self.base[:, self.off // 2:(self.off + nb) // 2]
        self.off += nb
        self.peak = max(self.peak, self.off)
        self.peak_since_release = max(self.peak_since_release, self.off)
        if dt == F32:
            a = a.bitcast(F32)
        return a[:, 0:n]

    def mark(self):
        return self.off

    def release(self, m):
        self.off = m
        self.peak_since_release = m


class Ring:
    def __init__(self, arena, n, cnt, dt, name):
        self.bufs = [(arena.alloc(n, dt), Res("%s%d" % (name, i))) for i in range(cnt)]
        self.i = 0

    def next(self):
        b = self.bufs[self.i % len(self.bufs)]
        self.i += 1
        return b


def r3(ap, a):
    return ap.rearrange("p (a b) -> p a b", a=a)


def t5_bucket_np(rel):
    nb = 16
    max_exact = 8
    ret = np.where(rel > 0, nb, 0)
    n = np.abs(rel)
    nf = np.maximum(n, 1).astype(np.float32)
    large = max_exact + (np.log(nf / np.float32(max_exact)) / np.float32(math.log(128 / max_exact))
                         * np.float32(nb - max_exact)).astype(np.int32)
    large = np.minimum(large, nb - 1)
    return ret + np.where(n < max_exact, n, large)


def build_program(stage=99):
    nc = bass.Bass("TRN2", target_bir_lowering=False)
    dram = {}

    def IN(name, shape):
        dram[name] = nc.dram_tensor(name, list(shape), F32, kind="ExternalInput").ap()
        return dram[name]

    def OUT(name, shape):
        dram[name] = nc.dram_tensor(name, list(shape), F32, kind="ExternalOutput").ap()
        return dram[name]

    xin = IN("xin", [NCOL, D])
    cache_k = IN("cache_k", [PAST, 256])
    cache_v = IN("cache_v", [PAST, 256])
    cache_ki = IN("cache_ki", [PAST, 64])
    state_ssm = IN("state_ssm", [1024, 128])
    state_conv = IN("state_conv", [3, 2048])
    w_in = IN("w_in", [D, IN_DIM])
    w_br_ssd = IN("w_br_ssd", [D, D])
    w_br_att = IN("w_br_att", [D, D])
    w_out = IN("w_out", [D, D])
    w_gate = IN("w_gate", [D, DFF])
    w_up = IN("w_up", [D, DFF])
    w_down = IN("w_down", [DFF, D])
    vecs = IN("vecs", [128, 64])
    rows = IN("rows", [1, 512])
    convw = IN("convw", [128, 64])
    convb = IN("convb", [128, 16])
    ident_in = IN("ident", [128, 128])
    tri_in = IN("tri", [128, 3 * 128])
    nm_in = IN("nm", [128, 128])
    bt_in = IN("bt", [128, 2 * 8 * 128])
    pow2_in = IN("pow2", [1, 32])

    y_p = OUT("y_p", [SEQ, D])
    y_s = OUT("y_s", [TS, D])
    k_p = OUT("k_p", [TP, 256])
    v_p = OUT("v_p", [TP, 256])
    ki_p = OUT("ki_p", [TP, 64])
    ssm_p = OUT("ssm_p", [1024, 128])
    conv_p = OUT("conv_p", [3, 2048])
    k_s = OUT("k_s", [TS, 256])
    v_s = OUT("v_s", [TS, 256])
    ki_s = OUT("ki_s", [TS, 64])
    ssm_s = OUT("ssm_s", [1024, 128])
    conv_s = OUT("conv_s", [3, 2048])
    out_names = ("y_p", "y_s", "k_p", "v_p", "ki_p", "ssm_p", "conv_p", "k_s", "v_s", "ki_s", "ssm_s", "conv_s")
    out_res = {n: Res("o_" + n) for n in out_names}

    tiles = [("M", 0, 16)] + [("T%d" % j, 16 + 128 * (j - 1), 128) for j in range(1, 17)] + [("S", TP, TS)]
    NT = len(tiles)
    TS_IDX = NT - 1

    st = ExitStack()
    with st:
        P = Prog(nc)
        ARENA_BYTES = 212800
        arena_t = st.enter_context(nc.sbuf_tensor("arena", [128, ARENA_BYTES // 2], BF16))
        A = Arena(arena_t, ARENA_BYTES)
        psum_t = st.enter_context(nc.psum_tensor("psum", [128, 4096], F32))
        R_ps = [Res("ps%d" % b, excl=True) for b in range(8)]

        def psf(b, w=512, o=0):
            return psum_t[:, b * 512 + o: b * 512 + o + w]

        def psb(b, w=1024, o=0):
            return psum_t[:, b * 512:(b + 1) * 512].bitcast(BF16)[:, o:o + w]

        def MM(out, lhsT, rhs, start, stop, reads, writes):
            P.op("pe", lambda e: e.matmul(out, lhsT=lhsT, rhs=rhs, start=start, stop=stop), reads, writes)

        def TR(out, in_, ident, reads, writes):
            P.op("pe", lambda e: e.transpose(out, in_, ident), reads, writes)

        def ACT(out, in_, func, reads, writes, bias=None, scale=None, accum_out=None):
            kw = {}
            if bias is not None:
                kw["bias"] = bias
            if scale is not None:
                kw["scale"] = scale
            if accum_out is not None:
                kw["accum_out"] = accum_out
            P.op("act", lambda e: e.activation(out=out, in_=in_, func=func, **kw), reads, writes)

        def TS_(eng, out, in0, s1, s2, op0, op1, reads, writes, accum_out=None):
            if op1 is None:
                P.op(eng, lambda e: e.tensor_scalar(out=out, in0=in0, scalar1=s1, scalar2=None, op0=op0), reads, writes)
            elif accum_out is None:
                P.op(eng, lambda e: e.tensor_scalar(out=out, in0=in0, scalar1=s1, scalar2=s2, op0=op0, op1=op1),
                     reads, writes)
            else:
                P.op(eng, lambda e: e.tensor_scalar(out=out, in0=in0, scalar1=s1, scalar2=s2, op0=op0, op1=op1,
                                                    accum_out=accum_out), reads, writes)

        def TT(eng, out, in0, in1, op, reads, writes):
            P.op(eng, lambda e: e.tensor_tensor(out=out, in0=in0, in1=in1, op=op), reads, writes)

        def STT(eng, out, in0, scalar, in1, op0, op1, reads, writes):
            P.op(eng, lambda e: e.scalar_tensor_tensor(out=out, in0=in0, scalar=scalar, in1=in1, op0=op0, op1=op1),
                 reads, writes)

        def CP(eng, out, in_, reads, writes):
            if eng == "act":
                P.op("act", lambda e: e.activation(out=out, in_=in_, func=AF.Copy), reads, writes)
            else:
                P.op(eng, lambda e: e.tensor_copy(out=out, in_=in_), reads, writes)

        def MEMSET(eng, ap, val, writes):
            P.op(eng, lambda e: e.memset(ap, val), (), writes)

        def RECIP(out, in_, reads, writes):
            P.op("dve", lambda e: e.reciprocal(out=out, in_=in_), reads, writes)

        def wsrc(w, c0, c1):
            return w.rearrange("(kc p) c -> p kc c", p=128)[:, :, c0:c1]

        def load_w(dst3, R, w, c0, c1, step=1024):
            first = True
            for a in range(c0, c1, step):
                b = min(c1, a + step)
                if first:
                    P.dma("pool", dst3[:, :, a - c0:b - c0], wsrc(w, a, b), writes=[R], sem_res=R)
                    first = False
                else:
                    P.dma_more("pool", dst3[:, :, a - c0:b - c0], wsrc(w, a, b), R)

        ident_f = A.alloc(128, F32); R_identf = Res("identf")
        ident_b = A.alloc(128, BF16); R_identb = Res("identb")
        vec = A.alloc(64, F32); R_vec = Res("vec")
        rowb = A.alloc(512, F32); R_rowb = Res("rowb")
        mhalf = A.alloc(8, F32); R_mhalf = Res("mhalf")
        P.dma("sp", ident_f, ident_in, writes=[R_identf])
        P.dma("pool", ident_b, ident_in, writes=[R_identb])
        P.dma("sp", vec, vecs, writes=[R_vec])
        P.dma("sp", rowb, rows.partition_broadcast(128), writes=[R_rowb])
        MEMSET("pool", mhalf, -0.5, [R_mhalf])
        V_N1W, V_N2W, V_SNW, V_QNW = 0, 8, 16, 24
        RB_KNW, RB_KINW, RB_DTB, RB_ALOG, RB_DSK, RB_CB = 0, 128, 192, 208, 224, 240

        MG_BYTES = KC * NCOL * 2
        A.nbytes = ARENA_BYTES - MG_BYTES
        mergedT = r3(arena_t[:, (ARENA_BYTES - MG_BYTES) // 2: ARENA_BYTES // 2], KC)
        R_mg = [Res("mg%d" % t) for t in range(NT)]
        junk_act = A.alloc(D, BF16); R_junk_act = Res("junk_act")
        qnw_s = A.alloc(1, F32); R_qnws = Res("qnws")
        TS_("dve", qnw_s, vec[:, V_QNW:V_QNW + 1], 128.0 ** -0.5, None, ALU.mult, None, [R_vec], [R_qnws])
        yattT_s = r3(A.alloc(KC * TS, BF16), KC)
        persist_mark = A.mark()

        mm_banks = [0, 1, 2, 7]
        tr_banks = [3, 4]
        o_banks = [5, 6]
        cnt = {"mm": 0, "tr": 0, "o": 0, "at": 0, "fr": 0}
        bank_pools = {"mm": mm_banks, "tr": tr_banks, "o": o_banks, "at": [0, 1]}

        def nb(kind):
            pool = bank_pools[kind]
            b = pool[cnt[kind] % len(pool)]
            cnt[kind] += 1
            return b

        def rsqrt_small(out, in_, scale, n, w, reads, writes):
            TS_("pool", out, in_, scale, EPS, ALU.mult, ALU.add, reads, writes)
            TT("pool", out, out, mhalf[0:n, 0:w], ALU.pow, list(writes) + [R_mhalf], writes)

        class HN:
            def __init__(self, vcol, nxt=2, nhn=2):
                self.xt = Ring(A, D, nxt, F32, "xt") if nxt else None
                self.xn = Ring(A, D, 1, BF16, "xn")
                self.hn = Ring(A, KC * 128, nhn, BF16, "hn") if nhn else None
                self.sm = Ring(A, 4, 2, F32, "smh")
                self.vcol = vcol
                self.pool = "tr"

            def tile_gen(self, t, src=None, R_src=None, dst=None, R_dst=None, pool=None):
                pool = self.pool if pool is None else pool
                tname, col0, n = tiles[t]
                if src is None:
                    xt, R_xt = self.xt.next()
                    P.dma("sp", xt[0:n, :], xin[col0:col0 + n, :], writes=[R_xt])
                else:
                    xt, R_xt = src, R_src
                sm, R_sm = self.sm.next()
                if dst is None:
                    hn2, R_hn = self.hn.next()
                    hn = r3(hn2, KC)
                else:
                    hn, R_hn = dst, R_dst
                ACT(junk_act[0:n, :], xt[0:n, :], AF.Square, [R_xt], [R_junk_act, R_sm], accum_out=sm[0:n, 0:1])
                yield
                TS_("pool", sm[0:n, 1:2], sm[0:n, 0:1], 1.0 / D, EPS, ALU.mult, ALU.add, [R_sm], [R_sm])
                yield
                TT("pool", sm[0:n, 1:2], sm[0:n, 1:2], mhalf[0:n, 0:1], ALU.pow, [R_sm, R_mhalf], [R_sm])
                yield
                xn, R_xn = self.xn.next()
                TS_("dve", xn[0:n, :], xt[0:n, :], sm[0:n, 1:2], None, ALU.mult, None, [R_xt, R_sm], [R_xn])
                b = nb(pool)
                pt = r3(psb(b), 8)
                for kc in range(KC):
                    TR(pt[:, kc, 0:n], xn[0:n, kc * 128:(kc + 1) * 128], ident_b[0:n, 0:n], [R_xn, R_identb], [R_ps[b]])
                yield
                TT("dve", hn[:, :, 0:n], pt[:, :, 0:n],
                   vec[:, self.vcol:self.vcol + 8].unsqueeze(2).to_broadcast([128, 8, n]), ALU.mult,
                   [R_ps[b], R_vec], [R_hn])
                return hn, R_hn

            def tile(self, t, src=None, R_src=None, dst=None, R_dst=None):
                g = self.tile_gen(t, src, R_src, dst, R_dst)
                try:
                    while True:
                        next(g)
                except StopIteration as e:
                    return e.value

        def gate_branch(t, hn, R_hn, yT, R_yT, wbr, R_wbr, wg, R_wg, first, sg_ring, tmpm_ring):
            tname, col0, n = tiles[t]
            bbs = [nb("mm"), nb("mm")]
            bgs = [nb("mm"), nb("mm")]
            for half in range(2):
                pbr = r3(psf(bbs[half]), 4)
                for cc in range(4):
                    c = half * 4 + cc
                    for kc in range(KC):
                        MM(pbr[:, cc, 0:n], wbr[:, kc, c * 128:(c + 1) * 128], yT[:, kc, 0:n], kc == 0, kc == KC - 1,
                           [R_wbr, R_yT], [R_ps[bbs[half]]])
            for half in range(2):
                pg = r3(psf(bgs[half]), 4)
                for cc in range(4):
                    c = half * 4 + cc
                    for kc in range(KC):
                        MM(pg[:, cc, 0:n], wg[:, kc, c * 128:(c + 1) * 128], hn[:, kc, 0:n], kc == 0, kc == KC - 1,
                           [R_wg, R_hn], [R_ps[bgs[half]]])
            for half in range(2):
                pbr = r3(psf(bbs[half]), 4)
                pg = r3(psf(bgs[half]), 4)
                sg2, R_sg = sg_ring.next()
                sg = r3(sg2, 4)
                ACT(sg[:, :, 0:n], pg[:, :, 0:n], AF.Sigmoid, [R_ps[bgs[half]]], [R_sg])
                dst = mergedT[:, half * 4:(half + 1) * 4, col0:col0 + n]
                if first:
                    TT("dve", dst, pbr[:, :, 0:n], sg[:, :, 0:n], ALU.mult, [R_ps[bbs[half]], R_sg], [R_mg[t]])
                else:
                    tm2, R_tm = tmpm_ring.next()
                    tm = r3(tm2, 4)
                    TT("dve", tm[:, :, 0:n], pbr[:, :, 0:n], sg[:, :, 0:n], ALU.mult, [R_ps[bbs[half]], R_sg], [R_tm])
                    TT("pool", dst, dst, tm[:, :, 0:n], ALU.add, [R_tm, R_mg[t]], [R_mg[t]])

        if stage >= 2:
            hnS = HN(V_N1W)
            wS = r3(A.alloc(KC * 3088, BF16), KC); R_wS = Res("wS")
            R_wSz = Res("wSz")
            load_w(wS[:, :, C_XBC:3088], R_wS, w_in, C_XBC, 3088)
            load_w(wS[:, :, 0:C_XBC], R_wSz, w_in, 0, C_XBC)
            cw = A.alloc(64, F32); R_cw = Res("cw")
            P.dma("sp", cw, convw, writes=[R_cw])
            cb = A.alloc(16, F32); R_cb = Res("cb")
            P.dma("sp", cb, convb, writes=[R_cb])
            tri = r3(A.alloc(3 * 128, F32), 3); R_tri = Res("tri")
            P.dma("sp", tri, tri_in.rearrange("p (a b) -> p a b", a=3), writes=[R_tri])
            U_f, SL_f, ONES_f = tri[:, 0, :], tri[:, 1, :], tri[:, 2, :]
            NM = r3(A.alloc(4 * 128, BF16), 4); R_NM = Res("NM")
            for h in range(4):
                if h == 0:
                    P.dma("pool", NM[:, h, :], nm_in, writes=[R_NM], sem_res=R_NM)
                else:
                    P.dma_more("pool", NM[:, h, :], nm_in, R_NM)
            dconv = r3(A.alloc(64 * 128, BF16), 64); R_dconv = Res("dconv"); R_dconv2 = Res("dconv2")
            for i in range(64):
                if i < 32:
                    TS_("dve", dconv[:, i, :], ident_f, cw[:, i:i + 1], None, ALU.mult, None, [R_identf, R_cw], [R_dconv])
                else:
                    ACT(dconv[:, i, :], ident_f, AF.Copy, [R_identf, R_cw], [R_dconv2], scale=cw[:, i:i + 1])
            aneg = A.alloc(16, F32); R_aneg = Res("aneg")
            ACT(aneg, rowb[:, RB_ALOG:RB_ALOG + 16], AF.Exp, [R_rowb], [R_aneg])
            TS_("dve", aneg, aneg, -1.0, None, ALU.mult, None, [R_aneg], [R_aneg])
            ST = A.alloc(1024, F32); R_ST = Res("ST")
            Sb = A.alloc(1024, BF16); R_Sb = Res("Sb")
            hist = r3(A.alloc(16 * 3, BF16), 16); R_hist = Res("hist")
            Rm = A.alloc(2048, F32); R_Rm = Res("Rm")
            cvT = A.alloc(48, F32); R_cvT = Res("cvT")
            cvio = A.alloc(128, F32); R_cvio = Res("cvio")
            sso = r3(Rm[:, 0:1024], 8)
            stin = r3(Rm[:, 1024:2048], 8)
            xraw_ring = Ring(A, 16 * 131, 2, BF16, "xraw")
            xact_ring = Ring(A, 16 * 128, 2, BF16, "xact")
            xtok_ring = Ring(A, 1024, 2, BF16, "xtok")
            btok_ring = Ring(A, 512, 2, BF16, "btok")
            xdt_ring = Ring(A, 1024, 2, BF16, "xdt")
            xD_ring = Ring(A, 1024, 2, BF16, "xD")
            xdec_ring = Ring(A, 1024, 2, BF16, "xdec")
            Lt_ring = Ring(A, 2048, 2, BF16, "Lt")
            t1_ring = Ring(A, 1024, 2, F32, "t1")
            yn_ring = Ring(A, 1024, 2, BF16, "yn")
            szh_ring = Ring(A, 512, 2, F32, "szh")
            smS_ring = Ring(A, 160, 2, F32, "smS")
            etb_ring = Ring(A, 16, 2, F32, "etb")
            for par in range(2):
                bank_pools["s_mm%d" % par] = [0, 1] if par == 0 else [4, 5]
                bank_pools["s_tr%d" % par] = [2] if par == 0 else [6]
                bank_pools["s_o%d" % par] = [3] if par == 0 else [7]
                for k_ in ("s_mm", "s_tr", "s_o"):
                    cnt["%s%d" % (k_, par)] = 0
            flags = {"hist": -1, "state": -1}

            def ssd_gen(t, first, last, seq, par, prev_t, S):
                mmp, trp, op_ = "s_mm%d" % par, "s_tr%d" % par, "s_o%d" % par
                tname, col0, n = tiles[t]
                hn, R_hn = yield from hnS.tile_gen(t, pool=trp)
                yield
                sm, R_sm = smS_ring.next()
                xraw2, R_xraw = xraw_ring.next(); xraw = r3(xraw2, 16)
                xact2, R_xact = xact_ring.next(); xact = r3(xact2, 16)
                xtok, R_xtok = xtok_ring.next()
                btok, R_btok = btok_ring.next()
                xdt, R_xdt = xdt_ring.next()
                xD, R_xD = xD_ring.next()
                xdec, R_xdec = xdec_ring.next()
                Lt, R_Lt = Lt_ring.next()
                t1, R_t1 = t1_ring.next()
                yn, R_yn = yn_ring.next()
                etb, R_etb = etb_ring.next()
                if first and seq == "p":
                    MEMSET("pool", xraw[:, :, 0:3], 0.0, [R_xraw])
                elif first and seq == "s":
                    P.dma("sp", S.cvio[0:48, :], state_conv.rearrange("t (c f) -> (t c) f", f=128), writes=[S.R_cvio])
                    bt = nb(op_)
                    TR(psf(bt)[:, 0:48], S.cvio[0:48, :], ident_f[0:48, 0:48], [S.R_cvio, R_identf], [R_ps[bt]])
                    CP("dve", xraw[:, :, 0:3], psf(bt)[:, 0:48].rearrange("p (t c) -> p c t", t=3), [R_ps[bt]], [R_xraw])
                else:
                    while S.flags["hist"] < prev_t:
                        yield
                    CP("pool", xraw[:, :, 0:3], S.hist[:, :, :], [S.R_hist], [R_xraw])
                for hf in range(2):
                    banks = [nb(mmp), nb(mmp)]
                    for cc in range(8):
                        c = hf * 8 + cc
                        b = banks[cc // 4]
                        for kc in range(KC):
                            MM(r3(psf(b), 4)[:, cc % 4, 0:n], wS[:, kc, C_XBC + c * 128: C_XBC + (c + 1) * 128],
                               hn[:, kc, 0:n], kc == 0, kc == KC - 1, [R_wS, R_hn], [R_ps[b]])
                    yield
                    for q2 in range(2):
                        q = hf * 2 + q2
                        CP("act" if q2 == 0 else "dve", xraw[:, q * 4:(q + 1) * 4, 3:3 + n],
                           r3(psf(banks[q2]), 4)[:, :, 0:n], [R_ps[banks[q2]]], [R_xraw])
                        if last:
                            CP("dve", S.cvT.rearrange("p (t c) -> p c t", t=3)[:, q * 4:(q + 1) * 4, :],
                               r3(psf(banks[q2]), 4)[:, :, n - 3:n], [R_ps[banks[q2]]], [S.R_cvT])
                    yield
                CP("pool", S.hist[:, :, :], xraw[:, :, n:n + 3], [R_xraw], [S.R_hist])
                S.flags["hist"] = t
                if last:
                    bt = nb(op_)
                    TR(psf(bt)[0:48, 0:128], S.cvT[:, 0:48], ident_f, [S.R_cvT, R_identf], [R_ps[bt]])
                    CP("act", S.cvio[0:48, :], psf(bt)[0:48, 0:128], [R_ps[bt]], [S.R_cvio])
                    oname = "conv_p" if seq == "p" else "conv_s"
                    P.dma("sp", dram[oname].rearrange("t (c f) -> (t c) f", f=128), S.cvio[0:48, :], reads=[S.R_cvio],
                          writes=[out_res[oname]], sem_res=S.R_cvio)
                yield
                for hf in range(2):
                    banks = [nb(mmp), nb(mmp)]
                    for cc in range(8):
                        c = hf * 8 + cc
                        b = banks[cc // 4]
                        o = r3(psf(b), 4)[:, cc % 4, 0:n]
                        for i in range(4):
                            MM(o, dconv[:, i * 16 + c, :], xraw[:, c, i:i + n], i == 0, i == 3,
                               [R_dconv if i < 2 else R_dconv2, R_xraw], [R_ps[b]])
                    yield
                    for cc in range(8):
                        c = hf * 8 + cc
                        b = banks[cc // 4]
                        o = r3(psf(b), 4)[:, cc % 4, 0:n]
                        ACT(xact[:, c, 0:n], o, AF.Silu, [R_ps[b], R_cb], [R_xact], bias=cb[:, c:c + 1])
                    yield
                bx = nb(trp)
                for c in range(8):
                    TR(psb(bx)[0:n, c * 128:(c + 1) * 128], xact[:, c, 0:n], ident_b, [R_xact, R_identb], [R_ps[bx]])
                yield
                CP("act", xtok[0:n, :], psb(bx)[0:n, :], [R_ps[bx]], [R_xtok])
                yield
                bB = nb(trp)
                for c in range(4):
                    TR(psb(bB)[0:n, c * 128:(c + 1) * 128], xact[:, 8 + c, 0:n], ident_b, [R_xact, R_identb], [R_ps[bB]])
                yield
                CP("dve", btok[0:n, :], psb(bB)[0:n, 0:512], [R_ps[bB]], [R_btok])
                yield
                bd = nb(op_)
                for kc in range(KC):
                    MM(psf(bd)[0:n, 0:16], hn[:, kc, 0:n], wS[:, kc, C_DT:C_DT + 16], kc == 0, kc == KC - 1,
                       [R_hn, R_wS], [R_ps[bd]])
                yield
                TT("dve", sm[0:n, 0:16], psf(bd)[0:n, 0:16], rowb[0:n, RB_DTB:RB_DTB + 16], ALU.add,
                   [R_ps[bd], R_rowb], [R_sm])
                yield
                STT("dve", sm[0:n, 16:32], sm[0:n, 0:16], -1.0, sm[0:n, 0:16], ALU.mult, ALU.max, [R_sm], [R_sm])
                yield
                ACT(sm[0:n, 16:32], sm[0:n, 16:32], AF.Exp, [R_sm], [R_sm], scale=-1.0)
                yield
                ACT(sm[0:n, 16:32], sm[0:n, 16:32], AF.Ln, [R_sm], [R_sm], bias=1.0)
                yield
                STT("dve", sm[0:n, 32:48], sm[0:n, 0:16], 0.0, sm[0:n, 16:32], ALU.max, ALU.add, [R_sm], [R_sm])
                yield
                TT("dve", sm[0:n, 48:64], sm[0:n, 32:48], aneg[0:n, :], ALU.mult, [R_sm, R_aneg], [R_sm])
                yield
                dt_ = sm[0:n, 32:48]
                dtA = sm[0:n, 48:64]
                MM(psf(bd)[0:n, 16:32], U_f[0:n, 0:n], dtA, True, True, [R_tri, R_sm], [R_ps[bd]])
                MM(psf(bd)[:, 32:48], ONES_f[0:n, :], dtA, True, True, [R_tri, R_sm], [R_ps[bd]])
                yield
                CP("dve", sm[0:n, 64:80], psf(bd)[0:n, 16:32], [R_ps[bd]], [R_sm])
                ACT(sm[0:n, 80:96], psf(bd)[0:n, 16:32], AF.Exp, [R_ps[bd]], [R_sm])
                yield
                TT("dve", sm[0:n, 96:112], psf(bd)[0:n, 32:48], sm[0:n, 64:80], ALU.subtract, [R_ps[bd], R_sm], [R_sm])
                ACT(etb, psf(bd)[:, 32:48], AF.Exp, [R_ps[bd]], [R_etb])
                yield
                ACT(sm[0:n, 96:112], sm[0:n, 96:112], AF.Exp, [R_sm], [R_sm])
                yield
                ea = sm[0:n, 80:96]
                dec = sm[0:n, 96:112]
                xtok3 = r3(xtok, 16)
                TT("dve", r3(xdt, 16)[0:n], xtok3[0:n], dt_.unsqueeze(2).to_broadcast([n, 16, 64]), ALU.mult,
                   [R_xtok, R_sm], [R_xdt])
                TT("pool", r3(xD, 16)[0:n], xtok3[0:n], rowb[0:n, RB_DSK:RB_DSK + 16].unsqueeze(2).to_broadcast([n, 16, 64]),
                   ALU.mult, [R_xtok, R_rowb], [R_xD])
                yield
                TT("pool", r3(xdec, 16)[0:n], r3(xdt, 16)[0:n], dec.unsqueeze(2).to_broadcast([n, 16, 64]), ALU.mult,
                   [R_xdt, R_sm], [R_xdec])
                yield
                if not first:
                    while S.flags["state"] < prev_t:
                        yield
                byo = [nb(mmp), nb(mmp)]
                for g in range(4):
                    b = byo[g // 2]
                    MM(psf(b)[0:n, (g % 2) * 256:(g % 2 + 1) * 256], xact[:, 12 + g, 0:n], S.Sb[:, g * 256:(g + 1) * 256],
                       True, True, [R_xact, S.R_Sb], [R_ps[b]])
                yield
                for q in range(2):
                    TT("dve", r3(t1, 16)[0:n, q * 8:(q + 1) * 8, :], r3(psf(byo[q]), 8)[0:n],
                       ea[:, q * 8:(q + 1) * 8].unsqueeze(2).to_broadcast([n, 8, 64]), ALU.mult,
                       [R_ps[byo[q]], R_sm], [R_t1])
                yield
                bs = [nb(mmp), nb(mmp)]
                for g in range(4):
                    b = bs[g // 2]
                    MM(psf(b)[:, (g % 2) * 256:(g % 2 + 1) * 256], btok[0:n, g * 128:(g + 1) * 128],
                       xdec[0:n, g * 256:(g + 1) * 256], True, True, [R_btok, R_xdec], [R_ps[b]])
                TT("pool", r3(S.ST, 16), r3(S.ST, 16), etb.unsqueeze(2).to_broadcast([128, 16, 64]), ALU.mult,
                   [S.R_ST, R_etb], [S.R_ST])
                yield
                for q in range(2):
                    TT("dve", S.ST[:, q * 512:(q + 1) * 512], S.ST[:, q * 512:(q + 1) * 512], psf(bs[q]), ALU.add,
                       [S.R_ST, R_ps[bs[q]]], [S.R_ST])
                yield
                CP("pool", S.Sb, S.ST, [S.R_ST], [S.R_Sb])
                S.flags["state"] = t
                yield
                if last:
                    for q in range(2):
                        b = nb(mmp)
                        for cc in range(4):
                            c = q * 4 + cc
                            TR(psf(b)[:, cc * 128:(cc + 1) * 128], S.ST[:, c * 128:(c + 1) * 128], ident_f,
                               [S.R_ST, R_identf], [R_ps[b]])
                        CP("act", S.sso[:, q * 4:(q + 1) * 4, :], r3(psf(b), 4), [R_ps[b]], [S.R_sso])
                    oname = "ssm_p" if seq == "p" else "ssm_s"
                    P.dma("sp", dram[oname].rearrange("(c p) n -> p c n", p=128), S.sso, reads=[S.R_sso],
                          writes=[out_res[oname]], sem_res=S.R_sso)
                    yield
                Rm3 = r3(Rm, 16)
                TT("dve", Rm3[0:n, :, 0:n], U_f[0:n, 0:n].unsqueeze(1).to_broadcast([n, 16, n]),
                   dtA.unsqueeze(2).to_broadcast([n, 16, n]), ALU.mult, [R_tri, R_sm], [R_Rm])
                Lt3 = r3(Lt, 16)
                for q in range(4):
                    b = nb(mmp)
                    o = r3(psf(b), 4)[0:n, :, 0:n]
                    if n == 128:
                        MM(o, SL_f[0:n, 0:n], Rm3[0:n, q * 4:(q + 1) * 4, 0:n], True, False, [R_tri, R_Rm], [R_ps[b]])
                        MM(o, ident_b[0:n, 0:n], NM[0:n, :, 0:n], False, True, [R_identb, R_NM], [R_ps[b]])
                    else:
                        for r_ in range(4):
                            MM(o[:, r_, :], SL_f[0:n, 0:n], Rm3[0:n, q * 4 + r_, 0:n], True, False, [R_tri, R_Rm],
                               [R_ps[b]])
                            MM(o[:, r_, :], ident_b[0:n, 0:n], NM[0:n, r_, 0:n], False, True, [R_identb, R_NM],
                               [R_ps[b]])
                    ACT(Lt3[0:n, q * 4:(q + 1) * 4, 0:n], o, AF.Exp, [R_ps[b]], [R_Lt])
                yield
                bc = nb(op_)
                pcb = r3(psf(bc), 4)
                for g in range(4):
                    MM(pcb[0:n, g, 0:n], xact[:, 8 + g, 0:n], xact[:, 12 + g, 0:n], True, True, [R_xact], [R_ps[bc]])
                yield
                Lt4 = Lt.rearrange("p (g r i) -> p g r i", g=4, r=4)
                TT("dve", Lt4[0:n, :, :, 0:n], Lt4[0:n, :, :, 0:n],
                   pcb[0:n, :, 0:n].unsqueeze(2).to_broadcast([n, 4, 4, n]), ALU.mult, [R_Lt, R_ps[bc]], [R_Lt])
                yield
                byd = [nb(mmp), nb(mmp)]
                for h in range(16):
                    b = byd[h // 8]
                    o = psf(b)[0:n, (h % 8) * 64:(h % 8 + 1) * 64]
                    MM(o, Lt3[0:n, h, 0:n], xdt[0:n, h * 64:(h + 1) * 64], True, False, [R_Lt, R_xdt], [R_ps[b]])
                    MM(o, ident_b[0:n, 0:n], xD[0:n, h * 64:(h + 1) * 64], False, True, [R_identb, R_xD], [R_ps[b]])
                yield
                for q in range(2):
                    TT("dve", t1[0:n, q * 512:(q + 1) * 512], psf(byd[q])[0:n, :], t1[0:n, q * 512:(q + 1) * 512], ALU.add,
                       [R_ps[byd[q]], R_t1], [R_t1])
                yield
                for q in range(2):
                    bz = nb(mmp)
                    for kc in range(KC):
                        MM(psf(bz)[0:n, :], hn[:, kc, 0:n], wS[:, kc, C_Z + q * 512:C_Z + (q + 1) * 512], kc == 0,
                           kc == KC - 1, [R_hn, R_wSz], [R_ps[bz]])
                    szh, R_szh = szh_ring.next()
                    ACT(szh[0:n, :], psf(bz)[0:n, :], AF.Silu, [R_ps[bz]], [R_szh])
                    yield
                    TT("pool", t1[0:n, q * 512:(q + 1) * 512], t1[0:n, q * 512:(q + 1) * 512], szh[0:n, :], ALU.mult,
                       [R_t1, R_szh], [R_t1])
                    yield
                ACT(junk_act[0:n, :], t1[0:n, :], AF.Square, [R_t1], [R_junk_act, R_sm], accum_out=sm[0:n, 112:113])
                yield
                TS_("pool", sm[0:n, 113:114], sm[0:n, 112:113], 1.0 / 1024, EPS, ALU.mult, ALU.add, [R_sm], [R_sm])
                yield
                TT("pool", sm[0:n, 113:114], sm[0:n, 113:114], mhalf[0:n, 0:1], ALU.pow, [R_sm, R_mhalf], [R_sm])
                yield
                TS_("dve", yn[0:n, :], t1[0:n, :], sm[0:n, 113:114], None, ALU.mult, None, [R_t1, R_sm], [R_yn])
                yield
                bt = nb(trp)
                pt = r3(psb(bt), 8)
                for kc in range(KC):
                    TR(pt[:, kc, 0:n], yn[0:n, kc * 128:(kc + 1) * 128], ident_b[0:n, 0:n], [R_yn, R_identb], [R_ps[bt]])
                yield
                TT("dve", mergedT[:, :, col0:col0 + n], pt[:, :, 0:n],
                   vec[:, V_SNW:V_SNW + 8].unsqueeze(2).to_broadcast([128, 8, n]),
                   ALU.mult, [R_ps[bt], R_vec], [R_mg[t]])
                yield

            def run_two(gens, lag):
                gens = list(gens)
                active = []
                nxt = 0
                while nxt < len(gens) or active:
                    if not active or (len(active) == 1 and nxt < len(gens) and active[0][1] >= lag):
                        if nxt < len(gens):
                            active.append([gens[nxt], 0])
                            nxt += 1
                    for a_ in list(active):
                        try:
                            next(a_[0])
                            a_[1] += 1
                        except StopIteration:
                            active.remove(a_)

            class SeqState:
                pass
            Sp = SeqState()
            Sp.ST, Sp.R_ST, Sp.Sb, Sp.R_Sb, Sp.hist, Sp.R_hist = ST, R_ST, Sb, R_Sb, hist, R_hist
            Sp.cvT, Sp.R_cvT, Sp.cvio, Sp.R_cvio, Sp.sso, Sp.R_sso = cvT, R_cvT, cvio, R_cvio, sso, R_Rm
            Sp.flags = {"hist": -1, "state": -1}
            Ss = SeqState()
            Ss.ST = A.alloc(1024, F32); Ss.R_ST = Res("STs")
            Ss.Sb = A.alloc(1024, BF16); Ss.R_Sb = Res("Sbs")
            Ss.hist = r3(A.alloc(16 * 3, BF16), 16); Ss.R_hist = Res("hists")
            Ss.cvT = A.alloc(48, F32); Ss.R_cvT = Res("cvTs")
            Ss.cvio = A.alloc(128, F32); Ss.R_cvio = Res("cvios")
            Ss.sso = r3(A.alloc(1024, F32), 8); Ss.R_sso = Res("ssos")
            Ss.flags = {"hist": -1, "state": -1}
            P.dma("sp", stin, state_ssm.rearrange("(c p) n -> p c n", p=128), reads=[], writes=[R_Rm])
            for q in range(2):
                b = nb("mm")
                for cc in range(4):
                    c = q * 4 + cc
                    TR(psf(b)[:, cc * 128:(cc + 1) * 128], stin[:, c, :], ident_f, [R_Rm, R_identf], [R_ps[b]])
                CP("dve", Ss.ST[:, q * 512:(q + 1) * 512], psf(b), [R_ps[b]], [Ss.R_ST])
            CP("pool", Ss.Sb, Ss.ST, [Ss.R_ST], [Ss.R_Sb])
            MEMSET("dve", ST, 0.0, [R_ST])
            MEMSET("pool", Sb, 0.0, [R_Sb])
            gens = [ssd_gen(t, t == 0, t == NT - 2, "p", t % 2, t - 1, Sp) for t in range(0, NT - 1)]
            gens.append(ssd_gen(TS_IDX, True, True, "s", TS_IDX % 2, None, Ss))
            run_two(gens, 24)
            hnS.pool = "tr"
            print("pass S1 arena peak", A.peak)
            A.release(persist_mark)
            P.barrier()

        if stage >= 3:
            XBYTES = 33792
            regX = A.alloc(XBYTES // 2, BF16)
            afterX_mark = A.mark()
            yattT = r3(regX[:, 0:KC * TP], KC)
            hnA = HN(V_N1W, nxt=1, nhn=2)
            R_yattT = [Res("yat%d" % t) for t in range(NT)]
            NWA = C_GS - C_Q
            wA_off = A.mark()
            wA_raw = A.alloc(KC * NWA, BF16)
            wA_end = A.mark()
            wA = r3(wA_raw, KC); R_wA = Res("wA")
            load_w(wA, R_wA, w_in, C_Q, C_GS)
            WQ, WK, WV, WQI, WKI, WWI = 0, C_K - C_Q, C_V - C_Q, C_QI - C_Q, C_KI - C_Q, C_WI - C_Q
            bt2 = A.alloc(2 * 8 * 128, F32); R_bt = Res("bt")
            P.dma("sp", bt2, bt_in, writes=[R_bt])
            bt = bt2.rearrange("p (w h i) -> p w h i", w=2, h=8)
            BT_PREV, BT_SAME, BT_META = 0, 1, 2
            qb_ring = Ring(A, D, 1, BF16, "qb")
            qT_ring = Ring(A, 8 * 128, 3, BF16, "qT")
            qiT_ring = Ring(A, 4 * 128, 2, BF16, "qiT")
            ko_ring = Ring(A, 256, 2, F32, "ko")
            vo_ring = Ring(A, 256, 1, F32, "vo")
            kio_ring = Ring(A, 64, 2, F32, "kio")
            kb_ring = Ring(A, 256, 2, BF16, "kb")
            ki2_ring = Ring(A, 128, 2, BF16, "ki2")
            r_ring = Ring(A, 512, 2, F32, "rbuf")
            pm_ring = Ring(A, 512, 3, BF16, "pm")
            ya_ring = Ring(A, D, 1, BF16, "ya")
            smA_ring = Ring(A, 96, 3, F32, "smA")
            bis_ring = Ring(A, 64, 2, F32, "bis")
            pow2 = A.alloc(32, F32); R_pow2 = Res("pow2")
            P.dma("sp", pow2, pow2_in.partition_broadcast(128), writes=[R_pow2])
            commonA_mark = A.mark()
            bank_pools["at"] = [0, 1]
            bank_pools["fr"] = [2, 3]
            hnA.pool = "fr"

            class TileCtx:
                pass

            def front(t, B, ktile_idx, key_tiles, bias_kind, adm_fill, outs, fr):
                kT, vb, kiT2 = B["kT"], B["vb"], B["kiT2"]
                R_k = B["R_k"]
                score, R_score = B["score_ring"].next()
                bis, R_bis = bis_ring.next()
                hn, R_hn = yield from hnA.tile_gen(t, pool=fr)
                yield
                mask, R_mask = B["mask_ring"].next()
                maskT2, R_maskT = B["maskT_ring"].next()
                maskT = r3(maskT2, B["NKT"])
                tname, col0, n = tiles[t]
                kc0, nk_self = key_tiles[ktile_idx]
                L = kc0 + nk_self
                sm, R_sm = smA_ring.next()
                ctx = TileCtx()
                ctx.sm, ctx.R_sm, ctx.maskT, ctx.R_maskT = sm, R_sm, maskT, R_maskT
                qb, R_qb = qb_ring.next()
                qbanks = [nb(fr), nb(fr)]
                for half in range(2):
                    b = qbanks[half]
                    for kc in range(KC):
                        MM(psf(b)[0:n, :], hn[:, kc, 0:n], wA[:, kc, WQ + half * 512: WQ + (half + 1) * 512],
                           kc == 0, kc == KC - 1, [R_hn, R_wA], [R_ps[b]])
                yield
                for half in range(2):
                    b = qbanks[half]
                    for hh in range(4):
                        h = half * 4 + hh
                        ACT(junk_act[0:n, 0:128], psf(b)[0:n, hh * 128:(hh + 1) * 128], AF.Square, [R_ps[b]],
                            [R_junk_act, R_sm], accum_out=sm[0:n, h:h + 1])
                yield
                TS_("pool", sm[0:n, 8:16], sm[0:n, 0:8], 1.0 / 128, EPS, ALU.mult, ALU.add, [R_sm], [R_sm])
                yield
                TT("pool", sm[0:n, 8:16], sm[0:n, 8:16], mhalf[0:n, 0:8], ALU.pow, [R_sm, R_mhalf], [R_sm])
                yield
                for half in range(2):
                    b = qbanks[half]
                    TT("dve", r3(qb[0:n, half * 512:(half + 1) * 512], 4), r3(psf(b)[0:n, :], 4),
                       sm[0:n, 8 + half * 4:12 + half * 4].unsqueeze(2).to_broadcast([n, 4, 128]), ALU.mult,
                       [R_ps[b], R_sm], [R_qb])
                qT2, R_qT = qT_ring.next()
                qT = r3(qT2, 8)
                ctx.qT, ctx.R_qT = qT, R_qT
                btq = nb(fr)
                ptq = r3(psb(btq), 8)
                for h in range(8):
                    TR(ptq[:, h, 0:n], qb[0:n, h * 128:(h + 1) * 128], ident_b[0:n, 0:n], [R_qb, R_identb], [R_ps[btq]])
                bkv = nb(fr)
                for kc in range(KC):
                    MM(psf(bkv)[0:n, :], hn[:, kc, 0:n], wA[:, kc, WK:WK + 512], kc == 0, kc == KC - 1,
                       [R_hn, R_wA], [R_ps[bkv]])
                yield
                ACT(qT[:, :, 0:n], ptq[:, :, 0:n], AF.Identity, [R_ps[btq], R_qnws], [R_qT], scale=qnw_s[:, 0:1])
                bki = nb(fr)
                for kc in range(KC):
                    MM(psf(bki)[0:n, 0:72], hn[:, kc, 0:n], wA[:, kc, WKI:WKI + 72], kc == 0, kc == KC - 1,
                       [R_hn, R_wA], [R_ps[bki]])
                yield
                for g in range(2):
                    ACT(junk_act[0:n, 0:128], psf(bkv)[0:n, g * 128:(g + 1) * 128], AF.Square, [R_ps[bkv]],
                        [R_junk_act, R_sm], accum_out=sm[0:n, 16 + g:17 + g])
                ACT(junk_act[0:n, 0:64], psf(bki)[0:n, 0:64], AF.Square, [R_ps[bki]], [R_junk_act, R_sm],
                    accum_out=sm[0:n, 20:21])
                ACT(sm[0:n, 24:32], psf(bki)[0:n, 64:72], AF.Abs, [R_ps[bki]], [R_sm], scale=WI_SCALE)
                ACT(sm[0:n, 32:40], psf(bki)[0:n, 64:72], AF.Sign, [R_ps[bki]], [R_sm])
                yield
                TS_("pool", sm[0:n, 18:20], sm[0:n, 16:18], 1.0 / 128, EPS, ALU.mult, ALU.add, [R_sm], [R_sm])
                TS_("pool", sm[0:n, 21:22], sm[0:n, 20:21], 1.0 / 64, EPS, ALU.mult, ALU.add, [R_sm], [R_sm])
                yield
                TT("pool", sm[0:n, 18:22], sm[0:n, 18:22], mhalf[0:n, 0:4], ALU.pow, [R_sm, R_mhalf], [R_sm])
                yield
                ko, R_ko = ko_ring.next()
                vo, R_vo = vo_ring.next()
                b = bkv
                for g in range(2):
                    STT("dve", ko[0:n, g * 128:(g + 1) * 128], psf(b)[0:n, g * 128:(g + 1) * 128], sm[0:n, 18 + g:19 + g],
                        rowb[0:n, RB_KNW:RB_KNW + 128], ALU.mult, ALU.mult, [R_ps[b], R_sm, R_rowb], [R_ko])
                CP("act", vo[0:n, :], psf(b)[0:n, 256:512], [R_ps[b]], [R_vo])
                CP("act", vb[0:n, ktile_idx, :, 0:128], r3(psf(b)[0:n, 256:512], 2), [R_ps[b]], [R_k[ktile_idx]])
                P.dma("sp", outs["k"], ko[0:n, :], reads=[R_ko], writes=[out_res[outs["kn"]]], sem_res=R_ko)
                P.dma("sp", outs["v"], vo[0:n, :], reads=[R_vo], writes=[out_res[outs["vn"]]], sem_res=R_vo)
                b = bki
                kio, R_kio = kio_ring.next()
                STT("dve", kio[0:n, :], psf(b)[0:n, 0:64], sm[0:n, 21:22], rowb[0:n, RB_KINW:RB_KINW + 64],
                    ALU.mult, ALU.mult, [R_ps[b], R_sm, R_rowb], [R_kio])
                P.dma("sp", outs["ki"], kio[0:n, :], reads=[R_kio], writes=[out_res[outs["kin"]]], sem_res=R_kio)
                yield
                kb, R_kb = kb_ring.next()
                CP("pool", kb[0:n, :], ko[0:n, :], [R_ko], [R_kb])
                ki2, R_ki2 = ki2_ring.next()
                CP("pool", r3(ki2[0:n, :], 2), kio[0:n, :].unsqueeze(1).to_broadcast([n, 2, 64]), [R_kio], [R_ki2])
                yield
                bt_ = nb(fr)
                pt = r3(psb(bt_), 8)
                for g in range(2):
                    TR(pt[:, g, 0:n], kb[0:n, g * 128:(g + 1) * 128], ident_b[0:n, 0:n], [R_kb, R_identb], [R_ps[bt_]])
                bt2_ = nb(fr)
                TR(psb(bt2_)[:, 0:n], ki2[0:n, :], ident_b[0:n, 0:n], [R_ki2, R_identb], [R_ps[bt2_]])
                yield
                CP("act", kT[:, :, kc0:kc0 + n], pt[:, 0:2, 0:n], [R_ps[bt_]], [R_k[ktile_idx]])
                CP("act", kiT2[:, kc0:kc0 + n], psb(bt2_)[:, 0:n], [R_ps[bt2_]], [R_k[ktile_idx]])
                yield
                if L > TOPK:
                    qiT2, R_qiT = qiT_ring.next()
                    qiT = r3(qiT2, 4)
                    b = nb(fr)
                    pq = r3(psf(b), 4)
                    for m in range(4):
                        for kc in range(KC):
                            MM(pq[:, m, 0:n], wA[:, kc, WQI + m * 128: WQI + (m + 1) * 128], hn[:, kc, 0:n],
                               kc == 0, kc == KC - 1, [R_hn, R_wA], [R_ps[b]])
                    CP("act", qiT[:, :, 0:n], pq[:, :, 0:n], [R_ps[b]], [R_qiT])
                    for c0_ in range(0, L, 512):
                        c1_ = min(L, c0_ + 512)
                        wd = c1_ - c0_
                        rk = [R_k[i] for i, (kc_, nk_) in enumerate(key_tiles[:ktile_idx + 1])
                              if kc_ < c1_ and kc_ + nk_ > c0_]
                        for h in range(8):
                            m, hh = h // 2, h % 2
                            b = nb(fr)
                            MM(psf(b)[0:n, 0:wd], qiT[hh * 64:(hh + 1) * 64, m, 0:n],
                               kiT2[hh * 64:(hh + 1) * 64, c0_:c1_], True, True, [R_qiT] + rk, [R_ps[b]])
                            rb, R_rb = r_ring.next()
                            ACT(rb[0:n, 0:wd], psf(b)[0:n, 0:wd], AF.Relu, [R_ps[b], R_sm], [R_rb],
                                scale=sm[0:n, 24 + h:25 + h])
                            eng = "dve"
                            if h == 0:
                                TS_("dve", score[0:n, c0_:c1_], rb[0:n, 0:wd], sm[0:n, 32:33], None, ALU.mult, None,
                                    [R_rb, R_sm], [R_score])
                            else:
                                STT(eng, score[0:n, c0_:c1_], rb[0:n, 0:wd], sm[0:n, 32 + h:33 + h],
                                    score[0:n, c0_:c1_], ALU.mult, ALU.add, [R_rb, R_sm, R_score], [R_score])
                            if h % 4 == 3:
                                yield
                    P.op("dve", lambda e: e.tensor_reduce(out=sm[0:n, 40:41], in_=score[0:n, 0:L], axis=AX.X,
                                                          op=ALU.max, apply_absolute_value=True), [R_score], [R_sm])
                    if adm_fill is not None:
                        (r0, r1, fc0, fc1) = adm_fill
                        MEMSET("dve", score[r0:r1, fc0:fc1], -1e30, [R_score])
                    TS_("dve", sm[0:n, 41:42], sm[0:n, 40:41], 1.0, None, ALU.add, None, [R_sm], [R_sm])
                    TS_("dve", bis[0:n, 0:NBIS + 2], pow2[0:n, 0:NBIS + 2], sm[0:n, 41:42], None, ALU.mult, None,
                        [R_pow2, R_sm], [R_bis])
                    MEMSET("dve", sm[0:n, 42:43], 0.0, [R_sm])
                    yield
                    for k in range(1, NBIS + 1):
                        TS_("dve", mask[0:n, 0:L], score[0:n, 0:L], sm[0:n, 42:43], 0.0, ALU.is_ge, ALU.add,
                            [R_score, R_sm], [R_mask, R_sm], accum_out=sm[0:n, 43:44])
                        TS_("dve", sm[0:n, 44:45], sm[0:n, 43:44], float(TOPK), 0.5, ALU.is_ge, ALU.subtract,
                            [R_sm], [R_sm])
                        STT("dve", sm[0:n, 42:43], sm[0:n, 44:45], bis[0:n, k - 1:k], sm[0:n, 42:43], ALU.mult, ALU.add,
                            [R_sm, R_bis], [R_sm])
                        yield
                    STT("dve", sm[0:n, 45:46], bis[0:n, NBIS:NBIS + 1], -1.0, sm[0:n, 42:43], ALU.mult, ALU.add,
                        [R_sm, R_bis], [R_sm])
                    TS_("dve", mask[0:n, 0:L], score[0:n, 0:L], sm[0:n, 45:46], None, ALU.is_ge, None,
                        [R_score, R_sm], [R_mask])
                    yield
                else:
                    MEMSET("pool", mask[0:n, 0:L], 1.0, [R_mask])
                    if adm_fill is not None:
                        (r0, r1, fc0, fc1) = adm_fill
                        MEMSET("pool", mask[r0:r1, fc0:fc1], 0.0, [R_mask])
                nkt = ktile_idx + 1
                for k0 in range(0, nkt, 8):
                    k1 = min(nkt, k0 + 8)
                    bt_ = nb(fr)
                    pt = r3(psb(bt_), 8)
                    for kt in range(k0, k1):
                        kc_, nk_ = key_tiles[kt]
                        TR(pt[0:nk_, kt - k0, 0:n], mask[0:n, kc_:kc_ + nk_], ident_b[0:n, 0:n], [R_mask, R_identb],
                           [R_ps[bt_]])
                    ACT(maskT[:, k0:k1, 0:n], pt[:, 0:k1 - k0, 0:n], AF.Identity, [R_ps[bt_]], [R_maskT],
                        scale=-NEG, bias=NEG)
                    yield
                B["ctx"][t] = ctx

            def attn(t, ctx, B, ktile_idx, key_tiles, bias_kind, yT_dst, R_yT):
                kT, vb = B["kT"], B["vb"]
                R_k = B["R_k"]
                tname, col0, n = tiles[t]
                sm, R_sm, maskT, R_maskT, qT, R_qT = ctx.sm, ctx.R_sm, ctx.maskT, ctx.R_maskT, ctx.qT, ctx.R_qT
                nkt = ktile_idx + 1
                near = {kt: bias_kind[kt] for kt in bias_kind}
                far_tiles = [kt for kt in range(nkt) if kt not in near]
                groups = [far_tiles[i:i + 4] for i in range(0, len(far_tiles), 4)]
                near_tiles = sorted(near.keys())
                if near_tiles:
                    groups.append(near_tiles)
                ya, R_ya = ya_ring.next()
                items = [(h, gi) for h in range(8) for gi in range(len(groups))]
                pend = []
                state = {}

                def stage1(h, gi):
                    g = h // 4
                    grp = groups[gi]
                    isnear = grp[0] in near
                    b = nb("at")
                    ps3 = r3(psf(b), 4)
                    for j, kt in enumerate(grp):
                        kc_, nk_ = key_tiles[kt]
                        o = ps3[0:nk_, j, 0:n]
                        MM(o, kT[:, g, kc_:kc_ + nk_], qT[:, h, 0:n], True, False, [R_k[kt], R_qT], [R_ps[b]])
                        MM(o, ident_b[0:nk_, 0:nk_], maskT[0:nk_, kt, 0:n], False, not isnear,
                           [R_identb, R_maskT], [R_ps[b]])
                        if isnear:
                            kind = near[kt]
                            if kind == BT_META:
                                MM(o, ident_f[:, 112:112 + nk_], bt[:, BT_PREV, h, 0:n], False, True,
                                   [R_identf, R_bt], [R_ps[b]])
                            else:
                                MM(o, ident_f[0:nk_, 0:nk_], bt[0:nk_, kind, h, 0:n], False, True,
                                   [R_identf, R_bt], [R_ps[b]])
                    ng = len(grp)
                    pm2, R_pm = pm_ring.next()
                    pm = r3(pm2, 4)
                    if isnear:
                        ACT(pm[:, 0:ng, 0:n], ps3[:, 0:ng, 0:n], AF.Exp, [R_ps[b]], [R_pm])
                    else:
                        ACT(pm[:, 0:ng, 0:n], ps3[:, 0:ng, 0:n], AF.Exp, [R_ps[b], R_rowb], [R_pm],
                            bias=rowb[:, RB_CB + h:RB_CB + h + 1])
                    return (h, gi, pm, R_pm)

                def stage2(h, gi, pm, R_pm):
                    g = h // 4
                    grp = groups[gi]
                    if gi == 0:
                        state["bo"] = nb("o")
                    bo = state["bo"]
                    O = psf(bo)[0:n, 0:129]
                    for j, kt in enumerate(grp):
                        kc_, nk_ = key_tiles[kt]
                        first = (gi == 0 and j == 0)
                        last = (gi == len(groups) - 1 and j == len(grp) - 1)
                        MM(O, pm[0:nk_, j, 0:n], vb[0:nk_, kt, g, :], first, last, [R_pm, R_k[kt]], [R_ps[bo]])
                    if gi == len(groups) - 1:
                        RECIP(sm[0:n, 48 + h:49 + h], psf(bo)[0:n, 128:129], [R_ps[bo]], [R_sm])
                        ACT(ya[0:n, h * 128:(h + 1) * 128], psf(bo)[0:n, 0:128], AF.Identity, [R_ps[bo], R_sm], [R_ya],
                            scale=sm[0:n, 48 + h:49 + h])

                SKEW = 2
                for it in items:
                    pend.append(stage1(*it))
                    if len(pend) > SKEW:
                        stage2(*pend.pop(0))
                    yield
                while pend:
                    stage2(*pend.pop(0))
                bt_ = nb("at")
                pt = r3(psb(bt_), 8)
                for kc in range(KC):
                    TR(pt[:, kc, 0:n], ya[0:n, kc * 128:(kc + 1) * 128], ident_b[0:n, 0:n], [R_ya, R_identb], [R_ps[bt_]])
                CP("act", yT_dst, pt[:, :, 0:n], [R_ps[bt_]], [R_yT])
                yield

            def key_bufs(LMAX, NKT, nmask, pre_alloc=None):
                B = {"NKT": NKT, "ctx": {}}
                if pre_alloc is None:
                    B["kT"] = r3(A.alloc(2 * LMAX, BF16), 2)
                    B["vb"] = A.alloc(NKT * 2 * 129, BF16).rearrange("p (t g d) -> p t g d", t=NKT, g=2)
                else:
                    B["kT"] = r3(pre_alloc[:, 0:2 * LMAX], 2)
                    B["vb"] = pre_alloc[:, 2 * LMAX:2 * LMAX + NKT * 2 * 129].rearrange("p (t g d) -> p t g d",
                                                                                      t=NKT, g=2)
                B["kiT2"] = A.alloc(LMAX, BF16)
                raws = [A.alloc(2 * LMAX, BF16) for _ in range(nmask)]
                B["score_raw"] = raws[0]

                class _SR:
                    def __init__(self, items):
                        self.items = items
                        self.i = 0

                    def next(self):
                        it = self.items[self.i % len(self.items)]
                        self.i += 1
                        return it
                B["score_ring"] = _SR([(r_.bitcast(F32)[:, 0:LMAX], Res("score%d" % i)) for i, r_ in enumerate(raws)])
                B["mask_ring"] = Ring(A, LMAX, nmask, BF16, "mask")
                B["maskT_ring"] = Ring(A, NKT * 128, nmask + 1 if nmask > 1 else 1, BF16, "maskT")
                B["R_k"] = [Res("kt%d" % i) for i in range(NKT)]
                MEMSET("pool", B["vb"][:, :, :, 128:129], 1.0, B["R_k"])
                return B

            def gate_gen(t, hn, R_hn, yT, R_yT):
                tname, col0, n = tiles[t]
                for half in range(2):
                    bb = 7
                    pbr = r3(psf(bb), 4)
                    for cc in range(4):
                        c = half * 4 + cc
                        for kc in range(KC):
                            MM(pbr[:, cc, 0:n], wbrA[:, kc, c * 128:(c + 1) * 128], yT[:, kc, 0:n], kc == 0,
                               kc == KC - 1, [R_wbrA, R_yT], [R_ps[bb]])
                    yield
                    bg = 4
                    pg = r3(psf(bg), 4)
                    for cc in range(4):
                        c = half * 4 + cc
                        for kc in range(KC):
                            MM(pg[:, cc, 0:n], wgA[:, kc, c * 128:(c + 1) * 128], hn[:, kc, 0:n], kc == 0,
                               kc == KC - 1, [R_wgA, R_hn], [R_ps[bg]])
                    yield
                    sg2, R_sg = sgA_ring.next()
                    sg = r3(sg2, 4)
                    ACT(sg[:, :, 0:n], pg[:, :, 0:n], AF.Sigmoid, [R_ps[bg]], [R_sg])
                    yield
                    tm2, R_tm = tmA_ring.next()
                    tm = r3(tm2, 4)
                    TT("dve", tm[:, :, 0:n], pbr[:, :, 0:n], sg[:, :, 0:n], ALU.mult, [R_ps[bb], R_sg], [R_tm])
                    yield
                    dst = mergedT[:, half * 4:(half + 1) * 4, col0:col0 + n]
                    TT("pool", dst, dst, tm[:, :, 0:n], ALU.add, [R_tm, R_mg[t]], [R_mg[t]])
                    yield

            def interleave(streams):
                gens = [s[0] for s in streams]
                est = [max(1, s[1]) for s in streams]
                prog = [0] * len(gens)
                alive = [True] * len(gens)
                while any(alive):
                    best = None
                    for i in range(len(gens)):
                        if alive[i] and (best is None or prog[i] / est[i] < prog[best] / est[best]):
                            best = i
                    try:
                        next(gens[best])
                        prog[best] += 1
                    except StopIteration:
                        alive[best] = False

            def run_all(g):
                for _ in g:
                    pass

            LS = PAST + TS
            assert 2 * LS + 33 * 2 * 129 <= XBYTES // 2
            Bs = key_bufs(LS, 33, 1, pre_alloc=regX)
            R_kc = Bs["R_k"]
            R_stg = Bs["score_ring"].items[0][1]
            stg = Bs["score_raw"][:, 0:4096]
            stg_i = stg.rearrange("p (t r d) -> p t r d", t=32, r=2)
            P.dma("pool", stg_i[:, :, 0, :], cache_ki.rearrange("(t p) d -> p t d", p=128), writes=[R_stg], sem_res=R_stg)
            P.dma_more("pool", stg_i[:, :, 1, :], cache_ki.rearrange("(t p) d -> p t d", p=128), R_stg)
            for t0 in range(0, 32, 8):
                bt_ = nb("tr")
                pt = r3(psb(bt_), 8)
                for j in range(8):
                    TR(pt[:, j, :], stg_i[:, t0 + j].rearrange("p r d -> p (r d)"), ident_b, [R_stg, R_identb],
                       [R_ps[bt_]])
                CP("act" if (t0 // 8) % 2 == 0 else "dve", Bs["kiT2"][:, t0 * 128:(t0 + 8) * 128], psb(bt_), [R_ps[bt_]],
                   [R_kc[i] for i in range(t0, t0 + 8)])
            stgk2 = A.alloc(8 * 256, BF16); R_stgk = Res("stgk")
            stg_k = r3(stgk2, 8)
            bank_pools["frk"] = [4, 7]
            cnt["frk"] = 0

            def cache_kv_gen():
                for t0 in range(0, 32, 8):
                    cvv = cache_v.rearrange("(t p) (g d) -> p t g d", p=128, g=2)
                    P.dma("pool", Bs["vb"][:, t0:t0 + 8, 0, 0:128], cvv[:, t0:t0 + 8, 0, :],
                          writes=[R_kc[i] for i in range(t0, t0 + 8)], sem_res=R_kc[t0])
                    P.dma_more("pool", Bs["vb"][:, t0:t0 + 8, 1, 0:128], cvv[:, t0:t0 + 8, 1, :], R_kc[t0])
                    for i in range(t0 + 1, t0 + 8):
                        R_kc[i].w = dict(R_kc[t0].w)
                yield
                for t0 in range(0, 32, 8):
                    P.dma("pool", stg_k, cache_k.rearrange("(t p) c -> p t c", p=128)[:, t0:t0 + 8, :], writes=[R_stgk])
                    yield
                    for g in range(2):
                        bt_ = nb("frk")
                        pt = r3(psb(bt_), 8)
                        for j in range(8):
                            TR(pt[:, j, :], stg_k[:, j, g * 128:(g + 1) * 128], ident_b, [R_stgk, R_identb], [R_ps[bt_]])
                        yield
                        CP("act", Bs["kT"][:, g, t0 * 128:(t0 + 8) * 128], psb(bt_), [R_ps[bt_]],
                           [R_kc[i] for i in range(t0, t0 + 8)])
                        yield
            key_tiles_s = [(i * 128, 128) for i in range(32)] + [(PAST, TS)]
            bk_s = {31: BT_PREV, 32: BT_SAME}
            interleave([(front(TS_IDX, Bs, 32, key_tiles_s, bk_s, None,
                               dict(k=k_s, v=v_s, ki=ki_s, kn="k_s", vn="v_s", kin="ki_s"), "fr"), 60),
                        (cache_kv_gen(), 18)])
            run_all(attn(TS_IDX, Bs["ctx"][TS_IDX], Bs, 32, key_tiles_s, bk_s,
                         yattT_s[:, :, :], R_yattT[TS_IDX]))
            print("pass A(sample) arena peak", A.peak)
            A.release(commonA_mark)
            P.barrier()

            Bp = key_bufs(TP, 17, 2)
            key_tiles_p = [(0, 16)] + [(16 + 128 * (j - 1), 128) for j in range(1, 17)]
            NP_ = NT - 1
            bank_pools["fr0"] = [2, 3]
            bank_pools["fr1"] = [7, 4]
            cnt["fr0"] = 0
            cnt["fr1"] = 0

            def bias_kind_p(t):
                if t == 0:
                    return {0: BT_SAME}
                if t == 1:
                    return {0: BT_META, 1: BT_SAME}
                return {t - 1: BT_PREV, t: BT_SAME}

            def front_p(t):
                tname, col0, n = tiles[t]
                adm = (0, 64, col0 + 64, col0 + 128) if t >= 1 else None
                return front(t, Bp, t, key_tiles_p, bias_kind_p(t), adm,
                             dict(k=k_p[col0:col0 + n, :], v=v_p[col0:col0 + n, :], ki=ki_p[col0:col0 + n, :],
                                  kn="k_p", vn="v_p", kin="ki_p"), "fr%d" % (t % 2))

            def front_steps(t):
                L = key_tiles_p[t][0] + key_tiles_p[t][1]
                if L <= TOPK:
                    return 20
                return 20 + 2 * ((L + 511) // 512) + NBIS + 5 + (t + 8) // 8

            def limited(g, k):
                for _ in range(k):
                    try:
                        next(g)
                    except StopIteration:
                        return
                    yield

            def attn_p(t):
                tname, col0, n = tiles[t]
                return attn(t, Bp["ctx"][t], Bp, t, key_tiles_p, bias_kind_p(t), yattT[:, :, col0:col0 + n], R_yattT[t])

            run_all(front_p(0))
            fcur = front_p(1)
            run_all(limited(fcur, front_steps(1) // 2))
            for t in range(NP_):
                nfar_ = (t + 1) - len(bias_kind_p(t))
                streams = [(attn_p(t), 8 * ((nfar_ + 3) // 4 + 1) + 1)]
                fnext = None
                if t + 1 < NP_:
                    streams.append((fcur, max(1, front_steps(t + 1) - front_steps(t + 1) // 2)))
                if t + 2 < NP_:
                    fnext = front_p(t + 2)
                    streams.append((limited(fnext, front_steps(t + 2) // 2), front_steps(t + 2) // 2))
                interleave(streams)
                if t + 1 < NP_:
                    run_all(fcur)
                fcur = fnext
                if t == NP_ - 2:
                    wbrS = r3(wA_raw[:, 0:KC * D], KC)
                    wgS = r3(wA_raw[:, KC * D:2 * KC * D], KC)
                    for j_, (dst_, w_, c0_) in enumerate(((wbrS, w_br_ssd, 0), (wgS, w_in, C_GS))):
                        if j_ == 0:
                            P.dma("pool", dst_, wsrc(w_, c0_, c0_ + D), writes=[R_wA], sem_res=R_wA)
                        else:
                            P.dma_more("pool", dst_, wsrc(w_, c0_, c0_ + D), R_wA)
            print("pass A arena peak", A.peak)
            A.release(afterX_mark)
            P.barrier()
            assert 2 * KC * D <= KC * NWA
            R_wbrS = R_wA
            R_wgS = R_wA
            sgM_ring = Ring(A, 512, 4, F32, "sgM")
            assert A.mark() <= wA_off, (A.mark(), wA_off)
            A.off = wA_end
            hnM = HN(V_N1W)
            wbrA = r3(A.alloc(KC * D, BF16), KC); R_wbrA = Res("wbrA")
            load_w(wbrA, R_wbrA, w_br_att, 0, D)
            wgA = r3(A.alloc(KC * D, BF16), KC); R_wgA = Res("wgA")
            load_w(wgA, R_wgA, w_in, C_GA, C_GA + D)
            tmM_ring = Ring(A, 512, 4, F32, "tmM")
            for par in range(2):
                bank_pools["m_mm%d" % par] = [0, 1, 2] if par == 0 else [4, 5, 6]
                bank_pools["m_tr%d" % par] = [3] if par == 0 else [7]
                cnt["m_mm%d" % par] = 0
                cnt["m_tr%d" % par] = 0

            def gateM(t, par):
                mmp = "m_mm%d" % par
                tname, col0, n = tiles[t]
                hn, R_hn = yield from hnM.tile_gen(t, pool="m_tr%d" % par)
                yield
                mg = mergedT[:, :, col0:col0 + n]

                def mm32(bank, w, R_w, src, R_src, half):
                    p4 = r3(psf(bank), 4)
                    for cc in range(4):
                        c = half * 4 + cc
                        for kc in range(KC):
                            MM(p4[:, cc, 0:n], w[:, kc, c * 128:(c + 1) * 128], src[:, kc, 0:n], kc == 0, kc == KC - 1,
                               [R_w, R_src], [R_ps[bank]])
                    return p4
                ba = nb(mmp); pa = mm32(ba, wbrS, R_wbrS, mg, R_mg[t], 0)
                bb = nb(mmp); pb_ = mm32(bb, wbrS, R_wbrS, mg, R_mg[t], 1)
                yield
                for half, (bbr, pbr) in enumerate(((ba, pa), (bb, pb_))):
                    bg = nb(mmp); pg = mm32(bg, wgS, R_wgS, hn, R_hn, half)
                    yield
                    sg2, R_sg = sgM_ring.next(); sg = r3(sg2, 4)
                    ACT(sg[:, :, 0:n], pg[:, :, 0:n], AF.Sigmoid, [R_ps[bg]], [R_sg])
                    yield
                    TT("dve", mergedT[:, half * 4:(half + 1) * 4, col0:col0 + n], pbr[:, :, 0:n], sg[:, :, 0:n], ALU.mult,
                       [R_ps[bbr], R_sg], [R_mg[t]])
                    yield
                yT = yattT_s if t == TS_IDX else yattT[:, :, col0:col0 + n]
                for half in range(2):
                    bbr = nb(mmp); pbr = mm32(bbr, wbrA, R_wbrA, yT, R_yattT[t], half)
                    bg = nb(mmp); pg = mm32(bg, wgA, R_wgA, hn, R_hn, half)
                    yield
                    sg2, R_sg = sgM_ring.next(); sg = r3(sg2, 4)
                    ACT(sg[:, :, 0:n], pg[:, :, 0:n], AF.Sigmoid, [R_ps[bg]], [R_sg])
                    yield
                    tm2, R_tm = tmM_ring.next(); tm = r3(tm2, 4)
                    TT("dve", tm[:, :, 0:n], pbr[:, :, 0:n], sg[:, :, 0:n], ALU.mult, [R_ps[bbr], R_sg], [R_tm])
                    yield
                    dst = mergedT[:, half * 4:(half + 1) * 4, col0:col0 + n]
                    TT("pool", dst, dst, tm[:, :, 0:n], ALU.add, [R_tm, R_mg[t]], [R_mg[t]])
                    yield

            def run_two_m(gens, lag):
                gens = list(gens)
                active = []
                nxt = 0
                while nxt < len(gens) or active:
                    if not active or (len(active) == 1 and nxt < len(gens) and active[0][1] >= lag):
                        if nxt < len(gens):
                            active.append([gens[nxt], 0])
                            nxt += 1
                    for a_ in list(active):
                        try:
                            next(a_[0])
                            a_[1] += 1
                        except StopIteration:
                            active.remove(a_)
            run_two_m([gateM(t, t % 2) for t in range(NT)], 5)
            hnM.pool = "tr"
            print("pass M arena peak", A.peak)
            A.release(persist_mark)
            P.barrier()

        if stage >= 4:
            h_all = r3(A.alloc(NT * D, F32), NT); R_h = [Res("h%d" % t) for t in range(NT)]
            hn2T = r3(A.alloc(KC * NCOL, BF16), KC); R_hn2 = [Res("hn2_%d" % t) for t in range(NT)]
            markO = A.mark()
            hnO = HN(V_N2W, nxt=2, nhn=0)
            wo = r3(A.alloc(KC * D, BF16), KC); R_wo = Res("wo")
            load_w(wo, R_wo, w_out, 0, D)
            for par in range(2):
                bank_pools["o_mm%d" % par] = [0, 1] if par == 0 else [4, 5]
                bank_pools["o_tr%d" % par] = [2] if par == 0 else [6]
                cnt["o_mm%d" % par] = 0
                cnt["o_tr%d" % par] = 0

            def passO_gen(t, par):
                tname, col0, n = tiles[t]
                xt, R_xt = hnO.xt.next()
                P.dma("sp", xt[0:n, :], xin[col0:col0 + n, :], writes=[R_xt])
                banks = [nb("o_mm%d" % par), nb("o_mm%d" % par)]
                for q in range(2):
                    b = banks[q]
                    for kc in range(KC):
                        MM(psf(b)[0:n, :], mergedT[:, kc, col0:col0 + n], wo[:, kc, q * 512:(q + 1) * 512], kc == 0,
                           kc == KC - 1, [R_mg[t], R_wo], [R_ps[b]])
                yield
                for q in range(2):
                    b = banks[q]
                    TT("dve", h_all[0:n, t, q * 512:(q + 1) * 512], psf(b)[0:n, :], xt[0:n, q * 512:(q + 1) * 512],
                       ALU.add, [R_ps[b], R_xt], [R_h[t]])
                yield
                yield from hnO.tile_gen(t, src=h_all[:, t, :], R_src=R_h[t], dst=hn2T[:, :, col0:col0 + n],
                                        R_dst=R_hn2[t], pool="o_tr%d" % par)
                yield
            run_two_m([passO_gen(t, t % 2) for t in range(NT)], 4)
            print("pass O arena peak", A.peak)
            A.release(markO)
            A.nbytes = ARENA_BYTES
            P.barrier()

            slices = [(0, 6), (6, 12), (12, 17), (17, 22)]
            NFMAX = 6
            wgt_ring = Ring(A, KC * 256, 2, BF16, "wgt")
            wup_ring = Ring(A, KC * 256, 2, BF16, "wup")
            actT = r3(A.alloc(NFMAX * NCOL, BF16), NFMAX); R_act = [Res("act%d" % c) for c in range(NFMAX)]
            wd_ring = Ring(A, NFMAX * D, 1, BF16, "wd")
            s_ring = Ring(A, 512, 2, F32, "silu")
            blocks = [(0, 512), (512, 1024), (1024, 1536), (1536, 2048), (2048, NCOL)]

            def tiles_in(c0, c1):
                return [t for t, (_, col0, n) in enumerate(tiles) if col0 < c1 and col0 + n > c0]
            for (f0, f1) in slices:
                nf = f1 - f0
                for fp in range(f0, f1, 2):
                    npair = min(2, f1 - fp)
                    wgt2, R_wgt = wgt_ring.next()
                    wup2, R_wup = wup_ring.next()
                    wgt = r3(wgt2, KC)
                    wup = r3(wup2, KC)
                    P.dma("pool", wgt[:, :, 0:npair * 128], wsrc(w_gate, fp * 128, (fp + npair) * 128), writes=[R_wgt])
                    P.dma("pool", wup[:, :, 0:npair * 128], wsrc(w_up, fp * 128, (fp + npair) * 128), writes=[R_wup])
                    for ci in range(npair):
                        cl = fp + ci - f0
                        for (c0, c1) in blocks:
                            wdt = c1 - c0
                            rt = [R_hn2[t] for t in tiles_in(c0, c1)]
                            bg = nb("mm")
                            bu = nb("mm")
                            for kc in range(KC):
                                MM(psf(bg)[:, 0:wdt], wgt[:, kc, ci * 128:(ci + 1) * 128], hn2T[:, kc, c0:c1], kc == 0,
                                   kc == KC - 1, [R_wgt] + rt, [R_ps[bg]])
                            for kc in range(KC):
                                MM(psf(bu)[:, 0:wdt], wup[:, kc, ci * 128:(ci + 1) * 128], hn2T[:, kc, c0:c1], kc == 0,
                                   kc == KC - 1, [R_wup] + rt, [R_ps[bu]])
                            sl_, R_sl = s_ring.next()
                            ACT(sl_[:, 0:wdt], psf(bg)[:, 0:wdt], AF.Silu, [R_ps[bg]], [R_sl])
                            TT("dve", actT[:, cl, c0:c1], psf(bu)[:, 0:wdt], sl_[:, 0:wdt], ALU.mult,
                               [R_ps[bu], R_sl], [R_act[cl]])
                wd2, R_wd = wd_ring.next()
                wd = r3(wd2, NFMAX)
                P.dma("pool", wd[:, 0:nf, :], w_down.rearrange("(c p) d -> p c d", p=128)[:, f0:f1, :], writes=[R_wd])
                for t in range(NT):
                    tname, col0, n = tiles[t]
                    for q in range(2):
                        b = nb("o")
                        for cl in range(nf):
                            MM(psf(b)[0:n, :], actT[:, cl, col0:col0 + n], wd[:, cl, q * 512:(q + 1) * 512], cl == 0,
                               cl == nf - 1, [R_act[cl], R_wd], [R_ps[b]])
                        TT("dve", h_all[0:n, t, q * 512:(q + 1) * 512], h_all[0:n, t, q * 512:(q + 1) * 512],
                           psf(b)[0:n, :], ALU.add, [R_ps[b], R_h[t]], [R_h[t]])
            for t in range(1, NT):
                tname, col0, n = tiles[t]
                if t == TS_IDX:
                    P.dma("sp", y_s[:, :], h_all[0:n, t, :], reads=[R_h[t]], writes=[out_res["y_s"]], sem_res=R_h[t])
                else:
                    P.dma("sp", y_p[col0 - 16:col0 - 16 + n, :], h_all[0:n, t, :], reads=[R_h[t]],
                          writes=[out_res["y_p"]], sem_res=R_h[t])
            print("pass F arena peak", A.peak)

        P.final = [out_res[k] for k in out_names]
        P.emit(st)
    return nc


def _static_consts():
    c = {}
    c["ident"] = np.eye(128, dtype=np.float32)
    k = np.arange(128)[:, None]
    i = np.arange(128)[None, :]
    U = (k <= i).astype(np.float32)
    SL = (k > i).astype(np.float32)
    ones = np.ones((128, 128), np.float32)
    c["tri"] = np.ascontiguousarray(np.concatenate([U, SL, ones], axis=1))
    c["nm"] = np.where(i < k, np.float32(NEG), np.float32(0.0)).astype(np.float32)
    c["pow2"] = (2.0 ** -np.arange(32, dtype=np.float64)).astype(np.float32)[None, :]
    return c


def _bias_tables(rel_bias):
    ss = np.arange(128)[:, None]
    ii = np.arange(128)[None, :]
    tabs = []
    for off in (-128, 0):
        tabs.append(rel_bias[t5_bucket_np(ss + off - ii)])
    bt = np.stack(tabs, axis=1)
    bt = bt.transpose(0, 1, 3, 2)
    return np.ascontiguousarray(bt.reshape(128, -1)).astype(np.float32)


def make_in_maps(inputs, cores):
    g = lambda k: np.asarray(inputs[k], dtype=np.float32)
    x_prompt, x_sample = g("x_prompt"), g("x_sample")
    meta = g("meta_tokens")
    consts = _static_consts()
    rel_bias = g("rel_bias")
    bt = _bias_tables(rel_bias)
    vecs = np.zeros((128, 64), np.float32)
    vecs[:, 0:8] = g("norm1_w")[0].reshape(8, 128).T
    vecs[:, 8:16] = g("norm2_w")[0].reshape(8, 128).T
    vecs[:, 16:24] = g("ssd_norm_w")[0].reshape(8, 128).T
    vecs[:, 24] = g("q_norm_w")[0]
    rows = np.zeros((1, 512), np.float32)
    rows[0, 0:128] = g("k_norm_w")[0]
    rows[0, 128:192] = g("idx_k_norm_w")[0]
    rows[0, 192:208] = g("dt_bias")[0]
    rows[0, 208:224] = g("a_log")[0]
    rows[0, 224:240] = g("d_skip")[0]
    rows[0, 240:248] = rel_bias[15]
    convw = np.ascontiguousarray(g("conv_w")[0].reshape(4, 16, 128).transpose(2, 0, 1).reshape(128, 64))
    convb = np.ascontiguousarray(g("conv_b")[0].reshape(16, 128).T)
    shared = dict(
        w_in=g("w_in")[0], w_br_ssd=g("w_br_ssd")[0], w_br_att=g("w_br_att")[0], w_out=g("w_out")[0],
        w_gate=g("w_gate")[0], w_up=g("w_up")[0], w_down=g("w_down")[0],
        vecs=vecs, rows=rows, convw=convw, convb=convb, ident=consts["ident"], tri=consts["tri"], nm=consts["nm"],
        bt=bt, pow2=consts["pow2"])
    in_maps = []
    for b in cores:
        m = dict(shared)
        m["xin"] = np.ascontiguousarray(np.concatenate([meta, x_prompt[b], x_sample[b]], axis=0))
        m["cache_k"] = np.ascontiguousarray(g("cache_k")[0, b].reshape(PAST, 256))
        m["cache_v"] = np.ascontiguousarray(g("cache_v")[0, b].reshape(PAST, 256))
        m["cache_ki"] = np.ascontiguousarray(g("cache_kidx")[0, b])
        m["state_ssm"] = np.ascontiguousarray(g("state_ssm")[0, b].reshape(1024, 128))
        m["state_conv"] = np.ascontiguousarray(g("state_conv")[0, b])
        in_maps.append(m)
    return in_maps


_NC_CACHE = {}


def kernel(**inputs):
    if "nc" not in _NC_CACHE:
        _NC_CACHE["nc"] = build_program()
    nc = _NC_CACHE["nc"]
    cores = list(range(8))
    in_maps = make_in_maps(inputs, cores)
    res = run_bass_kernel_spmd(nc, in_maps, core_ids=cores)
    r = res.results
    st = lambda name: np.stack([np.asarray(r[b][name], dtype=np.float32) for b in cores], axis=0)
    y_prompt = st("y_p")
    y_sample = st("y_s")
    k_prompt = st("k_p").reshape(1, 8, TP, 2, 128)
    v_prompt = st("v_p").reshape(1, 8, TP, 2, 128)
    kidx_prompt = st("ki_p").reshape(1, 8, TP, 64)
    ssm_prompt = st("ssm_p").reshape(1, 8, 16, 64, 128)
    conv_prompt = st("conv_p").reshape(1, 8, 3, 2048)
    k_sample = st("k_s").reshape(1, 8, TS, 2, 128)
    v_sample = st("v_s").reshape(1, 8, TS, 2, 128)
    kidx_sample = st("ki_s").reshape(1, 8, TS, 64)
    ssm_sample = st("ssm_s").reshape(1, 8, 16, 64, 128)
    conv_sample = st("conv_s").reshape(1, 8, 3, 2048)
    return (y_prompt, y_sample, k_prompt, v_prompt, kidx_prompt, ssm_prompt, conv_prompt,
            k_sample, v_sample, kidx_sample, ssm_sample, conv_sample)
```

```python
import math
from contextlib import ExitStack

import numpy as np
import concourse.bass as bass
import concourse.mybir as mybir
from concourse.bass_utils import run_bass_kernel_spmd

F32 = mybir.dt.float32
BF16 = mybir.dt.bfloat16
ALU = mybir.AluOpType
AF = mybir.ActivationFunctionType
AX = mybir.AxisListType

ENGS = ("pe", "act", "dve", "pool", "sp")

D = 1024
SEQ = 2048
NMETA = 16
TP = NMETA + SEQ
TS = 64
PAST = 4096
NCOL = TP + TS
KC = 8
IN_DIM = 7256
C_Z, C_XBC, C_DT, C_Q, C_K, C_V, C_QI, C_KI, C_WI, C_GS, C_GA = (
    0, 1024, 3072, 3088, 4112, 4368, 4624, 5136, 5200, 5208, 6232)
DFF = 2816
NFF = 22
EPS = 1e-6
TOPK = 256
NBIS = 18
NEG = -30000.0
WI_SCALE = (8 ** -0.5) * (64 ** -0.5)


class Res:
    __slots__ = ("name", "w", "r", "dsem", "dcnt", "excl")

    def __init__(self, name, excl=False):
        self.name = name
        self.w = {}
        self.r = {}
        self.dsem = None
        self.dcnt = 0
        self.excl = excl


class Ins:
    __slots__ = ("eng", "fn", "waits", "signal", "ticket", "dma")

    def __init__(self, eng, fn):
        self.eng = eng
        self.fn = fn
        self.waits = []
        self.signal = False
        self.ticket = None
        self.dma = None


class Prog:
    def __init__(self, nc):
        self.nc = nc
        self.q = {e: [] for e in ENGS}
        self.dma_res = []
        self.last = {e: None for e in ENGS}
        self.pending = {e: [] for e in ENGS}
        self.final = []

    def _add(self, ins, waits):
        eng = ins.eng
        if self.pending[eng]:
            waits = waits + self.pending[eng]
            self.pending[eng] = []
        ins.waits = waits
        for ev in waits:
            if ev[0] == 'c':
                ev[1].signal = True
        self.q[eng].append(ins)

    def op(self, eng, fn, reads=(), writes=()):
        ins = Ins(eng, fn)
        waits = []
        for R in reads:
            for ev in R.w.values():
                if ev[0] == 'c' and ev[1].eng == eng and eng == "pe":
                    continue
                waits.append(ev)
            if R.excl:
                for ev in R.r.values():
                    if ev[0] == 'c' and ev[1].eng == eng:
                        continue
                    waits.append(ev)
        for R in writes:
            for ev in R.w.values():
                if ev[0] == 'c' and ev[1].eng == eng:
                    continue
                waits.append(ev)
            for ev in R.r.values():
                if ev[0] == 'c' and ev[1].eng == eng:
                    continue
                waits.append(ev)
        self._add(ins, waits)
        me = ('c', ins)
        for R in reads:
            R.r[eng] = me
        for R in writes:
            R.w = {eng: me}
            R.r = {}
        self.last[eng] = ins
        return ins

    def dma(self, eng, out, in_, reads=(), writes=(), sem_res=None, **kw):
        if sem_res is None:
            sem_res = writes[0] if writes else reads[0]
        if sem_res.dsem is None:
            sem_res.dsem = True
            self.dma_res.append(sem_res)

        def fn(e, out=out, in_=in_, kw=kw):
            return e.dma_start(out=out, in_=in_, **kw)
        ins = Ins(eng, fn)
        waits = []
        for R in reads:
            waits.extend(R.w.values())
        for R in writes:
            waits.extend(R.w.values())
            waits.extend(R.r.values())
        self._add(ins, waits)
        sem_res.dcnt += 16
        ins.dma = (sem_res, sem_res.dcnt)
        me = ('d', sem_res, sem_res.dcnt)
        key = ('d', id(sem_res))
        for R in reads:
            R.r[key] = me
        for R in writes:
            R.w = {key: me}
            R.r = {}
        return ins

    def dma_more(self, eng, out, in_, R, **kw):
        def fn(e, out=out, in_=in_, kw=kw):
            return e.dma_start(out=out, in_=in_, **kw)
        ins = Ins(eng, fn)
        self._add(ins, [])
        R.dcnt += 16
        ins.dma = (R, R.dcnt)
        R.w = {('d', id(R)): ('d', R, R.dcnt)}
        return ins

    def barrier(self):
        evs = []
        for e in ENGS:
            if self.last[e] is not None:
                evs.append(('c', self.last[e]))
        for R in self.dma_res:
            evs.append(('d', R, R.dcnt))
        for e in ENGS:
            self.pending[e] = self.pending[e] + [
                ev for ev in evs if not (ev[0] == 'c' and ev[1].eng == e)]

    def emit(self, stack):
        nc = self.nc
        esem = {e: stack.enter_context(nc.semaphore("s_" + e)) for e in ENGS}
        for i, R in enumerate(self.dma_res):
            R.dsem = stack.enter_context(nc.semaphore("d%d" % i))
        for e in ENGS:
            t = 0
            for ins in self.q[e]:
                if ins.signal:
                    t += 1
                    ins.ticket = t
        block = stack.enter_context(nc.Block())
        final = self.final

        def run(e, eo):
            seen = {}

            def wait(ev):
                if ev[0] == 'c':
                    sem, val, key = esem[ev[1].eng], ev[1].ticket, ev[1].eng
                else:
                    sem, val, key = ev[1].dsem, ev[2], id(ev[1])
                if seen.get(key, 0) >= val:
                    return
                seen[key] = val
                eo.wait_ge(sem, val)
            for ins in self.q[e]:
                for ev in ins.waits:
                    wait(ev)
                bi = ins.fn(eo)
                if ins.dma is not None:
                    bi.then_inc(ins.dma[0].dsem, 16)
                elif ins.signal:
                    bi.then_inc(esem[e], 1)
            if e == "sp":
                for R in final:
                    for ev in R.w.values():
                        wait(ev)

        @block.tensor
        def _(eo):
            run("pe", eo)

        @block.scalar
        def _(eo):
            run("act", eo)

        @block.vector
        def _(eo):
            run("dve", eo)

        @block.gpsimd
        def _(eo):
            run("pool", eo)

        @block.sync
        def _(eo):
            run("sp", eo)


class Arena:
    def __init__(self, base, nbytes):
        self.base = base
        self.nbytes = nbytes
        self.off = 0
        self.peak = 0
        self.peak_since_release = 0

    def alloc(self, n, dt):
        esz = 4 if dt == F32 else 2
        nb = (n * esz + 63) // 64 * 64
        assert self.off + nb <= self.nbytes, ("arena overflow", self.off, nb, self.nbytes)
        a = self.base[:, self.off // 2:(self.off + nb) // 2]
        self.off += nb
        self.peak = max(self.peak, self.off)
        self.peak_since_release = max(self.peak_since_release, self.off)
        if dt == F32:
            a = a.bitcast(F32)
        return a[:, 0:n]

    def mark(self):
        return self.off

    def release(self, m):
        self.off = m
        self.peak_since_release = m


class Ring:
    def __init__(self, arena, n, cnt, dt, name):
        self.bufs = [(arena.alloc(n, dt), Res("%s%d" % (name, i))) for i in range(cnt)]
        self.i = 0

    def next(self):
        b = self.bufs[self.i % len(self.bufs)]
        self.i += 1
        return b


def r3(ap, a):
    return ap.rearrange("p (a b) -> p a b", a=a)


def t5_bucket_np(rel):
    nb = 16
    max_exact = 8
    ret = np.where(rel > 0, nb, 0)
    n = np.abs(rel)
    nf = np.maximum(n, 1).astype(np.float32)
    large = max_exact + (np.log(nf / np.float32(max_exact)) / np.float32(math.log(128 / max_exact))
                         * np.float32(nb - max_exact)).astype(np.int32)
    large = np.minimum(large, nb - 1)
    return ret + np.where(n < max_exact, n, large)


def build_program(stage=99):
    nc = bass.Bass("TRN2", target_bir_lowering=False)
    dram = {}

    def IN(name, shape):
        dram[name] = nc.dram_tensor(name, list(shape), F32, kind="ExternalInput").ap()
        return dram[name]

    def OUT(name, shape):
        dram[name] = nc.dram_tensor(name, list(shape), F32, kind="ExternalOutput").ap()
        return dram[name]

    xin = IN("xin", [NCOL, D])
    cache_k = IN("cache_k", [PAST, 256])
    cache_v = IN("cache_v", [PAST, 256])
    cache_ki = IN("cache_ki", [PAST, 64])
    state_ssm = IN("state_ssm", [1024, 128])
    state_conv = IN("state_conv", [3, 2048])
    w_in = IN("w_in", [D, IN_DIM])
    w_br_ssd = IN("w_br_ssd", [D, D])
    w_br_att = IN("w_br_att", [D, D])
    w_out = IN("w_out", [D, D])
    w_gate = IN("w_gate", [D, DFF])
    w_up = IN("w_up", [D, DFF])
    w_down = IN("w_down", [DFF, D])
    vecs = IN("vecs", [128, 64])
    rows = IN("rows", [1, 512])
    convw = IN("convw", [128, 64])
    convb = IN("convb", [128, 16])
    ident_in = IN("ident", [128, 128])
    tri_in = IN("tri", [128, 3 * 128])
    nm_in = IN("nm", [128, 128])
    bt_in = IN("bt", [128, 2 * 8 * 128])
    pow2_in = IN("pow2", [1, 32])

    y_p = OUT("y_p", [SEQ, D])
    y_s = OUT("y_s", [TS, D])
    k_p = OUT("k_p", [TP, 256])
    v_p = OUT("v_p", [TP, 256])
    ki_p = OUT("ki_p", [TP, 64])
    ssm_p = OUT("ssm_p", [1024, 128])
    conv_p = OUT("conv_p", [3, 2048])
    k_s = OUT("k_s", [TS, 256])
    v_s = OUT("v_s", [TS, 256])
    ki_s = OUT("ki_s", [TS, 64])
    ssm_s = OUT("ssm_s", [1024, 128])
    conv_s = OUT("conv_s", [3, 2048])
    out_names = ("y_p", "y_s", "k_p", "v_p", "ki_p", "ssm_p", "conv_p", "k_s", "v_s", "ki_s", "ssm_s", "conv_s")
    out_res = {n: Res("o_" + n) for n in out_names}

    tiles = [("M", 0, 16)] + [("T%d" % j, 16 + 128 * (j - 1), 128) for j in range(1, 17)] + [("S", TP, TS)]
    NT = len(tiles)
    TS_IDX = NT - 1

    st = ExitStack()
    with st:
        P = Prog(nc)
        ARENA_BYTES = 212800
        arena_t = st.enter_context(nc.sbuf_tensor("arena", [128, ARENA_BYTES // 2], BF16))
        A = Arena(arena_t, ARENA_BYTES)
        psum_t = st.enter_context(nc.psum_tensor("psum", [128, 4096], F32))
        R_ps = [Res("ps%d" % b, excl=True) for b in range(8)]

        def psf(b, w=512, o=0):
            return psum_t[:, b * 512 + o: b * 512 + o + w]

        def psb(b, w=1024, o=0):
            return psum_t[:, b * 512:(b + 1) * 512].bitcast(BF16)[:, o:o + w]

        def MM(out, lhsT, rhs, start, stop, reads, writes):
            P.op("pe", lambda e: e.matmul(out, lhsT=lhsT, rhs=rhs, start=start, stop=stop), reads, writes)

        def TR(out, in_, ident, reads, writes):
            P.op("pe", lambda e: e.transpose(out, in_, ident), reads, writes)

        def ACT(out, in_, func, reads, writes, bias=None, scale=None, accum_out=None):
            kw = {}
            if bias is not None:
                kw["bias"] = bias
            if scale is not None:
                kw["scale"] = scale
            if accum_out is not None:
                kw["accum_out"] = accum_out
            P.op("act", lambda e: e.activation(out=out, in_=in_, func=func, **kw), reads, writes)

        def TS_(eng, out, in0, s1, s2, op0, op1, reads, writes, accum_out=None):
            if op1 is None:
                P.op(eng, lambda e: e.tensor_scalar(out=out, in0=in0, scalar1=s1, scalar2=None, op0=op0), reads, writes)
            elif accum_out is None:
                P.op(eng, lambda e: e.tensor_scalar(out=out, in0=in0, scalar1=s1, scalar2=s2, op0=op0, op1=op1),
                     reads, writes)
            else:
                P.op(eng, lambda e: e.tensor_scalar(out=out, in0=in0, scalar1=s1, scalar2=s2, op0=op0, op1=op1,
                                                    accum_out=accum_out), reads, writes)

        def TT(eng, out, in0, in1, op, reads, writes):
            P.op(eng, lambda e: e.tensor_tensor(out=out, in0=in0, in1=in1, op=op), reads, writes)

        def STT(eng, out, in0, scalar, in1, op0, op1, reads, writes):
            P.op(eng, lambda e: e.scalar_tensor_tensor(out=out, in0=in0, scalar=scalar, in1=in1, op0=op0, op1=op1),
                 reads, writes)

        def CP(eng, out, in_, reads, writes):
            if eng == "act":
                P.op("act", lambda e: e.activation(out=out, in_=in_, func=AF.Copy), reads, writes)
            else:
                P.op(eng, lambda e: e.tensor_copy(out=out, in_=in_), reads, writes)

        def MEMSET(eng, ap, val, writes):
            P.op(eng, lambda e: e.memset(ap, val), (), writes)

        def RECIP(out, in_, reads, writes):
            P.op("dve", lambda e: e.reciprocal(out=out, in_=in_), reads, writes)

        def wsrc(w, c0, c1):
            return w.rearrange("(kc p) c -> p kc c", p=128)[:, :, c0:c1]

        def load_w(dst3, R, w, c0, c1, step=1024):
            first = True
            for a in range(c0, c1, step):
                b = min(c1, a + step)
                if first:
                    P.dma("pool", dst3[:, :, a - c0:b - c0], wsrc(w, a, b), writes=[R], sem_res=R)
                    first = False
                else:
                    P.dma_more("pool", dst3[:, :, a - c0:b - c0], wsrc(w, a, b), R)

        ident_f = A.alloc(128, F32); R_identf = Res("identf")
        ident_b = A.alloc(128, BF16); R_identb = Res("identb")
        vec = A.alloc(64, F32); R_vec = Res("vec")
        rowb = A.alloc(512, F32); R_rowb = Res("rowb")
        mhalf = A.alloc(8, F32); R_mhalf = Res("mhalf")
        P.dma("sp", ident_f, ident_in, writes=[R_identf])
        P.dma("pool", ident_b, ident_in, writes=[R_identb])
        P.dma("sp", vec, vecs, writes=[R_vec])
        P.dma("sp", rowb, rows.partition_broadcast(128), writes=[R_rowb])
        MEMSET("pool", mhalf, -0.5, [R_mhalf])
        V_N1W, V_N2W, V_SNW, V_QNW = 0, 8, 16, 24
        RB_KNW, RB_KINW, RB_DTB, RB_ALOG, RB_DSK, RB_CB = 0, 128, 192, 208, 224, 240

        MG_BYTES = KC * NCOL * 2
        A.nbytes = ARENA_BYTES - MG_BYTES
        mergedT = r3(arena_t[:, (ARENA_BYTES - MG_BYTES) // 2: ARENA_BYTES // 2], KC)
        R_mg = [Res("mg%d" % t) for t in range(NT)]
        junk_act = A.alloc(D, BF16); R_junk_act = Res("junk_act")
        qnw_s = A.alloc(1, F32); R_qnws = Res("qnws")
        TS_("dve", qnw_s, vec[:, V_QNW:V_QNW + 1], 128.0 ** -0.5, None, ALU.mult, None, [R_vec], [R_qnws])
        yattT_s = r3(A.alloc(KC * TS, BF16), KC)
        persist_mark = A.mark()

        mm_banks = [0, 1, 2, 7]
        tr_banks = [3, 4]
        o_banks = [5, 6]
        cnt = {"mm": 0, "tr": 0, "o": 0, "at": 0, "fr": 0}
        bank_pools = {"mm": mm_banks, "tr": tr_banks, "o": o_banks, "at": [0, 1]}

        def nb(kind):
            pool = bank_pools[kind]
            b = pool[cnt[kind] % len(pool)]
            cnt[kind] += 1
            return b

        def rsqrt_small(out, in_, scale, n, w, reads, writes):
            TS_("pool", out, in_, scale, EPS, ALU.mult, ALU.add, reads, writes)
            TT("pool", out, out, mhalf[0:n, 0:w], ALU.pow, list(writes) + [R_mhalf], writes)

        class HN:
            def __init__(self, vcol, nxt=2, nhn=2):
                self.xt = Ring(A, D, nxt, F32, "xt") if nxt else None
                self.xn = Ring(A, D, 1, BF16, "xn")
                self.hn = Ring(A, KC * 128, nhn, BF16, "hn") if nhn else None
                self.sm = Ring(A, 4, 2, F32, "smh")
                self.vcol = vcol
                self.pool = "tr"

            def tile_gen(self, t, src=None, R_src=None, dst=None, R_dst=None, pool=None):
                pool = self.pool if pool is None else pool
                tname, col0, n = tiles[t]
                if src is None:
                    xt, R_xt = self.xt.next()
                    P.dma("sp", xt[0:n, :], xin[col0:col0 + n, :], writes=[R_xt])
                else:
                    xt, R_xt = src, R_src
                sm, R_sm = self.sm.next()
                if dst is None:
                    hn2, R_hn = self.hn.next()
                    hn = r3(hn2, KC)
                else:
                    hn, R_hn = dst, R_dst
                ACT(junk_act[0:n, :], xt[0:n, :], AF.Square, [R_xt], [R_junk_act, R_sm], accum_out=sm[0:n, 0:1])
                yield
                TS_("pool", sm[0:n, 1:2], sm[0:n, 0:1], 1.0 / D, EPS, ALU.mult, ALU.add, [R_sm], [R_sm])
                yield
                TT("pool", sm[0:n, 1:2], sm[0:n, 1:2], mhalf[0:n, 0:1], ALU.pow, [R_sm, R_mhalf], [R_sm])
                yield
                xn, R_xn = self.xn.next()
                if getattr(self, "mul_on_act", False):
                    ACT(xn[0:n, :], xt[0:n, :], AF.Identity, [R_xt, R_sm], [R_xn], scale=sm[0:n, 1:2])
                else:
                    TS_("dve", xn[0:n, :], xt[0:n, :], sm[0:n, 1:2], None, ALU.mult, None, [R_xt, R_sm], [R_xn])
                b = nb(pool)
                pt = r3(psb(b), 8)
                for kc in range(KC):
                    TR(pt[:, kc, 0:n], xn[0:n, kc * 128:(kc + 1) * 128], ident_b[0:n, 0:n], [R_xn, R_identb], [R_ps[b]])
                yield
                TT("dve", hn[:, :, 0:n], pt[:, :, 0:n],
                   vec[:, self.vcol:self.vcol + 8].unsqueeze(2).to_broadcast([128, 8, n]), ALU.mult,
                   [R_ps[b], R_vec], [R_hn])
                return hn, R_hn

            def tile(self, t, src=None, R_src=None, dst=None, R_dst=None):
                g = self.tile_gen(t, src, R_src, dst, R_dst)
                try:
                    while True:
                        next(g)
                except StopIteration as e:
                    return e.value

        def gate_branch(t, hn, R_hn, yT, R_yT, wbr, R_wbr, wg, R_wg, first, sg_ring, tmpm_ring):
            tname, col0, n = tiles[t]
            bbs = [nb("mm"), nb("mm")]
            bgs = [nb("mm"), nb("mm")]
            for half in range(2):
                pbr = r3(psf(bbs[half]), 4)
                for cc in range(4):
                    c = half * 4 + cc
                    for kc in range(KC):
                        MM(pbr[:, cc, 0:n], wbr[:, kc, c * 128:(c + 1) * 128], yT[:, kc, 0:n], kc == 0, kc == KC - 1,
                           [R_wbr, R_yT], [R_ps[bbs[half]]])
            for half in range(2):
                pg = r3(psf(bgs[half]), 4)
                for cc in range(4):
                    c = half * 4 + cc
                    for kc in range(KC):
                        MM(pg[:, cc, 0:n], wg[:, kc, c * 128:(c + 1) * 128], hn[:, kc, 0:n], kc == 0, kc == KC - 1,
                           [R_wg, R_hn], [R_ps[bgs[half]]])
            for half in range(2):
                pbr = r3(psf(bbs[half]), 4)
                pg = r3(psf(bgs[half]), 4)
                sg2, R_sg = sg_ring.next()
                sg = r3(sg2, 4)
                ACT(sg[:, :, 0:n], pg[:, :, 0:n], AF.Sigmoid, [R_ps[bgs[half]]], [R_sg])
                dst = mergedT[:, half * 4:(half + 1) * 4, col0:col0 + n]
                if first:
                    TT("dve", dst, pbr[:, :, 0:n], sg[:, :, 0:n], ALU.mult, [R_ps[bbs[half]], R_sg], [R_mg[t]])
                else:
                    tm2, R_tm = tmpm_ring.next()
                    tm = r3(tm2, 4)
                    TT("dve", tm[:, :, 0:n], pbr[:, :, 0:n], sg[:, :, 0:n], ALU.mult, [R_ps[bbs[half]], R_sg], [R_tm])
                    TT("pool", dst, dst, tm[:, :, 0:n], ALU.add, [R_tm, R_mg[t]], [R_mg[t]])

        if stage >= 2:
            hnS = HN(V_N1W)
            wS = r3(A.alloc(KC * 3088, BF16), KC); R_wS = Res("wS")
            R_wSz = Res("wSz")
            load_w(wS[:, :, C_XBC:3088], R_wS, w_in, C_XBC, 3088)
            load_w(wS[:, :, 0:C_XBC], R_wSz, w_in, 0, C_XBC)
            cw = A.alloc(64, F32); R_cw = Res("cw")
            P.dma("sp", cw, convw, writes=[R_cw])
            cb = A.alloc(16, F32); R_cb = Res("cb")
            P.dma("sp", cb, convb, writes=[R_cb])
            tri = r3(A.alloc(3 * 128, F32), 3); R_tri = Res("tri")
            P.dma("sp", tri, tri_in.rearrange("p (a b) -> p a b", a=3), writes=[R_tri])
            U_f, SL_f, ONES_f = tri[:, 0, :], tri[:, 1, :], tri[:, 2, :]
            NM = r3(A.alloc(4 * 128, BF16), 4); R_NM = Res("NM")
            for h in range(4):
                if h == 0:
                    P.dma("pool", NM[:, h, :], nm_in, writes=[R_NM], sem_res=R_NM)
                else:
                    P.dma_more("pool", NM[:, h, :], nm_in, R_NM)
            dconv = r3(A.alloc(64 * 128, BF16), 64); R_dconv = Res("dconv"); R_dconv2 = Res("dconv2")
            for i in range(64):
                if i < 32:
                    TS_("dve", dconv[:, i, :], ident_f, cw[:, i:i + 1], None, ALU.mult, None, [R_identf, R_cw], [R_dconv])
                else:
                    ACT(dconv[:, i, :], ident_f, AF.Copy, [R_identf, R_cw], [R_dconv2], scale=cw[:, i:i + 1])
            aneg = A.alloc(16, F32); R_aneg = Res("aneg")
            ACT(aneg, rowb[:, RB_ALOG:RB_ALOG + 16], AF.Exp, [R_rowb], [R_aneg])
            TS_("dve", aneg, aneg, -1.0, None, ALU.mult, None, [R_aneg], [R_aneg])
            ST = A.alloc(1024, F32); R_ST = Res("ST")
            Sb = A.alloc(1024, BF16); R_Sb = Res("Sb")
            hist = r3(A.alloc(16 * 3, BF16), 16); R_hist = Res("hist")
            Rm = A.alloc(2048, F32); R_Rm = Res("Rm")
            cvT = A.alloc(48, F32); R_cvT = Res("cvT")
            cvio = A.alloc(128, F32); R_cvio = Res("cvio")
            sso = r3(Rm[:, 0:1024], 8)
            stin = r3(Rm[:, 1024:2048], 8)
            xraw_ring = Ring(A, 16 * 131, 2, BF16, "xraw")
            xact_ring = Ring(A, 16 * 128, 2, BF16, "xact")
            xtok_ring = Ring(A, 1024, 2, BF16, "xtok")
            btok_ring = Ring(A, 512, 2, BF16, "btok")
            xdt_ring = Ring(A, 1024, 2, BF16, "xdt")
            xD_ring = Ring(A, 1024, 2, BF16, "xD")
            xdec_ring = Ring(A, 1024, 2, BF16, "xdec")
            Lt_ring = Ring(A, 2048, 2, BF16, "Lt")
            t1_ring = Ring(A, 1024, 2, F32, "t1")
            yn_ring = Ring(A, 1024, 2, BF16, "yn")
            szh_ring = Ring(A, 512, 2, F32, "szh")
            smS_ring = Ring(A, 160, 2, F32, "smS")
            etb_ring = Ring(A, 16, 2, F32, "etb")
            for par in range(2):
                bank_pools["s_mm%d" % par] = [0, 1] if par == 0 else [4, 5]
                bank_pools["s_tr%d" % par] = [2] if par == 0 else [6]
                bank_pools["s_o%d" % par] = [3] if par == 0 else [7]
                for k_ in ("s_mm", "s_tr", "s_o"):
                    cnt["%s%d" % (k_, par)] = 0
            flags = {"hist": -1, "state": -1}

            def ssd_gen(t, first, last, seq, par, prev_t, S):
                mmp, trp, op_ = "s_mm%d" % par, "s_tr%d" % par, "s_o%d" % par
                tname, col0, n = tiles[t]
                hn, R_hn = yield from hnS.tile_gen(t, pool=trp)
                yield
                sm, R_sm = smS_ring.next()
                xraw2, R_xraw = xraw_ring.next(); xraw = r3(xraw2, 16)
                xact2, R_xact = xact_ring.next(); xact = r3(xact2, 16)
                xtok, R_xtok = xtok_ring.next()
                btok, R_btok = btok_ring.next()
                xdt, R_xdt = xdt_ring.next()
                xD, R_xD = xD_ring.next()
                xdec, R_xdec = xdec_ring.next()
                Lt, R_Lt = Lt_ring.next()
                t1, R_t1 = t1_ring.next()
                yn, R_yn = yn_ring.next()
                etb, R_etb = etb_ring.next()
                if first and seq == "p":
                    MEMSET("pool", xraw[:, :, 0:3], 0.0, [R_xraw])
                elif first and seq == "s":
                    P.dma("sp", S.cvio[0:48, :], state_conv.rearrange("t (c f) -> (t c) f", f=128), writes=[S.R_cvio])
                    bt = nb(op_)
                    TR(psf(bt)[:, 0:48], S.cvio[0:48, :], ident_f[0:48, 0:48], [S.R_cvio, R_identf], [R_ps[bt]])
                    CP("dve", xraw[:, :, 0:3], psf(bt)[:, 0:48].rearrange("p (t c) -> p c t", t=3), [R_ps[bt]], [R_xraw])
                else:
                    while S.flags["hist"] < prev_t:
                        yield
                    CP("pool", xraw[:, :, 0:3], S.hist[:, :, :], [S.R_hist], [R_xraw])
                for hf in range(2):
                    banks = [nb(mmp), nb(mmp)]
                    for cc in range(8):
                        c = hf * 8 + cc
                        b = banks[cc // 4]
                        for kc in range(KC):
                            MM(r3(psf(b), 4)[:, cc % 4, 0:n], wS[:, kc, C_XBC + c * 128: C_XBC + (c + 1) * 128],
                               hn[:, kc, 0:n], kc == 0, kc == KC - 1, [R_wS, R_hn], [R_ps[b]])
                    yield
                    for q2 in range(2):
                        q = hf * 2 + q2
                        CP("act" if q2 == 0 else "dve", xraw[:, q * 4:(q + 1) * 4, 3:3 + n],
                           r3(psf(banks[q2]), 4)[:, :, 0:n], [R_ps[banks[q2]]], [R_xraw])
                        if last:
                            CP("dve", S.cvT.rearrange("p (t c) -> p c t", t=3)[:, q * 4:(q + 1) * 4, :],
                               r3(psf(banks[q2]), 4)[:, :, n - 3:n], [R_ps[banks[q2]]], [S.R_cvT])
                    yield
                CP("pool", S.hist[:, :, :], xraw[:, :, n:n + 3], [R_xraw], [S.R_hist])
                S.flags["hist"] = t
                if last:
                    bt = nb(op_)
                    TR(psf(bt)[0:48, 0:128], S.cvT[:, 0:48], ident_f, [S.R_cvT, R_identf], [R_ps[bt]])
                    CP("act", S.cvio[0:48, :], psf(bt)[0:48, 0:128], [R_ps[bt]], [S.R_cvio])
                    oname = "conv_p" if seq == "p" else "conv_s"
                    P.dma("sp", dram[oname].rearrange("t (c f) -> (t c) f", f=128), S.cvio[0:48, :], reads=[S.R_cvio],
                          writes=[out_res[oname]], sem_res=S.R_cvio)
                yield
                for hf in range(2):
                    banks = [nb(mmp), nb(mmp)]
                    for cc in range(8):
                        c = hf * 8 + cc
                        b = banks[cc // 4]
                        o = r3(psf(b), 4)[:, cc % 4, 0:n]
                        for i in range(4):
                            MM(o, dconv[:, i * 16 + c, :], xraw[:, c, i:i + n], i == 0, i == 3,
                               [R_dconv if i < 2 else R_dconv2, R_xraw], [R_ps[b]])
                    yield
                    for cc in range(8):
                        c = hf * 8 + cc
                        b = banks[cc // 4]
                        o = r3(psf(b), 4)[:, cc % 4, 0:n]
                        ACT(xact[:, c, 0:n], o, AF.Silu, [R_ps[b], R_cb], [R_xact], bias=cb[:, c:c + 1])
                    yield
                bx = nb(trp)
                for c in range(8):
                    TR(psb(bx)[0:n, c * 128:(c + 1) * 128], xact[:, c, 0:n], ident_b, [R_xact, R_identb], [R_ps[bx]])
                yield
                CP("act", xtok[0:n, :], psb(bx)[0:n, :], [R_ps[bx]], [R_xtok])
                yield
                bB = nb(trp)
                for c in range(4):
                    TR(psb(bB)[0:n, c * 128:(c + 1) * 128], xact[:, 8 + c, 0:n], ident_b, [R_xact, R_identb], [R_ps[bB]])
                yield
                CP("dve", btok[0:n, :], psb(bB)[0:n, 0:512], [R_ps[bB]], [R_btok])
                yield
                bd = nb(op_)
                for kc in range(KC):
                    MM(psf(bd)[0:n, 0:16], hn[:, kc, 0:n], wS[:, kc, C_DT:C_DT + 16], kc == 0, kc == KC - 1,
                       [R_hn, R_wS], [R_ps[bd]])
                yield
                TT("dve", sm[0:n, 0:16], psf(bd)[0:n, 0:16], rowb[0:n, RB_DTB:RB_DTB + 16], ALU.add,
                   [R_ps[bd], R_rowb], [R_sm])
                yield
                STT("dve", sm[0:n, 16:32], sm[0:n, 0:16], -1.0, sm[0:n, 0:16], ALU.mult, ALU.max, [R_sm], [R_sm])
                yield
                ACT(sm[0:n, 16:32], sm[0:n, 16:32], AF.Exp, [R_sm], [R_sm], scale=-1.0)
                yield
                ACT(sm[0:n, 16:32], sm[0:n, 16:32], AF.Ln, [R_sm], [R_sm], bias=1.0)
                yield
                STT("dve", sm[0:n, 32:48], sm[0:n, 0:16], 0.0, sm[0:n, 16:32], ALU.max, ALU.add, [R_sm], [R_sm])
                yield
                TT("dve", sm[0:n, 48:64], sm[0:n, 32:48], aneg[0:n, :], ALU.mult, [R_sm, R_aneg], [R_sm])
                yield
                dt_ = sm[0:n, 32:48]
                dtA = sm[0:n, 48:64]
                MM(psf(bd)[0:n, 16:32], U_f[0:n, 0:n], dtA, True, True, [R_tri, R_sm], [R_ps[bd]])
                MM(psf(bd)[:, 32:48], ONES_f[0:n, :], dtA, True, True, [R_tri, R_sm], [R_ps[bd]])
                yield
                CP("dve", sm[0:n, 64:80], psf(bd)[0:n, 16:32], [R_ps[bd]], [R_sm])
                ACT(sm[0:n, 80:96], psf(bd)[0:n, 16:32], AF.Exp, [R_ps[bd]], [R_sm])
                yield
                TT("dve", sm[0:n, 96:112], psf(bd)[0:n, 32:48], sm[0:n, 64:80], ALU.subtract, [R_ps[bd], R_sm], [R_sm])
                ACT(etb, psf(bd)[:, 32:48], AF.Exp, [R_ps[bd]], [R_etb])
                yield
                ACT(sm[0:n, 96:112], sm[0:n, 96:112], AF.Exp, [R_sm], [R_sm])
                yield
                ea = sm[0:n, 80:96]
                dec = sm[0:n, 96:112]
                xtok3 = r3(xtok, 16)
                TT("dve", r3(xdt, 16)[0:n], xtok3[0:n], dt_.unsqueeze(2).to_broadcast([n, 16, 64]), ALU.mult,
                   [R_xtok, R_sm], [R_xdt])
                TT("pool", r3(xD, 16)[0:n], xtok3[0:n], rowb[0:n, RB_DSK:RB_DSK + 16].unsqueeze(2).to_broadcast([n, 16, 64]),
                   ALU.mult, [R_xtok, R_rowb], [R_xD])
                yield
                TT("pool", r3(xdec, 16)[0:n], r3(xdt, 16)[0:n], dec.unsqueeze(2).to_broadcast([n, 16, 64]), ALU.mult,
                   [R_xdt, R_sm], [R_xdec])
                yield
                if not first:
                    while S.flags["state"] < prev_t:
                        yield
                byo = [nb(mmp), nb(mmp)]
                for g in range(4):
                    b = byo[g // 2]
                    MM(psf(b)[0:n, (g % 2) * 256:(g % 2 + 1) * 256], xact[:, 12 + g, 0:n], S.Sb[:, g * 256:(g + 1) * 256],
                       True, True, [R_xact, S.R_Sb], [R_ps[b]])
                yield
                for q in range(2):
                    TT("dve", r3(t1, 16)[0:n, q * 8:(q + 1) * 8, :], r3(psf(byo[q]), 8)[0:n],
                       ea[:, q * 8:(q + 1) * 8].unsqueeze(2).to_broadcast([n, 8, 64]), ALU.mult,
                       [R_ps[byo[q]], R_sm], [R_t1])
                yield
                bs = [nb(mmp), nb(mmp)]
                for g in range(4):
                    b = bs[g // 2]
                    MM(psf(b)[:, (g % 2) * 256:(g % 2 + 1) * 256], btok[0:n, g * 128:(g + 1) * 128],
                       xdec[0:n, g * 256:(g + 1) * 256], True, True, [R_btok, R_xdec], [R_ps[b]])
                TT("pool", r3(S.ST, 16), r3(S.ST, 16), etb.unsqueeze(2).to_broadcast([128, 16, 64]), ALU.mult,
                   [S.R_ST, R_etb], [S.R_ST])
                yield
                for q in range(2):
                    TT("dve", S.ST[:, q * 512:(q + 1) * 512], S.ST[:, q * 512:(q + 1) * 512], psf(bs[q]), ALU.add,
                       [S.R_ST, R_ps[bs[q]]], [S.R_ST])
                yield
                CP("pool", S.Sb, S.ST, [S.R_ST], [S.R_Sb])
                S.flags["state"] = t
                yield
                if last:
                    for q in range(2):
                        b = nb(mmp)
                        for cc in range(4):
                            c = q * 4 + cc
                            TR(psf(b)[:, cc * 128:(cc + 1) * 128], S.ST[:, c * 128:(c + 1) * 128], ident_f,
                               [S.R_ST, R_identf], [R_ps[b]])
                        CP("act", S.sso[:, q * 4:(q + 1) * 4, :], r3(psf(b), 4), [R_ps[b]], [S.R_sso])
                    oname = "ssm_p" if seq == "p" else "ssm_s"
                    P.dma("sp", dram[oname].rearrange("(c p) n -> p c n", p=128), S.sso, reads=[S.R_sso],
                          writes=[out_res[oname]], sem_res=S.R_sso)
                    yield
                Rm3 = r3(Rm, 16)
                TT("dve", Rm3[0:n, :, 0:n], U_f[0:n, 0:n].unsqueeze(1).to_broadcast([n, 16, n]),
                   dtA.unsqueeze(2).to_broadcast([n, 16, n]), ALU.mult, [R_tri, R_sm], [R_Rm])
                Lt3 = r3(Lt, 16)
                for q in range(4):
                    b = nb(mmp)
                    o = r3(psf(b), 4)[0:n, :, 0:n]
                    if n == 128:
                        MM(o, SL_f[0:n, 0:n], Rm3[0:n, q * 4:(q + 1) * 4, 0:n], True, False, [R_tri, R_Rm], [R_ps[b]])
                        MM(o, ident_b[0:n, 0:n], NM[0:n, :, 0:n], False, True, [R_identb, R_NM], [R_ps[b]])
                    else:
                        for r_ in range(4):
                            MM(o[:, r_, :], SL_f[0:n, 0:n], Rm3[0:n, q * 4 + r_, 0:n], True, False, [R_tri, R_Rm],
                               [R_ps[b]])
                            MM(o[:, r_, :], ident_b[0:n, 0:n], NM[0:n, r_, 0:n], False, True, [R_identb, R_NM],
                               [R_ps[b]])
                    ACT(Lt3[0:n, q * 4:(q + 1) * 4, 0:n], o, AF.Exp, [R_ps[b]], [R_Lt])
                yield
                bc = nb(op_)
                pcb = r3(psf(bc), 4)
                for g in range(4):
                    MM(pcb[0:n, g, 0:n], xact[:, 8 + g, 0:n], xact[:, 12 + g, 0:n], True, True, [R_xact], [R_ps[bc]])
                yield
                Lt4 = Lt.rearrange("p (g r i) -> p g r i", g=4, r=4)
                TT("dve", Lt4[0:n, :, :, 0:n], Lt4[0:n, :, :, 0:n],
                   pcb[0:n, :, 0:n].unsqueeze(2).to_broadcast([n, 4, 4, n]), ALU.mult, [R_Lt, R_ps[bc]], [R_Lt])
                yield
                byd = [nb(mmp), nb(mmp)]
                for h in range(16):
                    b = byd[h // 8]
                    o = psf(b)[0:n, (h % 8) * 64:(h % 8 + 1) * 64]
                    MM(o, Lt3[0:n, h, 0:n], xdt[0:n, h * 64:(h + 1) * 64], True, False, [R_Lt, R_xdt], [R_ps[b]])
                    MM(o, ident_b[0:n, 0:n], xD[0:n, h * 64:(h + 1) * 64], False, True, [R_identb, R_xD], [R_ps[b]])
                yield
                for q in range(2):
                    TT("dve", t1[0:n, q * 512:(q + 1) * 512], psf(byd[q])[0:n, :], t1[0:n, q * 512:(q + 1) * 512], ALU.add,
                       [R_ps[byd[q]], R_t1], [R_t1])
                yield
                for q in range(2):
                    bz = nb(mmp)
                    for kc in range(KC):
                        MM(psf(bz)[0:n, :], hn[:, kc, 0:n], wS[:, kc, C_Z + q * 512:C_Z + (q + 1) * 512], kc == 0,
                           kc == KC - 1, [R_hn, R_wSz], [R_ps[bz]])
                    szh, R_szh = szh_ring.next()
                    ACT(szh[0:n, :], psf(bz)[0:n, :], AF.Silu, [R_ps[bz]], [R_szh])
                    yield
                    TT("pool", t1[0:n, q * 512:(q + 1) * 512], t1[0:n, q * 512:(q + 1) * 512], szh[0:n, :], ALU.mult,
                       [R_t1, R_szh], [R_t1])
                    yield
                ACT(junk_act[0:n, :], t1[0:n, :], AF.Square, [R_t1], [R_junk_act, R_sm], accum_out=sm[0:n, 112:113])
                yield
                TS_("pool", sm[0:n, 113:114], sm[0:n, 112:113], 1.0 / 1024, EPS, ALU.mult, ALU.add, [R_sm], [R_sm])
                yield
                TT("pool", sm[0:n, 113:114], sm[0:n, 113:114], mhalf[0:n, 0:1], ALU.pow, [R_sm, R_mhalf], [R_sm])
                yield
                TS_("dve", yn[0:n, :], t1[0:n, :], sm[0:n, 113:114], None, ALU.mult, None, [R_t1, R_sm], [R_yn])
                yield
                bt = nb(trp)
                pt = r3(psb(bt), 8)
                for kc in range(KC):
                    TR(pt[:, kc, 0:n], yn[0:n, kc * 128:(kc + 1) * 128], ident_b[0:n, 0:n], [R_yn, R_identb], [R_ps[bt]])
                yield
                TT("dve", mergedT[:, :, col0:col0 + n], pt[:, :, 0:n],
                   vec[:, V_SNW:V_SNW + 8].unsqueeze(2).to_broadcast([128, 8, n]),
                   ALU.mult, [R_ps[bt], R_vec], [R_mg[t]])
                yield

            def run_two(gens, lag):
                gens = list(gens)
                active = []
                nxt = 0
                while nxt < len(gens) or active:
                    if not active or (len(active) == 1 and nxt < len(gens) and active[0][1] >= lag):
                        if nxt < len(gens):
                            active.append([gens[nxt], 0])
                            nxt += 1
                    for a_ in list(active):
                        try:
                            next(a_[0])
                            a_[1] += 1
                        except StopIteration:
                            active.remove(a_)

            class SeqState:
                pass
            Sp = SeqState()
            Sp.ST, Sp.R_ST, Sp.Sb, Sp.R_Sb, Sp.hist, Sp.R_hist = ST, R_ST, Sb, R_Sb, hist, R_hist
            Sp.cvT, Sp.R_cvT, Sp.cvio, Sp.R_cvio, Sp.sso, Sp.R_sso = cvT, R_cvT, cvio, R_cvio, sso, R_Rm
            Sp.flags = {"hist": -1, "state": -1}
            Ss = SeqState()
            Ss.ST = A.alloc(1024, F32); Ss.R_ST = Res("STs")
            Ss.Sb = A.alloc(1024, BF16); Ss.R_Sb = Res("Sbs")
            Ss.hist = r3(A.alloc(16 * 3, BF16), 16); Ss.R_hist = Res("hists")
            Ss.cvT = A.alloc(48, F32); Ss.R_cvT = Res("cvTs")
            Ss.cvio = A.alloc(128, F32); Ss.R_cvio = Res("cvios")
            Ss.sso = r3(A.alloc(1024, F32), 8); Ss.R_sso = Res("ssos")
            Ss.flags = {"hist": -1, "state": -1}
            P.dma("sp", stin, state_ssm.rearrange("(c p) n -> p c n", p=128), reads=[], writes=[R_Rm])
            for q in range(2):
                b = nb("mm")
                for cc in range(4):
                    c = q * 4 + cc
                    TR(psf(b)[:, cc * 128:(cc + 1) * 128], stin[:, c, :], ident_f, [R_Rm, R_identf], [R_ps[b]])
                CP("dve", Ss.ST[:, q * 512:(q + 1) * 512], psf(b), [R_ps[b]], [Ss.R_ST])
            CP("pool", Ss.Sb, Ss.ST, [Ss.R_ST], [Ss.R_Sb])
            MEMSET("dve", ST, 0.0, [R_ST])
            MEMSET("pool", Sb, 0.0, [R_Sb])
            gens = [ssd_gen(t, t == 0, t == NT - 2, "p", t % 2, t - 1, Sp) for t in range(0, NT - 1)]
            gens.append(ssd_gen(TS_IDX, True, True, "s", TS_IDX % 2, None, Ss))
            run_two(gens, 24)
            hnS.pool = "tr"
            print("pass S1 arena peak", A.peak)
            A.release(persist_mark)
            P.barrier()

        if stage >= 3:
            XBYTES = 33792
            regX = A.alloc(XBYTES // 2, BF16)
            afterX_mark = A.mark()
            yattT = r3(regX[:, 0:KC * TP], KC)
            hnA = HN(V_N1W, nxt=1, nhn=2)
            hnA.mul_on_act = True
            R_yattT = [Res("yat%d" % t) for t in range(NT)]
            NWA = C_GS - C_Q
            wA_off = A.mark()
            wA_raw = A.alloc(KC * NWA, BF16)
            wA_end = A.mark()
            wA = r3(wA_raw, KC); R_wA = Res("wA")
            load_w(wA, R_wA, w_in, C_Q, C_GS)
            WQ, WK, WV, WQI, WKI, WWI = 0, C_K - C_Q, C_V - C_Q, C_QI - C_Q, C_KI - C_Q, C_WI - C_Q
            bt2 = A.alloc(2 * 8 * 128, F32); R_bt = Res("bt")
            P.dma("sp", bt2, bt_in, writes=[R_bt])
            bt = bt2.rearrange("p (w h i) -> p w h i", w=2, h=8)
            BT_PREV, BT_SAME, BT_META = 0, 1, 2
            qb_ring = Ring(A, D, 1, BF16, "qb")
            qT_ring = Ring(A, 8 * 128, 3, BF16, "qT")
            qiT_ring = Ring(A, 4 * 128, 2, BF16, "qiT")
            ko_ring = Ring(A, 256, 1, F32, "ko")
            vo_ring = Ring(A, 256, 1, F32, "vo")
            kio_ring = Ring(A, 64, 2, F32, "kio")
            kb_ring = Ring(A, 256, 2, BF16, "kb")
            ki2_ring = Ring(A, 128, 2, BF16, "ki2")
            r_ring = Ring(A, 512, 2, F32, "rbuf")
            pm_ring = Ring(A, 512, 4, BF16, "pm")
            ya_ring = Ring(A, D, 1, BF16, "ya")
            smA_ring = Ring(A, 96, 3, F32, "smA")
            bis_ring = Ring(A, 64, 2, F32, "bis")
            pow2 = A.alloc(32, F32); R_pow2 = Res("pow2")
            P.dma("sp", pow2, pow2_in.partition_broadcast(128), writes=[R_pow2])
            commonA_mark = A.mark()
            bank_pools["at"] = [0, 1]
            bank_pools["fr"] = [2, 3]
            hnA.pool = "fr"

            class TileCtx:
                pass

            def front(t, B, ktile_idx, key_tiles, bias_kind, adm_fill, outs, fr):
                kT, vb, kiT2 = B["kT"], B["vb"], B["kiT2"]
                R_k = B["R_k"]
                score, R_score = B["score_ring"].next()
                bis, R_bis = bis_ring.next()
                hn, R_hn = yield from hnA.tile_gen(t, pool=fr)
                yield
                mask, R_mask = B["mask_ring"].next()
                maskT2, R_maskT = B["maskT_ring"].next()
                maskT = r3(maskT2, B["NKT"])
                tname, col0, n = tiles[t]
                kc0, nk_self = key_tiles[ktile_idx]
                L = kc0 + nk_self
                sm, R_sm = smA_ring.next()
                ctx = TileCtx()
                ctx.sm, ctx.R_sm, ctx.maskT, ctx.R_maskT = sm, R_sm, maskT, R_maskT
                qb, R_qb = qb_ring.next()
                qbanks = [nb(fr), nb(fr)]
                for half in range(2):
                    b = qbanks[half]
                    for kc in range(KC):
                        MM(psf(b)[0:n, :], hn[:, kc, 0:n], wA[:, kc, WQ + half * 512: WQ + (half + 1) * 512],
                           kc == 0, kc == KC - 1, [R_hn, R_wA], [R_ps[b]])
                yield
                for half in range(2):
                    b = qbanks[half]
                    for hh in range(4):
                        h = half * 4 + hh
                        ACT(junk_act[0:n, 0:128], psf(b)[0:n, hh * 128:(hh + 1) * 128], AF.Square, [R_ps[b]],
                            [R_junk_act, R_sm], accum_out=sm[0:n, h:h + 1])
                yield
                TS_("pool", sm[0:n, 8:16], sm[0:n, 0:8], 1.0 / 128, EPS, ALU.mult, ALU.add, [R_sm], [R_sm])
                yield
                TT("pool", sm[0:n, 8:16], sm[0:n, 8:16], mhalf[0:n, 0:8], ALU.pow, [R_sm, R_mhalf], [R_sm])
                yield
                for half in range(2):
                    b = qbanks[half]
                    TT("dve", r3(qb[0:n, half * 512:(half + 1) * 512], 4), r3(psf(b)[0:n, :], 4),
                       sm[0:n, 8 + half * 4:12 + half * 4].unsqueeze(2).to_broadcast([n, 4, 128]), ALU.mult,
                       [R_ps[b], R_sm], [R_qb])
                qT2, R_qT = qT_ring.next()
                qT = r3(qT2, 8)
                ctx.qT, ctx.R_qT = qT, R_qT
                btq = nb(fr)
                ptq = r3(psb(btq), 8)
                for h in range(8):
                    TR(ptq[:, h, 0:n], qb[0:n, h * 128:(h + 1) * 128], ident_b[0:n, 0:n], [R_qb, R_identb], [R_ps[btq]])
                bkv = nb(fr)
                for kc in range(KC):
                    MM(psf(bkv)[0:n, :], hn[:, kc, 0:n], wA[:, kc, WK:WK + 512], kc == 0, kc == KC - 1,
                       [R_hn, R_wA], [R_ps[bkv]])
                yield
                ACT(qT[:, :, 0:n], ptq[:, :, 0:n], AF.Identity, [R_ps[btq], R_qnws], [R_qT], scale=qnw_s[:, 0:1])
                bki = nb(fr)
                for kc in range(KC):
                    MM(psf(bki)[0:n, 0:72], hn[:, kc, 0:n], wA[:, kc, WKI:WKI + 72], kc == 0, kc == KC - 1,
                       [R_hn, R_wA], [R_ps[bki]])
                yield
                for g in range(2):
                    ACT(junk_act[0:n, 0:128], psf(bkv)[0:n, g * 128:(g + 1) * 128], AF.Square, [R_ps[bkv]],
                        [R_junk_act, R_sm], accum_out=sm[0:n, 16 + g:17 + g])
                ACT(junk_act[0:n, 0:64], psf(bki)[0:n, 0:64], AF.Square, [R_ps[bki]], [R_junk_act, R_sm],
                    accum_out=sm[0:n, 20:21])
                ACT(sm[0:n, 24:32], psf(bki)[0:n, 64:72], AF.Abs, [R_ps[bki]], [R_sm], scale=WI_SCALE)
                ACT(sm[0:n, 32:40], psf(bki)[0:n, 64:72], AF.Sign, [R_ps[bki]], [R_sm])
                yield
                TS_("pool", sm[0:n, 18:20], sm[0:n, 16:18], 1.0 / 128, EPS, ALU.mult, ALU.add, [R_sm], [R_sm])
                TS_("pool", sm[0:n, 21:22], sm[0:n, 20:21], 1.0 / 64, EPS, ALU.mult, ALU.add, [R_sm], [R_sm])
                yield
                TT("pool", sm[0:n, 18:22], sm[0:n, 18:22], mhalf[0:n, 0:4], ALU.pow, [R_sm, R_mhalf], [R_sm])
                yield
                ko, R_ko = ko_ring.next()
                vo, R_vo = vo_ring.next()
                b = bkv
                for g in range(2):
                    STT("dve", ko[0:n, g * 128:(g + 1) * 128], psf(b)[0:n, g * 128:(g + 1) * 128], sm[0:n, 18 + g:19 + g],
                        rowb[0:n, RB_KNW:RB_KNW + 128], ALU.mult, ALU.mult, [R_ps[b], R_sm, R_rowb], [R_ko])
                CP("act", vo[0:n, :], psf(b)[0:n, 256:512], [R_ps[b]], [R_vo])
                CP("act", vb[0:n, ktile_idx, :, 0:128], r3(psf(b)[0:n, 256:512], 2), [R_ps[b]], [R_k[ktile_idx]])
                P.dma("sp", outs["k"], ko[0:n, :], reads=[R_ko], writes=[out_res[outs["kn"]]], sem_res=R_ko)
                P.dma("sp", outs["v"], vo[0:n, :], reads=[R_vo], writes=[out_res[outs["vn"]]], sem_res=R_vo)
                b = bki
                kio, R_kio = kio_ring.next()
                STT("dve", kio[0:n, :], psf(b)[0:n, 0:64], sm[0:n, 21:22], rowb[0:n, RB_KINW:RB_KINW + 64],
                    ALU.mult, ALU.mult, [R_ps[b], R_sm, R_rowb], [R_kio])
                P.dma("sp", outs["ki"], kio[0:n, :], reads=[R_kio], writes=[out_res[outs["kin"]]], sem_res=R_kio)
                yield
                kb, R_kb = kb_ring.next()
                CP("pool", kb[0:n, :], ko[0:n, :], [R_ko], [R_kb])
                ki2, R_ki2 = ki2_ring.next()
                CP("pool", r3(ki2[0:n, :], 2), kio[0:n, :].unsqueeze(1).to_broadcast([n, 2, 64]), [R_kio], [R_ki2])
                yield
                bt_ = nb(fr)
                pt = r3(psb(bt_), 8)
                for g in range(2):
                    TR(pt[:, g, 0:n], kb[0:n, g * 128:(g + 1) * 128], ident_b[0:n, 0:n], [R_kb, R_identb], [R_ps[bt_]])
                bt2_ = nb(fr)
                TR(psb(bt2_)[:, 0:n], ki2[0:n, :], ident_b[0:n, 0:n], [R_ki2, R_identb], [R_ps[bt2_]])
                yield
                CP("act", kT[:, :, kc0:kc0 + n], pt[:, 0:2, 0:n], [R_ps[bt_]], [R_k[ktile_idx]])
                CP("act", kiT2[:, kc0:kc0 + n], psb(bt2_)[:, 0:n], [R_ps[bt2_]], [R_k[ktile_idx]])
                yield
                if L > TOPK:
                    qiT2, R_qiT = qiT_ring.next()
                    qiT = r3(qiT2, 4)
                    b = nb(fr)
                    pq = r3(psf(b), 4)
                    for m in range(4):
                        for kc in range(KC):
                            MM(pq[:, m, 0:n], wA[:, kc, WQI + m * 128: WQI + (m + 1) * 128], hn[:, kc, 0:n],
                               kc == 0, kc == KC - 1, [R_hn, R_wA], [R_ps[b]])
                    CP("act", qiT[:, :, 0:n], pq[:, :, 0:n], [R_ps[b]], [R_qiT])
                    for c0_ in range(0, L, 512):
                        c1_ = min(L, c0_ + 512)
                        wd = c1_ - c0_
                        rk = [R_k[i] for i, (kc_, nk_) in enumerate(key_tiles[:ktile_idx + 1])
                              if kc_ < c1_ and kc_ + nk_ > c0_]
                        for h in range(8):
                            m, hh = h // 2, h % 2
                            b = nb(fr)
                            MM(psf(b)[0:n, 0:wd], qiT[hh * 64:(hh + 1) * 64, m, 0:n],
                               kiT2[hh * 64:(hh + 1) * 64, c0_:c1_], True, True, [R_qiT] + rk, [R_ps[b]])
                            rb, R_rb = r_ring.next()
                            ACT(rb[0:n, 0:wd], psf(b)[0:n, 0:wd], AF.Relu, [R_ps[b], R_sm], [R_rb],
                                scale=sm[0:n, 24 + h:25 + h])
                            eng = "dve"
                            if h == 0:
                                TS_("dve", score[0:n, c0_:c1_], rb[0:n, 0:wd], sm[0:n, 32:33], None, ALU.mult, None,
                                    [R_rb, R_sm], [R_score])
                            else:
                                STT(eng, score[0:n, c0_:c1_], rb[0:n, 0:wd], sm[0:n, 32 + h:33 + h],
                                    score[0:n, c0_:c1_], ALU.mult, ALU.add, [R_rb, R_sm, R_score], [R_score])
                            if h % 4 == 3:
                                yield
                    P.op("dve", lambda e: e.tensor_reduce(out=sm[0:n, 40:41], in_=score[0:n, 0:L], axis=AX.X,
                                                          op=ALU.max, apply_absolute_value=True), [R_score], [R_sm])
                    if adm_fill is not None:
                        (r0, r1, fc0, fc1) = adm_fill
                        MEMSET("dve", score[r0:r1, fc0:fc1], -1e30, [R_score])
                    TS_("dve", sm[0:n, 41:42], sm[0:n, 40:41], 1.0, None, ALU.add, None, [R_sm], [R_sm])
                    TS_("dve", bis[0:n, 0:NBIS + 2], pow2[0:n, 0:NBIS + 2], sm[0:n, 41:42], None, ALU.mult, None,
                        [R_pow2, R_sm], [R_bis])
                    MEMSET("dve", sm[0:n, 42:43], 0.0, [R_sm])
                    yield
                    for k in range(1, NBIS + 1):
                        TS_("dve", mask[0:n, 0:L], score[0:n, 0:L], sm[0:n, 42:43], 0.0, ALU.is_ge, ALU.add,
                            [R_score, R_sm], [R_mask, R_sm], accum_out=sm[0:n, 43:44])
                        TS_("dve", sm[0:n, 44:45], sm[0:n, 43:44], float(TOPK), 0.5, ALU.is_ge, ALU.subtract,
                            [R_sm], [R_sm])
                        STT("dve", sm[0:n, 42:43], sm[0:n, 44:45], bis[0:n, k - 1:k], sm[0:n, 42:43], ALU.mult, ALU.add,
                            [R_sm, R_bis], [R_sm])
                        yield
                    STT("dve", sm[0:n, 45:46], bis[0:n, NBIS:NBIS + 1], -1.0, sm[0:n, 42:43], ALU.mult, ALU.add,
                        [R_sm, R_bis], [R_sm])
                    TS_("dve", mask[0:n, 0:L], score[0:n, 0:L], sm[0:n, 45:46], None, ALU.is_ge, None,
                        [R_score, R_sm], [R_mask])
                    yield
                else:
                    MEMSET("pool", mask[0:n, 0:L], 1.0, [R_mask])
                    if adm_fill is not None:
                        (r0, r1, fc0, fc1) = adm_fill
                        MEMSET("pool", mask[r0:r1, fc0:fc1], 0.0, [R_mask])
                nkt = ktile_idx + 1
                for k0 in range(0, nkt, 8):
                    k1 = min(nkt, k0 + 8)
                    bt_ = nb(fr)
                    pt = r3(psb(bt_), 8)
                    for kt in range(k0, k1):
                        kc_, nk_ = key_tiles[kt]
                        TR(pt[0:nk_, kt - k0, 0:n], mask[0:n, kc_:kc_ + nk_], ident_b[0:n, 0:n], [R_mask, R_identb],
                           [R_ps[bt_]])
                    ACT(maskT[:, k0:k1, 0:n], pt[:, 0:k1 - k0, 0:n], AF.Identity, [R_ps[bt_]], [R_maskT],
                        scale=-NEG, bias=NEG)
                    yield
                B["ctx"][t] = ctx

            def attn(t, ctx, B, ktile_idx, key_tiles, bias_kind, yT_dst, R_yT):
                kT, vb = B["kT"], B["vb"]
                R_k = B["R_k"]
                tname, col0, n = tiles[t]
                sm, R_sm, maskT, R_maskT, qT, R_qT = ctx.sm, ctx.R_sm, ctx.maskT, ctx.R_maskT, ctx.qT, ctx.R_qT
                nkt = ktile_idx + 1
                near = {kt: bias_kind[kt] for kt in bias_kind}
                far_tiles = [kt for kt in range(nkt) if kt not in near]
                groups = [far_tiles[i:i + 4] for i in range(0, len(far_tiles), 4)]
                near_tiles = sorted(near.keys())
                if near_tiles:
                    groups.append(near_tiles)
                ya, R_ya = ya_ring.next()
                items = [(h, gi) for h in range(8) for gi in range(len(groups))]
                pend = []
                state = {}

                def stage1(h, gi):
                    g = h // 4
                    grp = groups[gi]
                    isnear = grp[0] in near
                    b = nb("at")
                    ps3 = r3(psf(b), 4)
                    for j, kt in enumerate(grp):
                        kc_, nk_ = key_tiles[kt]
                        o = ps3[0:nk_, j, 0:n]
                        MM(o, kT[:, g, kc_:kc_ + nk_], qT[:, h, 0:n], True, False, [R_k[kt], R_qT], [R_ps[b]])
                        MM(o, ident_b[0:nk_, 0:nk_], maskT[0:nk_, kt, 0:n], False, not isnear,
                           [R_identb, R_maskT], [R_ps[b]])
                        if isnear:
                            kind = near[kt]
                            if kind == BT_META:
                                MM(o, ident_f[:, 112:112 + nk_], bt[:, BT_PREV, h, 0:n], False, True,
                                   [R_identf, R_bt], [R_ps[b]])
                            else:
                                MM(o, ident_f[0:nk_, 0:nk_], bt[0:nk_, kind, h, 0:n], False, True,
                                   [R_identf, R_bt], [R_ps[b]])
                    ng = len(grp)
                    pm2, R_pm = pm_ring.next()
                    pm = r3(pm2, 4)
                    if isnear:
                        ACT(pm[:, 0:ng, 0:n], ps3[:, 0:ng, 0:n], AF.Exp, [R_ps[b]], [R_pm])
                    else:
                        ACT(pm[:, 0:ng, 0:n], ps3[:, 0:ng, 0:n], AF.Exp, [R_ps[b], R_rowb], [R_pm],
                            bias=rowb[:, RB_CB + h:RB_CB + h + 1])
                    return (h, gi, pm, R_pm)

                def stage2(h, gi, pm, R_pm):
                    g = h // 4
                    grp = groups[gi]
                    if gi == 0:
                        state["bo"] = nb("o")
                    bo = state["bo"]
                    O = psf(bo)[0:n, 0:129]
                    for j, kt in enumerate(grp):
                        kc_, nk_ = key_tiles[kt]
                        first = (gi == 0 and j == 0)
                        last = (gi == len(groups) - 1 and j == len(grp) - 1)
                        MM(O, pm[0:nk_, j, 0:n], vb[0:nk_, kt, g, :], first, last, [R_pm, R_k[kt]], [R_ps[bo]])
                    if gi == len(groups) - 1:
                        RECIP(sm[0:n, 48 + h:49 + h], psf(bo)[0:n, 128:129], [R_ps[bo]], [R_sm])
                        ACT(ya[0:n, h * 128:(h + 1) * 128], psf(bo)[0:n, 0:128], AF.Identity, [R_ps[bo], R_sm], [R_ya],
                            scale=sm[0:n, 48 + h:49 + h])

                SKEW = 3
                for it in items:
                    pend.append(stage1(*it))
                    if len(pend) > SKEW:
                        stage2(*pend.pop(0))
                    yield
                while pend:
                    stage2(*pend.pop(0))
                bt_ = nb("at")
                pt = r3(psb(bt_), 8)
                for kc in range(KC):
                    TR(pt[:, kc, 0:n], ya[0:n, kc * 128:(kc + 1) * 128], ident_b[0:n, 0:n], [R_ya, R_identb], [R_ps[bt_]])
                CP("act", yT_dst, pt[:, :, 0:n], [R_ps[bt_]], [R_yT])
                yield

            def key_bufs(LMAX, NKT, nmask, pre_alloc=None):
                B = {"NKT": NKT, "ctx": {}}
                if pre_alloc is None:
                    B["kT"] = r3(A.alloc(2 * LMAX, BF16), 2)
                    B["vb"] = A.alloc(NKT * 2 * 129, BF16).rearrange("p (t g d) -> p t g d", t=NKT, g=2)
                else:
                    B["kT"] = r3(pre_alloc[:, 0:2 * LMAX], 2)
                    B["vb"] = pre_alloc[:, 2 * LMAX:2 * LMAX + NKT * 2 * 129].rearrange("p (t g d) -> p t g d",
                                                                                      t=NKT, g=2)
                B["kiT2"] = A.alloc(LMAX, BF16)
                raws = [A.alloc(2 * LMAX, BF16) for _ in range(nmask)]
                B["score_raw"] = raws[0]

                class _SR:
                    def __init__(self, items):
                        self.items = items
                        self.i = 0

                    def next(self):
                        it = self.items[self.i % len(self.items)]
                        self.i += 1
                        return it
                B["score_ring"] = _SR([(r_.bitcast(F32)[:, 0:LMAX], Res("score%d" % i)) for i, r_ in enumerate(raws)])
                B["mask_ring"] = Ring(A, LMAX, nmask, BF16, "mask")
                B["maskT_ring"] = Ring(A, NKT * 128, nmask + 1 if nmask > 1 else 1, BF16, "maskT")
                B["R_k"] = [Res("kt%d" % i) for i in range(NKT)]
                MEMSET("pool", B["vb"][:, :, :, 128:129], 1.0, B["R_k"])
                return B

            def gate_gen(t, hn, R_hn, yT, R_yT):
                tname, col0, n = tiles[t]
                for half in range(2):
                    bb = 7
                    pbr = r3(psf(bb), 4)
                    for cc in range(4):
                        c = half * 4 + cc
                        for kc in range(KC):
                            MM(pbr[:, cc, 0:n], wbrA[:, kc, c * 128:(c + 1) * 128], yT[:, kc, 0:n], kc == 0,
                               kc == KC - 1, [R_wbrA, R_yT], [R_ps[bb]])
                    yield
                    bg = 4
                    pg = r3(psf(bg), 4)
                    for cc in range(4):
                        c = half * 4 + cc
                        for kc in range(KC):
                            MM(pg[:, cc, 0:n], wgA[:, kc, c * 128:(c + 1) * 128], hn[:, kc, 0:n], kc == 0,
                               kc == KC - 1, [R_wgA, R_hn], [R_ps[bg]])
                    yield
                    sg2, R_sg = sgA_ring.next()
                    sg = r3(sg2, 4)
                    ACT(sg[:, :, 0:n], pg[:, :, 0:n], AF.Sigmoid, [R_ps[bg]], [R_sg])
                    yield
                    tm2, R_tm = tmA_ring.next()
                    tm = r3(tm2, 4)
                    TT("dve", tm[:, :, 0:n], pbr[:, :, 0:n], sg[:, :, 0:n], ALU.mult, [R_ps[bb], R_sg], [R_tm])
                    yield
                    dst = mergedT[:, half * 4:(half + 1) * 4, col0:col0 + n]
                    TT("pool", dst, dst, tm[:, :, 0:n], ALU.add, [R_tm, R_mg[t]], [R_mg[t]])
                    yield

            def interleave(streams):
                gens = [s[0] for s in streams]
                est = [max(1, s[1]) for s in streams]
                prog = [0] * len(gens)
                alive = [True] * len(gens)
                while any(alive):
                    best = None
                    for i in range(len(gens)):
                        if alive[i] and (best is None or prog[i] / est[i] < prog[best] / est[best]):
                            best = i
                    try:
                        next(gens[best])
                        prog[best] += 1
                    except StopIteration:
                        alive[best] = False

            def run_all(g):
                for _ in g:
                    pass

            LS = PAST + TS
            assert 2 * LS + 33 * 2 * 129 <= XBYTES // 2
            Bs = key_bufs(LS, 33, 1, pre_alloc=regX)
            R_kc = Bs["R_k"]
            R_stg = Bs["score_ring"].items[0][1]
            stg = Bs["score_raw"][:, 0:4096]
            stg_i = stg.rearrange("p (t r d) -> p t r d", t=32, r=2)
            P.dma("pool", stg_i[:, :, 0, :], cache_ki.rearrange("(t p) d -> p t d", p=128), writes=[R_stg], sem_res=R_stg)
            P.dma_more("pool", stg_i[:, :, 1, :], cache_ki.rearrange("(t p) d -> p t d", p=128), R_stg)
            for t0 in range(0, 32, 8):
                bt_ = nb("tr")
                pt = r3(psb(bt_), 8)
                for j in range(8):
                    TR(pt[:, j, :], stg_i[:, t0 + j].rearrange("p r d -> p (r d)"), ident_b, [R_stg, R_identb],
                       [R_ps[bt_]])
                CP("act" if (t0 // 8) % 2 == 0 else "dve", Bs["kiT2"][:, t0 * 128:(t0 + 8) * 128], psb(bt_), [R_ps[bt_]],
                   [R_kc[i] for i in range(t0, t0 + 8)])
            stgk2 = A.alloc(8 * 256, BF16); R_stgk = Res("stgk")
            stg_k = r3(stgk2, 8)
            bank_pools["frk"] = [4, 7]
            cnt["frk"] = 0

            def cache_kv_gen():
                for t0 in range(0, 32, 8):
                    cvv = cache_v.rearrange("(t p) (g d) -> p t g d", p=128, g=2)
                    P.dma("pool", Bs["vb"][:, t0:t0 + 8, 0, 0:128], cvv[:, t0:t0 + 8, 0, :],
                          writes=[R_kc[i] for i in range(t0, t0 + 8)], sem_res=R_kc[t0])
                    P.dma_more("pool", Bs["vb"][:, t0:t0 + 8, 1, 0:128], cvv[:, t0:t0 + 8, 1, :], R_kc[t0])
                    for i in range(t0 + 1, t0 + 8):
                        R_kc[i].w = dict(R_kc[t0].w)
                yield
                for t0 in range(0, 32, 8):
                    P.dma("pool", stg_k, cache_k.rearrange("(t p) c -> p t c", p=128)[:, t0:t0 + 8, :], writes=[R_stgk])
                    yield
                    for g in range(2):
                        bt_ = nb("frk")
                        pt = r3(psb(bt_), 8)
                        for j in range(8):
                            TR(pt[:, j, :], stg_k[:, j, g * 128:(g + 1) * 128], ident_b, [R_stgk, R_identb], [R_ps[bt_]])
                        yield
                        CP("act", Bs["kT"][:, g, t0 * 128:(t0 + 8) * 128], psb(bt_), [R_ps[bt_]],
                           [R_kc[i] for i in range(t0, t0 + 8)])
                        yield
            key_tiles_s = [(i * 128, 128) for i in range(32)] + [(PAST, TS)]
            bk_s = {31: BT_PREV, 32: BT_SAME}
            interleave([(front(TS_IDX, Bs, 32, key_tiles_s, bk_s, None,
                               dict(k=k_s, v=v_s, ki=ki_s, kn="k_s", vn="v_s", kin="ki_s"), "fr"), 60),
                        (cache_kv_gen(), 18)])
            run_all(attn(TS_IDX, Bs["ctx"][TS_IDX], Bs, 32, key_tiles_s, bk_s,
                         yattT_s[:, :, :], R_yattT[TS_IDX]))
            print("pass A(sample) arena peak", A.peak)
            A.release(commonA_mark)
            P.barrier()

            Bp = key_bufs(TP, 17, 2)
            key_tiles_p = [(0, 16)] + [(16 + 128 * (j - 1), 128) for j in range(1, 17)]
            NP_ = NT - 1
            bank_pools["fr0"] = [2, 3]
            bank_pools["fr1"] = [7, 4]
            cnt["fr0"] = 0
            cnt["fr1"] = 0

            def bias_kind_p(t):
                if t == 0:
                    return {0: BT_SAME}
                if t == 1:
                    return {0: BT_META, 1: BT_SAME}
                return {t - 1: BT_PREV, t: BT_SAME}

            def front_p(t):
                tname, col0, n = tiles[t]
                adm = (0, 64, col0 + 64, col0 + 128) if t >= 1 else None
                return front(t, Bp, t, key_tiles_p, bias_kind_p(t), adm,
                             dict(k=k_p[col0:col0 + n, :], v=v_p[col0:col0 + n, :], ki=ki_p[col0:col0 + n, :],
                                  kn="k_p", vn="v_p", kin="ki_p"), "fr%d" % (t % 2))

            def front_steps(t):
                L = key_tiles_p[t][0] + key_tiles_p[t][1]
                if L <= TOPK:
                    return 20
                return 20 + 2 * ((L + 511) // 512) + NBIS + 5 + (t + 8) // 8

            def limited(g, k):
                for _ in range(k):
                    try:
                        next(g)
                    except StopIteration:
                        return
                    yield

            def attn_p(t):
                tname, col0, n = tiles[t]
                return attn(t, Bp["ctx"][t], Bp, t, key_tiles_p, bias_kind_p(t), yattT[:, :, col0:col0 + n], R_yattT[t])

            run_all(front_p(0))
            fcur = front_p(1)
            run_all(limited(fcur, front_steps(1) // 2))
            for t in range(NP_):
                nfar_ = (t + 1) - len(bias_kind_p(t))
                streams = [(attn_p(t), 8 * ((nfar_ + 3) // 4 + 1) + 1)]
                fnext = None
                if t + 1 < NP_:
                    streams.append((fcur, max(1, front_steps(t + 1) - front_steps(t + 1) // 2)))
                if t + 2 < NP_:
                    fnext = front_p(t + 2)
                    streams.append((limited(fnext, front_steps(t + 2) // 2), front_steps(t + 2) // 2))
                interleave(streams)
                if t + 1 < NP_:
                    run_all(fcur)
                fcur = fnext
                if t == NP_ - 2:
                    wbrS = r3(wA_raw[:, 0:KC * D], KC)
                    wgS = r3(wA_raw[:, KC * D:2 * KC * D], KC)
                    for j_, (dst_, w_, c0_) in enumerate(((wbrS, w_br_ssd, 0), (wgS, w_in, C_GS))):
                        if j_ == 0:
                            P.dma("pool", dst_, wsrc(w_, c0_, c0_ + D), writes=[R_wA], sem_res=R_wA)
                        else:
                            P.dma_more("pool", dst_, wsrc(w_, c0_, c0_ + D), R_wA)
            print("pass A arena peak", A.peak)
            A.release(afterX_mark)
            P.barrier()
            assert 2 * KC * D <= KC * NWA
            R_wbrS = R_wA
            R_wgS = R_wA
            sgM_ring = Ring(A, 512, 4, F32, "sgM")
            assert A.mark() <= wA_off, (A.mark(), wA_off)
            A.off = wA_end
            hnM = HN(V_N1W)
            wbrA = r3(A.alloc(KC * D, BF16), KC); R_wbrA = Res("wbrA")
            load_w(wbrA, R_wbrA, w_br_att, 0, D)
            wgA = r3(A.alloc(KC * D, BF16), KC); R_wgA = Res("wgA")
            load_w(wgA, R_wgA, w_in, C_GA, C_GA + D)
            tmM_ring = Ring(A, 512, 4, F32, "tmM")
            for par in range(2):
                bank_pools["m_mm%d" % par] = [0, 1, 2] if par == 0 else [4, 5, 6]
                bank_pools["m_tr%d" % par] = [3] if par == 0 else [7]
                cnt["m_mm%d" % par] = 0
                cnt["m_tr%d" % par] = 0

            def gateM(t, par):
                mmp = "m_mm%d" % par
                tname, col0, n = tiles[t]
                hn, R_hn = yield from hnM.tile_gen(t, pool="m_tr%d" % par)
                yield
                mg = mergedT[:, :, col0:col0 + n]

                def mm32(bank, w, R_w, src, R_src, half):
                    p4 = r3(psf(bank), 4)
                    for cc in range(4):
                        c = half * 4 + cc
                        for kc in range(KC):
                            MM(p4[:, cc, 0:n], w[:, kc, c * 128:(c + 1) * 128], src[:, kc, 0:n], kc == 0, kc == KC - 1,
                               [R_w, R_src], [R_ps[bank]])
                    return p4
                ba = nb(mmp); pa = mm32(ba, wbrS, R_wbrS, mg, R_mg[t], 0)
                bb = nb(mmp); pb_ = mm32(bb, wbrS, R_wbrS, mg, R_mg[t], 1)
                yield
                for half, (bbr, pbr) in enumerate(((ba, pa), (bb, pb_))):
                    bg = nb(mmp); pg = mm32(bg, wgS, R_wgS, hn, R_hn, half)
                    yield
                    sg2, R_sg = sgM_ring.next(); sg = r3(sg2, 4)
                    ACT(sg[:, :, 0:n], pg[:, :, 0:n], AF.Sigmoid, [R_ps[bg]], [R_sg])
                    yield
                    TT("dve", mergedT[:, half * 4:(half + 1) * 4, col0:col0 + n], pbr[:, :, 0:n], sg[:, :, 0:n], ALU.mult,
                       [R_ps[bbr], R_sg], [R_mg[t]])
                    yield
                yT = yattT_s if t == TS_IDX else yattT[:, :, col0:col0 + n]
                for half in range(2):
                    bbr = nb(mmp); pbr = mm32(bbr, wbrA, R_wbrA, yT, R_yattT[t], half)
                    bg = nb(mmp); pg = mm32(bg, wgA, R_wgA, hn, R_hn, half)
                    yield
                    sg2, R_sg = sgM_ring.next(); sg = r3(sg2, 4)
                    ACT(sg[:, :, 0:n], pg[:, :, 0:n], AF.Sigmoid, [R_ps[bg]], [R_sg])
                    yield
                    tm2, R_tm = tmM_ring.next(); tm = r3(tm2, 4)
                    TT("dve", tm[:, :, 0:n], pbr[:, :, 0:n], sg[:, :, 0:n], ALU.mult, [R_ps[bbr], R_sg], [R_tm])
                    yield
                    dst = mergedT[:, half * 4:(half + 1) * 4, col0:col0 + n]
                    TT("pool", dst, dst, tm[:, :, 0:n], ALU.add, [R_tm, R_mg[t]], [R_mg[t]])
                    yield

            def run_two_m(gens, lag):
                gens = list(gens)
                active = []
                nxt = 0
                while nxt < len(gens) or active:
                    if not active or (len(active) == 1 and nxt < len(gens) and active[0][1] >= lag):
                        if nxt < len(gens):
                            active.append([gens[nxt], 0])
                            nxt += 1
                    for a_ in list(active):
                        try:
                            next(a_[0])
                            a_[1] += 1
                        except StopIteration:
                            active.remove(a_)
            run_two_m([gateM(t, t % 2) for t in range(NT)], 5)
            hnM.pool = "tr"
            print("pass M arena peak", A.peak)
            A.release(persist_mark)
            P.barrier()

        if stage >= 4:
            h_all = r3(A.alloc(NT * D, F32), NT); R_h = [Res("h%d" % t) for t in range(NT)]
            hn2T = r3(A.alloc(KC * NCOL, BF16), KC); R_hn2 = [Res("hn2_%d" % t) for t in range(NT)]
            markO = A.mark()
            hnO = HN(V_N2W, nxt=2, nhn=0)
            wo = r3(A.alloc(KC * D, BF16), KC); R_wo = Res("wo")
            load_w(wo, R_wo, w_out, 0, D)
            for par in range(2):
                bank_pools["o_mm%d" % par] = [0, 1] if par == 0 else [4, 5]
                bank_pools["o_tr%d" % par] = [2] if par == 0 else [6]
                cnt["o_mm%d" % par] = 0
                cnt["o_tr%d" % par] = 0

            def passO_gen(t, par):
                tname, col0, n = tiles[t]
                xt, R_xt = hnO.xt.next()
                P.dma("sp", xt[0:n, :], xin[col0:col0 + n, :], writes=[R_xt])
                banks = [nb("o_mm%d" % par), nb("o_mm%d" % par)]
                for q in range(2):
                    b = banks[q]
                    for kc in range(KC):
                        MM(psf(b)[0:n, :], mergedT[:, kc, col0:col0 + n], wo[:, kc, q * 512:(q + 1) * 512], kc == 0,
                           kc == KC - 1, [R_mg[t], R_wo], [R_ps[b]])
                yield
                for q in range(2):
                    b = banks[q]
                    TT("dve", h_all[0:n, t, q * 512:(q + 1) * 512], psf(b)[0:n, :], xt[0:n, q * 512:(q + 1) * 512],
                       ALU.add, [R_ps[b], R_xt], [R_h[t]])
                yield
                yield from hnO.tile_gen(t, src=h_all[:, t, :], R_src=R_h[t], dst=hn2T[:, :, col0:col0 + n],
                                        R_dst=R_hn2[t], pool="o_tr%d" % par)
                yield
            run_two_m([passO_gen(t, t % 2) for t in range(NT)], 4)
            print("pass O arena peak", A.peak)
            A.release(markO)
            A.nbytes = ARENA_BYTES
            P.barrier()

            slices = [(0, 6), (6, 12), (12, 17), (17, 22)]
            NFMAX = 6
            wgt_ring = Ring(A, KC * 256, 2, BF16, "wgt")
            wup_ring = Ring(A, KC * 256, 2, BF16, "wup")
            actT = r3(A.alloc(NFMAX * NCOL, BF16), NFMAX); R_act = [Res("act%d" % c) for c in range(NFMAX)]
            wd_ring = Ring(A, NFMAX * D, 1, BF16, "wd")
            s_ring = Ring(A, 512, 2, F32, "silu")
            blocks = [(0, 512), (512, 1024), (1024, 1536), (1536, 2048), (2048, NCOL)]

            def tiles_in(c0, c1):
                return [t for t, (_, col0, n) in enumerate(tiles) if col0 < c1 and col0 + n > c0]
            for (f0, f1) in slices:
                nf = f1 - f0
                for fp in range(f0, f1, 2):
                    npair = min(2, f1 - fp)
                    wgt2, R_wgt = wgt_ring.next()
                    wup2, R_wup = wup_ring.next()
                    wgt = r3(wgt2, KC)
                    wup = r3(wup2, KC)
                    P.dma("pool", wgt[:, :, 0:npair * 128], wsrc(w_gate, fp * 128, (fp + npair) * 128), writes=[R_wgt])
                    P.dma("pool", wup[:, :, 0:npair * 128], wsrc(w_up, fp * 128, (fp + npair) * 128), writes=[R_wup])
                    for ci in range(npair):
                        cl = fp + ci - f0
                        for (c0, c1) in blocks:
                            wdt = c1 - c0
                            rt = [R_hn2[t] for t in tiles_in(c0, c1)]
                            bg = nb("mm")
                            bu = nb("mm")
                            for kc in range(KC):
                                MM(psf(bg)[:, 0:wdt], wgt[:, kc, ci * 128:(ci + 1) * 128], hn2T[:, kc, c0:c1], kc == 0,
                                   kc == KC - 1, [R_wgt] + rt, [R_ps[bg]])
                            for kc in range(KC):
                                MM(psf(bu)[:, 0:wdt], wup[:, kc, ci * 128:(ci + 1) * 128], hn2T[:, kc, c0:c1], kc == 0,
                                   kc == KC - 1, [R_wup] + rt, [R_ps[bu]])
                            sl_, R_sl = s_ring.next()
                            ACT(sl_[:, 0:wdt], psf(bg)[:, 0:wdt], AF.Silu, [R_ps[bg]], [R_sl])
                            TT("dve", actT[:, cl, c0:c1], psf(bu)[:, 0:wdt], sl_[:, 0:wdt], ALU.mult,
                               [R_ps[bu], R_sl], [R_act[cl]])
                wd2, R_wd = wd_ring.next()
                wd = r3(wd2, NFMAX)
                P.dma("pool", wd[:, 0:nf, :], w_down.rearrange("(c p) d -> p c d", p=128)[:, f0:f1, :], writes=[R_wd])
                for t in range(NT):
                    tname, col0, n = tiles[t]
                    for q in range(2):
                        b = nb("o")
                        for cl in range(nf):
                            MM(psf(b)[0:n, :], actT[:, cl, col0:col0 + n], wd[:, cl, q * 512:(q + 1) * 512], cl == 0,
                               cl == nf - 1, [R_act[cl], R_wd], [R_ps[b]])
                        TT("dve", h_all[0:n, t, q * 512:(q + 1) * 512], h_all[0:n, t, q * 512:(q + 1) * 512],
                           psf(b)[0:n, :], ALU.add, [R_ps[b], R_h[t]], [R_h[t]])
            for t in range(1, NT):
                tname, col0, n = tiles[t]
                if t == TS_IDX:
                    P.dma("sp", y_s[:, :], h_all[0:n, t, :], reads=[R_h[t]], writes=[out_res["y_s"]], sem_res=R_h[t])
                else:
                    P.dma("sp", y_p[col0 - 16:col0 - 16 + n, :], h_all[0:n, t, :], reads=[R_h[t]],
                          writes=[out_res["y_p"]], sem_res=R_h[t])
            print("pass F arena peak", A.peak)

        P.final = [out_res[k] for k in out_names]
        P.emit(st)
    return nc


def _static_consts():
    c = {}
    c["ident"] = np.eye(128, dtype=np.float32)
    k = np.arange(128)[:, None]
    i = np.arange(128)[None, :]
    U = (k <= i).astype(np.float32)
    SL = (k > i).astype(np.float32)
    ones = np.ones((128, 128), np.float32)
    c["tri"] = np.ascontiguousarray(np.concatenate([U, SL, ones], axis=1))
    c["nm"] = np.where(i < k, np.float32(NEG), np.float32(0.0)).astype(np.float32)
    c["pow2"] = (2.0 ** -np.arange(32, dtype=np.float64)).astype(np.float32)[None, :]
    return c


def _bias_tables(rel_bias):
    ss = np.arange(128)[:, None]
    ii = np.arange(128)[None, :]
    tabs = []
    for off in (-128, 0):
        tabs.append(rel_bias[t5_bucket_np(ss + off - ii)])
    bt = np.stack(tabs, axis=1)
    bt = bt.transpose(0, 1, 3, 2)
    return np.ascontiguousarray(bt.reshape(128, -1)).astype(np.float32)


def make_in_maps(inputs, cores):
    g = lambda k: np.asarray(inputs[k], dtype=np.float32)
    x_prompt, x_sample = g("x_prompt"), g("x_sample")
    meta = g("meta_tokens")
    consts = _static_consts()
    rel_bias = g("rel_bias")
    bt = _bias_tables(rel_bias)
    vecs = np.zeros((128, 64), np.float32)
    vecs[:, 0:8] = g("norm1_w")[0].reshape(8, 128).T
    vecs[:, 8:16] = g("norm2_w")[0].reshape(8, 128).T
    vecs[:, 16:24] = g("ssd_norm_w")[0].reshape(8, 128).T
    vecs[:, 24] = g("q_norm_w")[0]
    rows = np.zeros((1, 512), np.float32)
    rows[0, 0:128] = g("k_norm_w")[0]
    rows[0, 128:192] = g("idx_k_norm_w")[0]
    rows[0, 192:208] = g("dt_bias")[0]
    rows[0, 208:224] = g("a_log")[0]
    rows[0, 224:240] = g("d_skip")[0]
    rows[0, 240:248] = rel_bias[15]
    convw = np.ascontiguousarray(g("conv_w")[0].reshape(4, 16, 128).transpose(2, 0, 1).reshape(128, 64))
    convb = np.ascontiguousarray(g("conv_b")[0].reshape(16, 128).T)
    shared = dict(
        w_in=g("w_in")[0], w_br_ssd=g("w_br_ssd")[0], w_br_att=g("w_br_att")[0], w_out=g("w_out")[0],
        w_gate=g("w_gate")[0], w_up=g("w_up")[0], w_down=g("w_down")[0],
        vecs=vecs, rows=rows, convw=convw, convb=convb, ident=consts["ident"], tri=consts["tri"], nm=consts["nm"],
        bt=bt, pow2=consts["pow2"])
    in_maps = []
    for b in cores:
        m = dict(shared)
        m["xin"] = np.ascontiguousarray(np.concatenate([meta, x_prompt[b], x_sample[b]], axis=0))
        m["cache_k"] = np.ascontiguousarray(g("cache_k")[0, b].reshape(PAST, 256))
        m["cache_v"] = np.ascontiguousarray(g("cache_v")[0, b].reshape(PAST, 256))
        m["cache_ki"] = np.ascontiguousarray(g("cache_kidx")[0, b])
        m["state_ssm"] = np.ascontiguousarray(g("state_ssm")[0, b].reshape(1024, 128))
        m["state_conv"] = np.ascontiguousarray(g("state_conv")[0, b])
        in_maps.append(m)
    return in_maps


_NC_CACHE = {}


def kernel(**inputs):
    if "nc" not in _NC_CACHE:
        _NC_CACHE["nc"] = build_program()
    nc = _NC_CACHE["nc"]
    cores = list(range(8))
    in_maps = make_in_maps(inputs, cores)
    res = run_bass_kernel_spmd(nc, in_maps, core_ids=cores)
    r = res.results
    st = lambda name: np.stack([np.asarray(r[b][name], dtype=np.float32) for b in cores], axis=0)
    y_prompt = st("y_p")
    y_sample = st("y_s")
    k_prompt = st("k_p").reshape(1, 8, TP, 2, 128)
    v_prompt = st("v_p").reshape(1, 8, TP, 2, 128)
    kidx_prompt = st("ki_p").reshape(1, 8, TP, 64)
    ssm_prompt = st("ssm_p").reshape(1, 8, 16, 64, 128)
    conv_prompt = st("conv_p").reshape(1, 8, 3, 2048)
    k_sample = st("k_s").reshape(1, 8, TS, 2, 128)
    v_sample = st("v_s").reshape(1, 8, TS, 2, 128)
    kidx_sample = st("ki_s").reshape(1, 8, TS, 64)
    ssm_sample = st("ssm_s").reshape(1, 8, 16, 64, 128)
    conv_sample = st("conv_s").reshape(1, 8, 3, 2048)
    return (y_prompt, y_sample, k_prompt, v_prompt, kidx_prompt, ssm_prompt, conv_prompt,
            k_sample, v_sample, kidx_sample, ssm_sample, conv_sample)
```

```python
import math
from contextlib import ExitStack

import numpy as np
import concourse.bass as bass
import concourse.mybir as mybir
from concourse.bass_utils import run_bass_kernel_spmd

F32 = mybir.dt.float32
BF16 = mybir.dt.bfloat16
ALU = mybir.AluOpType
AF = mybir.ActivationFunctionType
AX = mybir.AxisListType

ENGS = ("pe", "act", "dve", "pool", "sp")

D = 1024
SEQ = 2048
NMETA = 16
TP = NMETA + SEQ
TS = 64
PAST = 4096
NCOL = TP + TS
KC = 8
IN_DIM = 7256
C_Z, C_XBC, C_DT, C_Q, C_K, C_V, C_QI, C_KI, C_WI, C_GS, C_GA = (
    0, 1024, 3072, 3088, 4112, 4368, 4624, 5136, 5200, 5208, 6232)
DFF = 2816
NFF = 22
EPS = 1e-6
TOPK = 256
NBIS = 18
NEG = -30000.0
WI_SCALE = (8 ** -0.5) * (64 ** -0.5)


class Res:
    __slots__ = ("name", "w", "r", "dsem", "dcnt", "excl")

    def __init__(self, name, excl=False):
        self.name = name
        self.w = {}
        self.r = {}
        self.dsem = None
        self.dcnt = 0
        self.excl = excl


class Ins:
    __slots__ = ("eng", "fn", "waits", "signal", "ticket", "dma")

    def __init__(self, eng, fn):
        self.eng = eng
        self.fn = fn
        self.waits = []
        self.signal = False
        self.ticket = None
        self.dma = None


class Prog:
    def __init__(self, nc):
        self.nc = nc
        self.q = {e: [] for e in ENGS}
        self.dma_res = []
        self.last = {e: None for e in ENGS}
        self.pending = {e: [] for e in ENGS}
        self.final = []

    def _add(self, ins, waits):
        eng = ins.eng
        if self.pending[eng]:
            waits = waits + self.pending[eng]
            self.pending[eng] = []
        ins.waits = waits
        for ev in waits:
            if ev[0] == 'c':
                ev[1].signal = True
        self.q[eng].append(ins)

    def op(self, eng, fn, reads=(), writes=()):
        ins = Ins(eng, fn)
        waits = []
        for R in reads:
            for ev in R.w.values():
                if ev[0] == 'c' and ev[1].eng == eng and eng == "pe":
                    continue
                waits.append(ev)
            if R.excl:
                for ev in R.r.values():
                    if ev[0] == 'c' and ev[1].eng == eng:
                        continue
                    waits.append(ev)
        for R in writes:
            for ev in R.w.values():
                if ev[0] == 'c' and ev[1].eng == eng:
                    continue
                waits.append(ev)
            for ev in R.r.values():
                if ev[0] == 'c' and ev[1].eng == eng:
                    continue
                waits.append(ev)
        self._add(ins, waits)
        me = ('c', ins)
        for R in reads:
            R.r[eng] = me
        for R in writes:
            R.w = {eng: me}
            R.r = {}
        self.last[eng] = ins
        return ins

    def dma(self, eng, out, in_, reads=(), writes=(), sem_res=None, **kw):
        if sem_res is None:
            sem_res = writes[0] if writes else reads[0]
        if sem_res.dsem is None:
            sem_res.dsem = True
            self.dma_res.append(sem_res)

        def fn(e, out=out, in_=in_, kw=kw):
            return e.dma_start(out=out, in_=in_, **kw)
        ins = Ins(eng, fn)
        waits = []
        for R in reads:
            waits.extend(R.w.values())
        for R in writes:
            waits.extend(R.w.values())
            waits.extend(R.r.values())
        self._add(ins, waits)
        sem_res.dcnt += 16
        ins.dma = (sem_res, sem_res.dcnt)
        me = ('d', sem_res, sem_res.dcnt)
        key = ('d', id(sem_res))
        for R in reads:
            R.r[key] = me
        for R in writes:
            R.w = {key: me}
            R.r = {}
        return ins

    def dma_more(self, eng, out, in_, R, **kw):
        def fn(e, out=out, in_=in_, kw=kw):
            return e.dma_start(out=out, in_=in_, **kw)
        ins = Ins(eng, fn)
        self._add(ins, [])
        R.dcnt += 16
        ins.dma = (R, R.dcnt)
        R.w = {('d', id(R)): ('d', R, R.dcnt)}
        return ins

    def barrier(self):
        evs = []
        for e in ENGS:
            if self.last[e] is not None:
                evs.append(('c', self.last[e]))
        for R in self.dma_res:
            evs.append(('d', R, R.dcnt))
        for e in ENGS:
            self.pending[e] = self.pending[e] + [
                ev for ev in evs if not (ev[0] == 'c' and ev[1].eng == e)]

    def emit(self, stack):
        nc = self.nc
        esem = {e: stack.enter_context(nc.semaphore("s_" + e)) for e in ENGS}
        for i, R in enumerate(self.dma_res):
            R.dsem = stack.enter_context(nc.semaphore("d%d" % i))
        for e in ENGS:
            t = 0
            for ins in self.q[e]:
                if ins.signal:
                    t += 1
                    ins.ticket = t
        block = stack.enter_context(nc.Block())
        final = self.final

        def run(e, eo):
            seen = {}

            def wait(ev):
                if ev[0] == 'c':
                    sem, val, key = esem[ev[1].eng], ev[1].ticket, ev[1].eng
                else:
                    sem, val, key = ev[1].dsem, ev[2], id(ev[1])
                if seen.get(key, 0) >= val:
                    return
                seen[key] = val
                eo.wait_ge(sem, val)
            for ins in self.q[e]:
                for ev in ins.waits:
                    wait(ev)
                bi = ins.fn(eo)
                if ins.dma is not None:
                    bi.then_inc(ins.dma[0].dsem, 16)
                elif ins.signal:
                    bi.then_inc(esem[e], 1)
            if e == "sp":
                for R in final:
                    for ev in R.w.values():
                        wait(ev)

        @block.tensor
        def _(eo):
            run("pe", eo)

        @block.scalar
        def _(eo):
            run("act", eo)

        @block.vector
        def _(eo):
            run("dve", eo)

        @block.gpsimd
        def _(eo):
            run("pool", eo)

        @block.sync
        def _(eo):
            run("sp", eo)


class Arena:
    def __init__(self, base, nbytes):
        self.base = base
        self.nbytes = nbytes
        self.off = 0
        self.peak = 0
        self.peak_since_release = 0

    def alloc(self, n, dt):
        esz = 4 if dt == F32 else 2
        nb = (n * esz + 63) // 64 * 64
        assert self.off + nb <= self.nbytes, ("arena overflow", self.off, nb, self.nbytes)
        a = self.base[:, self.off // 2:(self.off + nb) // 2]
        self.off += nb
        self.peak = max(self.peak, self.off)
        self.peak_since_release = max(self.peak_since_release, self.off)
        if dt == F32:
            a = a.bitcast(F32)
        return a[:, 0:n]

    def mark(self):
        return self.off

    def release(self, m):
        self.off = m
        self.peak_since_release = m


class Ring:
    def __init__(self, arena, n, cnt, dt, name):
        self.bufs = [(arena.alloc(n, dt), Res("%s%d" % (name, i))) for i in range(cnt)]
        self.i = 0

    def next(self):
        b = self.bufs[self.i % len(self.bufs)]
        self.i += 1
        return b


def r3(ap, a):
    return ap.rearrange("p (a b) -> p a b", a=a)


def t5_bucket_np(rel):
    nb = 16
    max_exact = 8
    ret = np.where(rel > 0, nb, 0)
    n = np.abs(rel)
    nf = np.maximum(n, 1).astype(np.float32)
    large = max_exact + (np.log(nf / np.float32(max_exact)) / np.float32(math.log(128 / max_exact))
                         * np.float32(nb - max_exact)).astype(np.int32)
    large = np.minimum(large, nb - 1)
    return ret + np.where(n < max_exact, n, large)


def build_program(stage=99):
    nc = bass.Bass("TRN2", target_bir_lowering=False)
    dram = {}

    def IN(name, shape):
        dram[name] = nc.dram_tensor(name, list(shape), F32, kind="ExternalInput").ap()
        return dram[name]

    def OUT(name, shape):
        dram[name] = nc.dram_tensor(name, list(shape), F32, kind="ExternalOutput").ap()
        return dram[name]

    xin = IN("xin", [NCOL, D])
    cache_k = IN("cache_k", [PAST, 256])
    cache_v = IN("cache_v", [PAST, 256])
    cache_ki = IN("cache_ki", [PAST, 64])
    state_ssm = IN("state_ssm", [1024, 128])
    state_conv = IN("state_conv", [3, 2048])
    w_in = IN("w_in", [D, IN_DIM])
    w_br_ssd = IN("w_br_ssd", [D, D])
    w_br_att = IN("w_br_att", [D, D])
    w_out = IN("w_out", [D, D])
    w_gate = IN("w_gate", [D, DFF])
    w_up = IN("w_up", [D, DFF])
    w_down = IN("w_down", [DFF, D])
    vecs = IN("vecs", [128, 64])
    rows = IN("rows", [1, 512])
    convw = IN("convw", [128, 64])
    convb = IN("convb", [128, 16])
    ident_in = IN("ident", [128, 128])
    tri_in = IN("tri", [128, 3 * 128])
    nm_in = IN("nm", [128, 128])
    bt_in = IN("bt", [128, 2 * 8 * 128])
    pow2_in = IN("pow2", [1, 32])

    y_p = OUT("y_p", [SEQ, D])
    y_s = OUT("y_s", [TS, D])
    k_p = OUT("k_p", [TP, 256])
    v_p = OUT("v_p", [TP, 256])
    ki_p = OUT("ki_p", [TP, 64])
    ssm_p = OUT("ssm_p", [1024, 128])
    conv_p = OUT("conv_p", [3, 2048])
    k_s = OUT("k_s", [TS, 256])
    v_s = OUT("v_s", [TS, 256])
    ki_s = OUT("ki_s", [TS, 64])
    ssm_s = OUT("ssm_s", [1024, 128])
    conv_s = OUT("conv_s", [3, 2048])
    out_names = ("y_p", "y_s", "k_p", "v_p", "ki_p", "ssm_p", "conv_p", "k_s", "v_s", "ki_s", "ssm_s", "conv_s")
    out_res = {n: Res("o_" + n) for n in out_names}

    tiles = [("M", 0, 16)] + [("T%d" % j, 16 + 128 * (j - 1), 128) for j in range(1, 17)] + [("S", TP, TS)]
    NT = len(tiles)
    TS_IDX = NT - 1

    st = ExitStack()
    with st:
        P = Prog(nc)
        ARENA_BYTES = 212800
        arena_t = st.enter_context(nc.sbuf_tensor("arena", [128, ARENA_BYTES // 2], BF16))
        A = Arena(arena_t, ARENA_BYTES)
        psum_t = st.enter_context(nc.psum_tensor("psum", [128, 4096], F32))
        R_ps = [Res("ps%d" % b, excl=True) for b in range(8)]

        def psf(b, w=512, o=0):
            return psum_t[:, b * 512 + o: b * 512 + o + w]

        def psb(b, w=1024, o=0):
            return psum_t[:, b * 512:(b + 1) * 512].bitcast(BF16)[:, o:o + w]

        def MM(out, lhsT, rhs, start, stop, reads, writes):
            P.op("pe", lambda e: e.matmul(out, lhsT=lhsT, rhs=rhs, start=start, stop=stop), reads, writes)

        def TR(out, in_, ident, reads, writes):
            P.op("pe", lambda e: e.transpose(out, in_, ident), reads, writes)

        def ACT(out, in_, func, reads, writes, bias=None, scale=None, accum_out=None):
            kw = {}
            if bias is not None:
                kw["bias"] = bias
            if scale is not None:
                kw["scale"] = scale
            if accum_out is not None:
                kw["accum_out"] = accum_out
            P.op("act", lambda e: e.activation(out=out, in_=in_, func=func, **kw), reads, writes)

        def TS_(eng, out, in0, s1, s2, op0, op1, reads, writes, accum_out=None):
            if op1 is None:
                P.op(eng, lambda e: e.tensor_scalar(out=out, in0=in0, scalar1=s1, scalar2=None, op0=op0), reads, writes)
            elif accum_out is None:
                P.op(eng, lambda e: e.tensor_scalar(out=out, in0=in0, scalar1=s1, scalar2=s2, op0=op0, op1=op1),
                     reads, writes)
            else:
                P.op(eng, lambda e: e.tensor_scalar(out=out, in0=in0, scalar1=s1, scalar2=s2, op0=op0, op1=op1,
                                                    accum_out=accum_out), reads, writes)

        def TT(eng, out, in0, in1, op, reads, writes):
            P.op(eng, lambda e: e.tensor_tensor(out=out, in0=in0, in1=in1, op=op), reads, writes)

        def STT(eng, out, in0, scalar, in1, op0, op1, reads, writes):
            P.op(eng, lambda e: e.scalar_tensor_tensor(out=out, in0=in0, scalar=scalar, in1=in1, op0=op0, op1=op1),
                 reads, writes)

        def CP(eng, out, in_, reads, writes):
            if eng == "act":
                P.op("act", lambda e: e.activation(out=out, in_=in_, func=AF.Copy), reads, writes)
            else:
                P.op(eng, lambda e: e.tensor_copy(out=out, in_=in_), reads, writes)

        def MEMSET(eng, ap, val, writes):
            P.op(eng, lambda e: e.memset(ap, val), (), writes)

        def RECIP(out, in_, reads, writes):
            P.op("dve", lambda e: e.reciprocal(out=out, in_=in_), reads, writes)

        def wsrc(w, c0, c1):
            return w.rearrange("(kc p) c -> p kc c", p=128)[:, :, c0:c1]

        def load_w(dst3, R, w, c0, c1, step=1024):
            first = True
            for a in range(c0, c1, step):
                b = min(c1, a + step)
                if first:
                    P.dma("pool", dst3[:, :, a - c0:b - c0], wsrc(w, a, b), writes=[R], sem_res=R)
                    first = False
                else:
                    P.dma_more("pool", dst3[:, :, a - c0:b - c0], wsrc(w, a, b), R)

        ident_f = A.alloc(128, F32); R_identf = Res("identf")
        ident_b = A.alloc(128, BF16); R_identb = Res("identb")
        vec = A.alloc(64, F32); R_vec = Res("vec")
        rowb = A.alloc(512, F32); R_rowb = Res("rowb")
        mhalf = A.alloc(8, F32); R_mhalf = Res("mhalf")
        P.dma("sp", ident_f, ident_in, writes=[R_identf])
        P.dma("pool", ident_b, ident_in, writes=[R_identb])
        P.dma("sp", vec, vecs, writes=[R_vec])
        P.dma("sp", rowb, rows.partition_broadcast(128), writes=[R_rowb])
        MEMSET("pool", mhalf, -0.5, [R_mhalf])
        V_N1W, V_N2W, V_SNW, V_QNW = 0, 8, 16, 24
        RB_KNW, RB_KINW, RB_DTB, RB_ALOG, RB_DSK, RB_CB = 0, 128, 192, 208, 224, 240

        MG_BYTES = KC * NCOL * 2
        A.nbytes = ARENA_BYTES - MG_BYTES
        mergedT = r3(arena_t[:, (ARENA_BYTES - MG_BYTES) // 2: ARENA_BYTES // 2], KC)
        R_mg = [Res("mg%d" % t) for t in range(NT)]
        junk_act = A.alloc(D, BF16); R_junk_act = Res("junk_act")
        qnw_s = A.alloc(1, F32); R_qnws = Res("qnws")
        TS_("dve", qnw_s, vec[:, V_QNW:V_QNW + 1], 128.0 ** -0.5, None, ALU.mult, None, [R_vec], [R_qnws])
        yattT_s = r3(A.alloc(KC * TS, BF16), KC)
        persist_mark = A.mark()

        mm_banks = [0, 1, 2, 7]
        tr_banks = [3, 4]
        o_banks = [5, 6]
        cnt = {"mm": 0, "tr": 0, "o": 0, "at": 0, "fr": 0}
        bank_pools = {"mm": mm_banks, "tr": tr_banks, "o": o_banks, "at": [0, 1]}

        def nb(kind):
            pool = bank_pools[kind]
            b = pool[cnt[kind] % len(pool)]
            cnt[kind] += 1
            return b

        def rsqrt_small(out, in_, scale, n, w, reads, writes):
            TS_("pool", out, in_, scale, EPS, ALU.mult, ALU.add, reads, writes)
            TT("pool", out, out, mhalf[0:n, 0:w], ALU.pow, list(writes) + [R_mhalf], writes)

        class HN:
            def __init__(self, vcol, nxt=2, nhn=2):
                self.xt = Ring(A, D, nxt, F32, "xt") if nxt else None
                self.xn = Ring(A, D, 1, BF16, "xn")
                self.hn = Ring(A, KC * 128, nhn, BF16, "hn") if nhn else None
                self.sm = Ring(A, 4, 2, F32, "smh")
                self.vcol = vcol
                self.pool = "tr"

            def tile_gen(self, t, src=None, R_src=None, dst=None, R_dst=None, pool=None):
                pool = self.pool if pool is None else pool
                tname, col0, n = tiles[t]
                if src is None:
                    xt, R_xt = self.xt.next()
                    P.dma("sp", xt[0:n, :], xin[col0:col0 + n, :], writes=[R_xt])
                else:
                    xt, R_xt = src, R_src
                sm, R_sm = self.sm.next()
                if dst is None:
                    hn2, R_hn = self.hn.next()
                    hn = r3(hn2, KC)
                else:
                    hn, R_hn = dst, R_dst
                ACT(junk_act[0:n, :], xt[0:n, :], AF.Square, [R_xt], [R_junk_act, R_sm], accum_out=sm[0:n, 0:1])
                yield
                TS_("pool", sm[0:n, 1:2], sm[0:n, 0:1], 1.0 / D, EPS, ALU.mult, ALU.add, [R_sm], [R_sm])
                yield
                TT("pool", sm[0:n, 1:2], sm[0:n, 1:2], mhalf[0:n, 0:1], ALU.pow, [R_sm, R_mhalf], [R_sm])
                yield
                xn, R_xn = self.xn.next()
                if getattr(self, "mul_on_act", False):
                    ACT(xn[0:n, :], xt[0:n, :], AF.Identity, [R_xt, R_sm], [R_xn], scale=sm[0:n, 1:2])
                else:
                    TS_("dve", xn[0:n, :], xt[0:n, :], sm[0:n, 1:2], None, ALU.mult, None, [R_xt, R_sm], [R_xn])
                b = nb(pool)
                pt = r3(psb(b), 8)
                for kc in range(KC):
                    TR(pt[:, kc, 0:n], xn[0:n, kc * 128:(kc + 1) * 128], ident_b[0:n, 0:n], [R_xn, R_identb], [R_ps[b]])
                yield
                TT("dve", hn[:, :, 0:n], pt[:, :, 0:n],
                   vec[:, self.vcol:self.vcol + 8].unsqueeze(2).to_broadcast([128, 8, n]), ALU.mult,
                   [R_ps[b], R_vec], [R_hn])
                return hn, R_hn

            def tile(self, t, src=None, R_src=None, dst=None, R_dst=None):
                g = self.tile_gen(t, src, R_src, dst, R_dst)
                try:
                    while True:
                        next(g)
                except StopIteration as e:
                    return e.value

        def gate_branch(t, hn, R_hn, yT, R_yT, wbr, R_wbr, wg, R_wg, first, sg_ring, tmpm_ring):
            tname, col0, n = tiles[t]
            bbs = [nb("mm"), nb("mm")]
            bgs = [nb("mm"), nb("mm")]
            for half in range(2):
                pbr = r3(psf(bbs[half]), 4)
                for cc in range(4):
                    c = half * 4 + cc
                    for kc in range(KC):
                        MM(pbr[:, cc, 0:n], wbr[:, kc, c * 128:(c + 1) * 128], yT[:, kc, 0:n], kc == 0, kc == KC - 1,
                           [R_wbr, R_yT], [R_ps[bbs[half]]])
            for half in range(2):
                pg = r3(psf(bgs[half]), 4)
                for cc in range(4):
                    c = half * 4 + cc
                    for kc in range(KC):
                        MM(pg[:, cc, 0:n], wg[:, kc, c * 128:(c + 1) * 128], hn[:, kc, 0:n], kc == 0, kc == KC - 1,
                           [R_wg, R_hn], [R_ps[bgs[half]]])
            for half in range(2):
                pbr = r3(psf(bbs[half]), 4)
                pg = r3(psf(bgs[half]), 4)
                sg2, R_sg = sg_ring.next()
                sg = r3(sg2, 4)
                ACT(sg[:, :, 0:n], pg[:, :, 0:n], AF.Sigmoid, [R_ps[bgs[half]]], [R_sg])
                dst = mergedT[:, half * 4:(half + 1) * 4, col0:col0 + n]
                if first:
                    TT("dve", dst, pbr[:, :, 0:n], sg[:, :, 0:n], ALU.mult, [R_ps[bbs[half]], R_sg], [R_mg[t]])
                else:
                    tm2, R_tm = tmpm_ring.next()
                    tm = r3(tm2, 4)
                    TT("dve", tm[:, :, 0:n], pbr[:, :, 0:n], sg[:, :, 0:n], ALU.mult, [R_ps[bbs[half]], R_sg], [R_tm])
                    TT("pool", dst, dst, tm[:, :, 0:n], ALU.add, [R_tm, R_mg[t]], [R_mg[t]])

        if stage >= 2:
            hnS = HN(V_N1W)
            wS = r3(A.alloc(KC * 3088, BF16), KC); R_wS = Res("wS")
            R_wSz = Res("wSz")
            load_w(wS[:, :, C_XBC:3088], R_wS, w_in, C_XBC, 3088)
            load_w(wS[:, :, 0:C_XBC], R_wSz, w_in, 0, C_XBC)
            cw = A.alloc(64, F32); R_cw = Res("cw")
            P.dma("sp", cw, convw, writes=[R_cw])
            cb = A.alloc(16, F32); R_cb = Res("cb")
            P.dma("sp", cb, convb, writes=[R_cb])
            tri = r3(A.alloc(3 * 128, F32), 3); R_tri = Res("tri")
            P.dma("sp", tri, tri_in.rearrange("p (a b) -> p a b", a=3), writes=[R_tri])
            U_f, SL_f, ONES_f = tri[:, 0, :], tri[:, 1, :], tri[:, 2, :]
            NM = r3(A.alloc(4 * 128, BF16), 4); R_NM = Res("NM")
            for h in range(4):
                if h == 0:
                    P.dma("pool", NM[:, h, :], nm_in, writes=[R_NM], sem_res=R_NM)
                else:
                    P.dma_more("pool", NM[:, h, :], nm_in, R_NM)
            dconv = r3(A.alloc(64 * 128, BF16), 64); R_dconv = Res("dconv"); R_dconv2 = Res("dconv2")
            for i in range(64):
                if i < 32:
                    TS_("dve", dconv[:, i, :], ident_f, cw[:, i:i + 1], None, ALU.mult, None, [R_identf, R_cw], [R_dconv])
                else:
                    ACT(dconv[:, i, :], ident_f, AF.Copy, [R_identf, R_cw], [R_dconv2], scale=cw[:, i:i + 1])
            aneg = A.alloc(16, F32); R_aneg = Res("aneg")
            ACT(aneg, rowb[:, RB_ALOG:RB_ALOG + 16], AF.Exp, [R_rowb], [R_aneg])
            TS_("dve", aneg, aneg, -1.0, None, ALU.mult, None, [R_aneg], [R_aneg])
            ST = A.alloc(1024, F32); R_ST = Res("ST")
            Sb = A.alloc(1024, BF16); R_Sb = Res("Sb")
            hist = r3(A.alloc(16 * 3, BF16), 16); R_hist = Res("hist")
            Rm = A.alloc(2048, F32); R_Rm = Res("Rm")
            cvT = A.alloc(48, F32); R_cvT = Res("cvT")
            cvio = A.alloc(128, F32); R_cvio = Res("cvio")
            sso = r3(Rm[:, 0:1024], 8)
            stin = r3(Rm[:, 1024:2048], 8)
            xraw_ring = Ring(A, 16 * 131, 2, BF16, "xraw")
            xact_ring = Ring(A, 16 * 128, 2, BF16, "xact")
            xtok_ring = Ring(A, 1024, 2, BF16, "xtok")
            btok_ring = Ring(A, 512, 2, BF16, "btok")
            xdt_ring = Ring(A, 1024, 2, BF16, "xdt")
            xD_ring = Ring(A, 1024, 2, BF16, "xD")
            xdec_ring = Ring(A, 1024, 2, BF16, "xdec")
            Lt_ring = Ring(A, 2048, 2, BF16, "Lt")
            t1_ring = Ring(A, 1024, 2, F32, "t1")
            yn_ring = Ring(A, 1024, 2, BF16, "yn")
            szh_ring = Ring(A, 512, 2, F32, "szh")
            smS_ring = Ring(A, 160, 2, F32, "smS")
            etb_ring = Ring(A, 16, 2, F32, "etb")
            for par in range(2):
                bank_pools["s_mm%d" % par] = [0, 1] if par == 0 else [4, 5]
                bank_pools["s_tr%d" % par] = [2] if par == 0 else [6]
                bank_pools["s_o%d" % par] = [3] if par == 0 else [7]
                for k_ in ("s_mm", "s_tr", "s_o"):
                    cnt["%s%d" % (k_, par)] = 0
            flags = {"hist": -1, "state": -1}

            def ssd_gen(t, first, last, seq, par, prev_t, S):
                mmp, trp, op_ = "s_mm%d" % par, "s_tr%d" % par, "s_o%d" % par
                tname, col0, n = tiles[t]
                hn, R_hn = yield from hnS.tile_gen(t, pool=trp)
                yield
                sm, R_sm = smS_ring.next()
                xraw2, R_xraw = xraw_ring.next(); xraw = r3(xraw2, 16)
                xact2, R_xact = xact_ring.next(); xact = r3(xact2, 16)
                xtok, R_xtok = xtok_ring.next()
                btok, R_btok = btok_ring.next()
                xdt, R_xdt = xdt_ring.next()
                xD, R_xD = xD_ring.next()
                xdec, R_xdec = xdec_ring.next()
                Lt, R_Lt = Lt_ring.next()
                t1, R_t1 = t1_ring.next()
                yn, R_yn = yn_ring.next()
                etb, R_etb = etb_ring.next()
                if first and seq == "p":
                    MEMSET("pool", xraw[:, :, 0:3], 0.0, [R_xraw])
                elif first and seq == "s":
                    P.dma("sp", S.cvio[0:48, :], state_conv.rearrange("t (c f) -> (t c) f", f=128), writes=[S.R_cvio])
                    bt = nb(op_)
                    TR(psf(bt)[:, 0:48], S.cvio[0:48, :], ident_f[0:48, 0:48], [S.R_cvio, R_identf], [R_ps[bt]])
                    CP("dve", xraw[:, :, 0:3], psf(bt)[:, 0:48].rearrange("p (t c) -> p c t", t=3), [R_ps[bt]], [R_xraw])
                else:
                    while S.flags["hist"] < prev_t:
                        yield
                    CP("pool", xraw[:, :, 0:3], S.hist[:, :, :], [S.R_hist], [R_xraw])
                for hf in range(2):
                    banks = [nb(mmp), nb(mmp)]
                    for cc in range(8):
                        c = hf * 8 + cc
                        b = banks[cc // 4]
                        for kc in range(KC):
                            MM(r3(psf(b), 4)[:, cc % 4, 0:n], wS[:, kc, C_XBC + c * 128: C_XBC + (c + 1) * 128],
                               hn[:, kc, 0:n], kc == 0, kc == KC - 1, [R_wS, R_hn], [R_ps[b]])
                    yield
                    for q2 in range(2):
                        q = hf * 2 + q2
                        CP("act" if q2 == 0 else "dve", xraw[:, q * 4:(q + 1) * 4, 3:3 + n],
                           r3(psf(banks[q2]), 4)[:, :, 0:n], [R_ps[banks[q2]]], [R_xraw])
                        if last:
                            CP("dve", S.cvT.rearrange("p (t c) -> p c t", t=3)[:, q * 4:(q + 1) * 4, :],
                               r3(psf(banks[q2]), 4)[:, :, n - 3:n], [R_ps[banks[q2]]], [S.R_cvT])
                    yield
                CP("pool", S.hist[:, :, :], xraw[:, :, n:n + 3], [R_xraw], [S.R_hist])
                S.flags["hist"] = t
                if last:
                    bt = nb(op_)
                    TR(psf(bt)[0:48, 0:128], S.cvT[:, 0:48], ident_f, [S.R_cvT, R_identf], [R_ps[bt]])
                    CP("act", S.cvio[0:48, :], psf(bt)[0:48, 0:128], [R_ps[bt]], [S.R_cvio])
                    oname = "conv_p" if seq == "p" else "conv_s"
                    P.dma("sp", dram[oname].rearrange("t (c f) -> (t c) f", f=128), S.cvio[0:48, :], reads=[S.R_cvio],
                          writes=[out_res[oname]], sem_res=S.R_cvio)
                yield
                for hf in range(2):
                    banks = [nb(mmp), nb(mmp)]
                    for cc in range(8):
                        c = hf * 8 + cc
                        b = banks[cc // 4]
                        o = r3(psf(b), 4)[:, cc % 4, 0:n]
                        for i in range(4):
                            MM(o, dconv[:, i * 16 + c, :], xraw[:, c, i:i + n], i == 0, i == 3,
                               [R_dconv if i < 2 else R_dconv2, R_xraw], [R_ps[b]])
                    yield
                    for cc in range(8):
                        c = hf * 8 + cc
                        b = banks[cc // 4]
                        o = r3(psf(b), 4)[:, cc % 4, 0:n]
                        ACT(xact[:, c, 0:n], o, AF.Silu, [R_ps[b], R_cb], [R_xact], bias=cb[:, c:c + 1])
                    yield
                bx = nb(trp)
                for c in range(8):
                    TR(psb(bx)[0:n, c * 128:(c + 1) * 128], xact[:, c, 0:n], ident_b, [R_xact, R_identb], [R_ps[bx]])
                yield
                CP("act", xtok[0:n, :], psb(bx)[0:n, :], [R_ps[bx]], [R_xtok])
                yield
                bB = nb(trp)
                for c in range(4):
                    TR(psb(bB)[0:n, c * 128:(c + 1) * 128], xact[:, 8 + c, 0:n], ident_b, [R_xact, R_identb], [R_ps[bB]])
                yield
                CP("dve", btok[0:n, :], psb(bB)[0:n, 0:512], [R_ps[bB]], [R_btok])
                yield
                bd = nb(op_)
                for kc in range(KC):
                    MM(psf(bd)[0:n, 0:16], hn[:, kc, 0:n], wS[:, kc, C_DT:C_DT + 16], kc == 0, kc == KC - 1,
                       [R_hn, R_wS], [R_ps[bd]])
                yield
                TT("dve", sm[0:n, 0:16], psf(bd)[0:n, 0:16], rowb[0:n, RB_DTB:RB_DTB + 16], ALU.add,
                   [R_ps[bd], R_rowb], [R_sm])
                yield
                STT("dve", sm[0:n, 16:32], sm[0:n, 0:16], -1.0, sm[0:n, 0:16], ALU.mult, ALU.max, [R_sm], [R_sm])
                yield
                ACT(sm[0:n, 16:32], sm[0:n, 16:32], AF.Exp, [R_sm], [R_sm], scale=-1.0)
                yield
                ACT(sm[0:n, 16:32], sm[0:n, 16:32], AF.Ln, [R_sm], [R_sm], bias=1.0)
                yield
                STT("dve", sm[0:n, 32:48], sm[0:n, 0:16], 0.0, sm[0:n, 16:32], ALU.max, ALU.add, [R_sm], [R_sm])
                yield
                TT("dve", sm[0:n, 48:64], sm[0:n, 32:48], aneg[0:n, :], ALU.mult, [R_sm, R_aneg], [R_sm])
                yield
                dt_ = sm[0:n, 32:48]
                dtA = sm[0:n, 48:64]
                MM(psf(bd)[0:n, 16:32], U_f[0:n, 0:n], dtA, True, True, [R_tri, R_sm], [R_ps[bd]])
                MM(psf(bd)[:, 32:48], ONES_f[0:n, :], dtA, True, True, [R_tri, R_sm], [R_ps[bd]])
                yield
                CP("dve", sm[0:n, 64:80], psf(bd)[0:n, 16:32], [R_ps[bd]], [R_sm])
                ACT(sm[0:n, 80:96], psf(bd)[0:n, 16:32], AF.Exp, [R_ps[bd]], [R_sm])
                yield
                TT("dve", sm[0:n, 96:112], psf(bd)[0:n, 32:48], sm[0:n, 64:80], ALU.subtract, [R_ps[bd], R_sm], [R_sm])
                ACT(etb, psf(bd)[:, 32:48], AF.Exp, [R_ps[bd]], [R_etb])
                yield
                ACT(sm[0:n, 96:112], sm[0:n, 96:112], AF.Exp, [R_sm], [R_sm])
                yield
                ea = sm[0:n, 80:96]
                dec = sm[0:n, 96:112]
                xtok3 = r3(xtok, 16)
                TT("dve", r3(xdt, 16)[0:n], xtok3[0:n], dt_.unsqueeze(2).to_broadcast([n, 16, 64]), ALU.mult,
                   [R_xtok, R_sm], [R_xdt])
                TT("pool", r3(xD, 16)[0:n], xtok3[0:n], rowb[0:n, RB_DSK:RB_DSK + 16].unsqueeze(2).to_broadcast([n, 16, 64]),
                   ALU.mult, [R_xtok, R_rowb], [R_xD])
                yield
                TT("pool", r3(xdec, 16)[0:n], r3(xdt, 16)[0:n], dec.unsqueeze(2).to_broadcast([n, 16, 64]), ALU.mult,
                   [R_xdt, R_sm], [R_xdec])
                yield
                if not first:
                    while S.flags["state"] < prev_t:
                        yield
                byo = [nb(mmp), nb(mmp)]
                for g in range(4):
                    b = byo[g // 2]
                    MM(psf(b)[0:n, (g % 2) * 256:(g % 2 + 1) * 256], xact[:, 12 + g, 0:n], S.Sb[:, g * 256:(g + 1) * 256],
                       True, True, [R_xact, S.R_Sb], [R_ps[b]])
                yield
                for q in range(2):
                    TT("dve", r3(t1, 16)[0:n, q * 8:(q + 1) * 8, :], r3(psf(byo[q]), 8)[0:n],
                       ea[:, q * 8:(q + 1) * 8].unsqueeze(2).to_broadcast([n, 8, 64]), ALU.mult,
                       [R_ps[byo[q]], R_sm], [R_t1])
                yield
                bs = [nb(mmp), nb(mmp)]
                for g in range(4):
                    b = bs[g // 2]
                    MM(psf(b)[:, (g % 2) * 256:(g % 2 + 1) * 256], btok[0:n, g * 128:(g + 1) * 128],
                       xdec[0:n, g * 256:(g + 1) * 256], True, True, [R_btok, R_xdec], [R_ps[b]])
                TT("pool", r3(S.ST, 16), r3(S.ST, 16), etb.unsqueeze(2).to_broadcast([128, 16, 64]), ALU.mult,
                   [S.R_ST, R_etb], [S.R_ST])
                yield
                for q in range(2):
                    TT("dve", S.ST[:, q * 512:(q + 1) * 512], S.ST[:, q * 512:(q + 1) * 512], psf(bs[q]), ALU.add,
                       [S.R_ST, R_ps[bs[q]]], [S.R_ST])
                yield
                CP("pool", S.Sb, S.ST, [S.R_ST], [S.R_Sb])
                S.flags["state"] = t
                yield
                if last:
                    for q in range(2):
                        b = nb(mmp)
                        for cc in range(4):
                            c = q * 4 + cc
                            TR(psf(b)[:, cc * 128:(cc + 1) * 128], S.ST[:, c * 128:(c + 1) * 128], ident_f,
                               [S.R_ST, R_identf], [R_ps[b]])
                        CP("act", S.sso[:, q * 4:(q + 1) * 4, :], r3(psf(b), 4), [R_ps[b]], [S.R_sso])
                    oname = "ssm_p" if seq == "p" else "ssm_s"
                    P.dma("sp", dram[oname].rearrange("(c p) n -> p c n", p=128), S.sso, reads=[S.R_sso],
                          writes=[out_res[oname]], sem_res=S.R_sso)
                    yield
                Rm3 = r3(Rm, 16)
                TT("dve", Rm3[0:n, :, 0:n], U_f[0:n, 0:n].unsqueeze(1).to_broadcast([n, 16, n]),
                   dtA.unsqueeze(2).to_broadcast([n, 16, n]), ALU.mult, [R_tri, R_sm], [R_Rm])
                Lt3 = r3(Lt, 16)
                for q in range(4):
                    b = nb(mmp)
                    o = r3(psf(b), 4)[0:n, :, 0:n]
                    if n == 128:
                        MM(o, SL_f[0:n, 0:n], Rm3[0:n, q * 4:(q + 1) * 4, 0:n], True, False, [R_tri, R_Rm], [R_ps[b]])
                        MM(o, ident_b[0:n, 0:n], NM[0:n, :, 0:n], False, True, [R_identb, R_NM], [R_ps[b]])
                    else:
                        for r_ in range(4):
                            MM(o[:, r_, :], SL_f[0:n, 0:n], Rm3[0:n, q * 4 + r_, 0:n], True, False, [R_tri, R_Rm],
                               [R_ps[b]])
                            MM(o[:, r_, :], ident_b[0:n, 0:n], NM[0:n, r_, 0:n], False, True, [R_identb, R_NM],
                               [R_ps[b]])
                    ACT(Lt3[0:n, q * 4:(q + 1) * 4, 0:n], o, AF.Exp, [R_ps[b]], [R_Lt])
                yield
                bc = nb(op_)
                pcb = r3(psf(bc), 4)
                for g in range(4):
                    MM(pcb[0:n, g, 0:n], xact[:, 8 + g, 0:n], xact[:, 12 + g, 0:n], True, True, [R_xact], [R_ps[bc]])
                yield
                Lt4 = Lt.rearrange("p (g r i) -> p g r i", g=4, r=4)
                TT("dve", Lt4[0:n, :, :, 0:n], Lt4[0:n, :, :, 0:n],
                   pcb[0:n, :, 0:n].unsqueeze(2).to_broadcast([n, 4, 4, n]), ALU.mult, [R_Lt, R_ps[bc]], [R_Lt])
                yield
                byd = [nb(mmp), nb(mmp)]
                for h in range(16):
                    b = byd[h // 8]
                    o = psf(b)[0:n, (h % 8) * 64:(h % 8 + 1) * 64]
                    MM(o, Lt3[0:n, h, 0:n], xdt[0:n, h * 64:(h + 1) * 64], True, False, [R_Lt, R_xdt], [R_ps[b]])
                    MM(o, ident_b[0:n, 0:n], xD[0:n, h * 64:(h + 1) * 64], False, True, [R_identb, R_xD], [R_ps[b]])
                yield
                for q in range(2):
                    TT("dve", t1[0:n, q * 512:(q + 1) * 512], psf(byd[q])[0:n, :], t1[0:n, q * 512:(q + 1) * 512], ALU.add,
                       [R_ps[byd[q]], R_t1], [R_t1])
                yield
                for q in range(2):
                    bz = nb(mmp)
                    for kc in range(KC):
                        MM(psf(bz)[0:n, :], hn[:, kc, 0:n], wS[:, kc, C_Z + q * 512:C_Z + (q + 1) * 512], kc == 0,
                           kc == KC - 1, [R_hn, R_wSz], [R_ps[bz]])
                    szh, R_szh = szh_ring.next()
                    ACT(szh[0:n, :], psf(bz)[0:n, :], AF.Silu, [R_ps[bz]], [R_szh])
                    yield
                    TT("pool", t1[0:n, q * 512:(q + 1) * 512], t1[0:n, q * 512:(q + 1) * 512], szh[0:n, :], ALU.mult,
                       [R_t1, R_szh], [R_t1])
                    yield
                ACT(junk_act[0:n, :], t1[0:n, :], AF.Square, [R_t1], [R_junk_act, R_sm], accum_out=sm[0:n, 112:113])
                yield
                TS_("pool", sm[0:n, 113:114], sm[0:n, 112:113], 1.0 / 1024, EPS, ALU.mult, ALU.add, [R_sm], [R_sm])
                yield
                TT("pool", sm[0:n, 113:114], sm[0:n, 113:114], mhalf[0:n, 0:1], ALU.pow, [R_sm, R_mhalf], [R_sm])
                yield
                TS_("dve", yn[0:n, :], t1[0:n, :], sm[0:n, 113:114], None, ALU.mult, None, [R_t1, R_sm], [R_yn])
                yield
                bt = nb(trp)
                pt = r3(psb(bt), 8)
                for kc in range(KC):
                    TR(pt[:, kc, 0:n], yn[0:n, kc * 128:(kc + 1) * 128], ident_b[0:n, 0:n], [R_yn, R_identb], [R_ps[bt]])
                yield
                TT("dve", mergedT[:, :, col0:col0 + n], pt[:, :, 0:n],
                   vec[:, V_SNW:V_SNW + 8].unsqueeze(2).to_broadcast([128, 8, n]),
                   ALU.mult, [R_ps[bt], R_vec], [R_mg[t]])
                yield

            def run_two(gens, lag):
                gens = list(gens)
                active = []
                nxt = 0
                while nxt < len(gens) or active:
                    if not active or (len(active) == 1 and nxt < len(gens) and active[0][1] >= lag):
                        if nxt < len(gens):
                            active.append([gens[nxt], 0])
                            nxt += 1
                    for a_ in list(active):
                        try:
                            next(a_[0])
                            a_[1] += 1
                        except StopIteration:
                            active.remove(a_)

            class SeqState:
                pass
            Sp = SeqState()
            Sp.ST, Sp.R_ST, Sp.Sb, Sp.R_Sb, Sp.hist, Sp.R_hist = ST, R_ST, Sb, R_Sb, hist, R_hist
            Sp.cvT, Sp.R_cvT, Sp.cvio, Sp.R_cvio, Sp.sso, Sp.R_sso = cvT, R_cvT, cvio, R_cvio, sso, R_Rm
            Sp.flags = {"hist": -1, "state": -1}
            Ss = SeqState()
            Ss.ST = A.alloc(1024, F32); Ss.R_ST = Res("STs")
            Ss.Sb = A.alloc(1024, BF16); Ss.R_Sb = Res("Sbs")
            Ss.hist = r3(A.alloc(16 * 3, BF16), 16); Ss.R_hist = Res("hists")
            Ss.cvT = A.alloc(48, F32); Ss.R_cvT = Res("cvTs")
            Ss.cvio = A.alloc(128, F32); Ss.R_cvio = Res("cvios")
            Ss.sso = r3(A.alloc(1024, F32), 8); Ss.R_sso = Res("ssos")
            Ss.flags = {"hist": -1, "state": -1}
            P.dma("sp", stin, state_ssm.rearrange("(c p) n -> p c n", p=128), reads=[], writes=[R_Rm])
            for q in range(2):
                b = nb("mm")
                for cc in range(4):
                    c = q * 4 + cc
                    TR(psf(b)[:, cc * 128:(cc + 1) * 128], stin[:, c, :], ident_f, [R_Rm, R_identf], [R_ps[b]])
                CP("dve", Ss.ST[:, q * 512:(q + 1) * 512], psf(b), [R_ps[b]], [Ss.R_ST])
            CP("pool", Ss.Sb, Ss.ST, [Ss.R_ST], [Ss.R_Sb])
            MEMSET("dve", ST, 0.0, [R_ST])
            MEMSET("pool", Sb, 0.0, [R_Sb])
            gens = [ssd_gen(t, t == 0, t == NT - 2, "p", t % 2, t - 1, Sp) for t in range(0, NT - 1)]
            gens.append(ssd_gen(TS_IDX, True, True, "s", TS_IDX % 2, None, Ss))
            run_two(gens, 24)
            hnS.pool = "tr"
            print("pass S1 arena peak", A.peak)
            A.release(persist_mark)
            P.barrier()

        if stage >= 3:
            XBYTES = 33792
            regX = A.alloc(XBYTES // 2, BF16)
            afterX_mark = A.mark()
            yattT = r3(regX[:, 0:KC * TP], KC)
            hnA = HN(V_N1W, nxt=1, nhn=2)
            hnA.mul_on_act = True
            R_yattT = [Res("yat%d" % t) for t in range(NT)]
            NWA = C_GS - C_Q
            wA_off = A.mark()
            wA_raw = A.alloc(KC * NWA, BF16)
            wA_end = A.mark()
            wA = r3(wA_raw, KC); R_wA = Res("wA")
            load_w(wA, R_wA, w_in, C_Q, C_GS)
            WQ, WK, WV, WQI, WKI, WWI = 0, C_K - C_Q, C_V - C_Q, C_QI - C_Q, C_KI - C_Q, C_WI - C_Q
            bt2 = A.alloc(2 * 8 * 128, F32); R_bt = Res("bt")
            P.dma("sp", bt2, bt_in, writes=[R_bt])
            bt = bt2.rearrange("p (w h i) -> p w h i", w=2, h=8)
            BT_PREV, BT_SAME, BT_META = 0, 1, 2
            qb_ring = Ring(A, D, 1, BF16, "qb")
            qT_ring = Ring(A, 8 * 128, 3, BF16, "qT")
            qiT_ring = Ring(A, 4 * 128, 2, BF16, "qiT")
            ko_ring = Ring(A, 256, 1, F32, "ko")
            vo_ring = Ring(A, 256, 1, F32, "vo")
            kio_ring = Ring(A, 64, 2, F32, "kio")
            kb_ring = Ring(A, 256, 2, BF16, "kb")
            ki2_ring = Ring(A, 128, 2, BF16, "ki2")
            r_ring = Ring(A, 512, 2, F32, "rbuf")
            pm_ring = Ring(A, 512, 4, BF16, "pm")
            ya_ring = Ring(A, D, 1, BF16, "ya")
            smA_ring = Ring(A, 96, 3, F32, "smA")
            bis_ring = Ring(A, 64, 2, F32, "bis")
            pow2 = A.alloc(32, F32); R_pow2 = Res("pow2")
            P.dma("sp", pow2, pow2_in.partition_broadcast(128), writes=[R_pow2])
            commonA_mark = A.mark()
            bank_pools["at"] = [0, 1]
            bank_pools["fr"] = [2, 3]
            hnA.pool = "fr"

            class TileCtx:
                pass

            def front(t, B, ktile_idx, key_tiles, bias_kind, adm_fill, outs, fr):
                kT, vb, kiT2 = B["kT"], B["vb"], B["kiT2"]
                R_k = B["R_k"]
                score, R_score = B["score_ring"].next()
                bis, R_bis = bis_ring.next()
                hn, R_hn = yield from hnA.tile_gen(t, pool=fr)
                yield
                mask, R_mask = B["mask_ring"].next()
                maskT2, R_maskT = B["maskT_ring"].next()
                maskT = r3(maskT2, B["NKT"])
                tname, col0, n = tiles[t]
                kc0, nk_self = key_tiles[ktile_idx]
                L = kc0 + nk_self
                sm, R_sm = smA_ring.next()
                ctx = TileCtx()
                ctx.sm, ctx.R_sm, ctx.maskT, ctx.R_maskT = sm, R_sm, maskT, R_maskT
                qb, R_qb = qb_ring.next()
                qbanks = [nb(fr), nb(fr)]
                for half in range(2):
                    b = qbanks[half]
                    for kc in range(KC):
                        MM(psf(b)[0:n, :], hn[:, kc, 0:n], wA[:, kc, WQ + half * 512: WQ + (half + 1) * 512],
                           kc == 0, kc == KC - 1, [R_hn, R_wA], [R_ps[b]])
                yield
                for half in range(2):
                    b = qbanks[half]
                    for hh in range(4):
                        h = half * 4 + hh
                        ACT(junk_act[0:n, 0:128], psf(b)[0:n, hh * 128:(hh + 1) * 128], AF.Square, [R_ps[b]],
                            [R_junk_act, R_sm], accum_out=sm[0:n, h:h + 1])
                yield
                TS_("pool", sm[0:n, 8:16], sm[0:n, 0:8], 1.0 / 128, EPS, ALU.mult, ALU.add, [R_sm], [R_sm])
                yield
                TT("pool", sm[0:n, 8:16], sm[0:n, 8:16], mhalf[0:n, 0:8], ALU.pow, [R_sm, R_mhalf], [R_sm])
                yield
                for half in range(2):
                    b = qbanks[half]
                    TT("dve", r3(qb[0:n, half * 512:(half + 1) * 512], 4), r3(psf(b)[0:n, :], 4),
                       sm[0:n, 8 + half * 4:12 + half * 4].unsqueeze(2).to_broadcast([n, 4, 128]), ALU.mult,
                       [R_ps[b], R_sm], [R_qb])
                qT2, R_qT = qT_ring.next()
                qT = r3(qT2, 8)
                ctx.qT, ctx.R_qT = qT, R_qT
                btq = nb(fr)
                ptq = r3(psb(btq), 8)
                for h in range(8):
                    TR(ptq[:, h, 0:n], qb[0:n, h * 128:(h + 1) * 128], ident_b[0:n, 0:n], [R_qb, R_identb], [R_ps[btq]])
                bkv = nb(fr)
                for kc in range(KC):
                    MM(psf(bkv)[0:n, :], hn[:, kc, 0:n], wA[:, kc, WK:WK + 512], kc == 0, kc == KC - 1,
                       [R_hn, R_wA], [R_ps[bkv]])
                yield
                ACT(qT[:, :, 0:n], ptq[:, :, 0:n], AF.Identity, [R_ps[btq], R_qnws], [R_qT], scale=qnw_s[:, 0:1])
                bki = nb(fr)
                for kc in range(KC):
                    MM(psf(bki)[0:n, 0:72], hn[:, kc, 0:n], wA[:, kc, WKI:WKI + 72], kc == 0, kc == KC - 1,
                       [R_hn, R_wA], [R_ps[bki]])
                yield
                for g in range(2):
                    ACT(junk_act[0:n, 0:128], psf(bkv)[0:n, g * 128:(g + 1) * 128], AF.Square, [R_ps[bkv]],
                        [R_junk_act, R_sm], accum_out=sm[0:n, 16 + g:17 + g])
                ACT(junk_act[0:n, 0:64], psf(bki)[0:n, 0:64], AF.Square, [R_ps[bki]], [R_junk_act, R_sm],
                    accum_out=sm[0:n, 20:21])
                ACT(sm[0:n, 24:32], psf(bki)[0:n, 64:72], AF.Abs, [R_ps[bki]], [R_sm], scale=WI_SCALE)
                ACT(sm[0:n, 32:40], psf(bki)[0:n, 64:72], AF.Sign, [R_ps[bki]], [R_sm])
                yield
                TS_("pool", sm[0:n, 18:20], sm[0:n, 16:18], 1.0 / 128, EPS, ALU.mult, ALU.add, [R_sm], [R_sm])
                TS_("pool", sm[0:n, 21:22], sm[0:n, 20:21], 1.0 / 64, EPS, ALU.mult, ALU.add, [R_sm], [R_sm])
                yield
                TT("pool", sm[0:n, 18:22], sm[0:n, 18:22], mhalf[0:n, 0:4], ALU.pow, [R_sm, R_mhalf], [R_sm])
                yield
                ko, R_ko = ko_ring.next()
                vo, R_vo = vo_ring.next()
                b = bkv
                for g in range(2):
                    STT("dve", ko[0:n, g * 128:(g + 1) * 128], psf(b)[0:n, g * 128:(g + 1) * 128], sm[0:n, 18 + g:19 + g],
                        rowb[0:n, RB_KNW:RB_KNW + 128], ALU.mult, ALU.mult, [R_ps[b], R_sm, R_rowb], [R_ko])
                CP("act", vo[0:n, :], psf(b)[0:n, 256:512], [R_ps[b]], [R_vo])
                CP("act", vb[0:n, ktile_idx, :, 0:128], r3(psf(b)[0:n, 256:512], 2), [R_ps[b]], [R_k[ktile_idx]])
                P.dma("sp", outs["k"], ko[0:n, :], reads=[R_ko], writes=[out_res[outs["kn"]]], sem_res=R_ko)
                P.dma("sp", outs["v"], vo[0:n, :], reads=[R_vo], writes=[out_res[outs["vn"]]], sem_res=R_vo)
                b = bki
                kio, R_kio = kio_ring.next()
                STT("dve", kio[0:n, :], psf(b)[0:n, 0:64], sm[0:n, 21:22], rowb[0:n, RB_KINW:RB_KINW + 64],
                    ALU.mult, ALU.mult, [R_ps[b], R_sm, R_rowb], [R_kio])
                P.dma("sp", outs["ki"], kio[0:n, :], reads=[R_kio], writes=[out_res[outs["kin"]]], sem_res=R_kio)
                yield
                kb, R_kb = kb_ring.next()
                CP("pool", kb[0:n, :], ko[0:n, :], [R_ko], [R_kb])
                ki2, R_ki2 = ki2_ring.next()
                CP("pool", r3(ki2[0:n, :], 2), kio[0:n, :].unsqueeze(1).to_broadcast([n, 2, 64]), [R_kio], [R_ki2])
                yield
                bt_ = nb(fr)
                pt = r3(psb(bt_), 8)
                for g in range(2):
                    TR(pt[:, g, 0:n], kb[0:n, g * 128:(g + 1) * 128], ident_b[0:n, 0:n], [R_kb, R_identb], [R_ps[bt_]])
                bt2_ = nb(fr)
                TR(psb(bt2_)[:, 0:n], ki2[0:n, :], ident_b[0:n, 0:n], [R_ki2, R_identb], [R_ps[bt2_]])
                yield
                CP("act", kT[:, :, kc0:kc0 + n], pt[:, 0:2, 0:n], [R_ps[bt_]], [R_k[ktile_idx]])
                CP("act", kiT2[:, kc0:kc0 + n], psb(bt2_)[:, 0:n], [R_ps[bt2_]], [R_k[ktile_idx]])
                yield
                if L > TOPK:
                    qiT2, R_qiT = qiT_ring.next()
                    qiT = r3(qiT2, 4)
                    b = nb(fr)
                    pq = r3(psf(b), 4)
                    for m in range(4):
                        for kc in range(KC):
                            MM(pq[:, m, 0:n], wA[:, kc, WQI + m * 128: WQI + (m + 1) * 128], hn[:, kc, 0:n],
                               kc == 0, kc == KC - 1, [R_hn, R_wA], [R_ps[b]])
                    CP("act", qiT[:, :, 0:n], pq[:, :, 0:n], [R_ps[b]], [R_qiT])
                    for c0_ in range(0, L, 512):
                        c1_ = min(L, c0_ + 512)
                        wd = c1_ - c0_
                        rk = [R_k[i] for i, (kc_, nk_) in enumerate(key_tiles[:ktile_idx + 1])
                              if kc_ < c1_ and kc_ + nk_ > c0_]
                        for h in range(8):
                            m, hh = h // 2, h % 2
                            b = nb(fr)
                            MM(psf(b)[0:n, 0:wd], qiT[hh * 64:(hh + 1) * 64, m, 0:n],
                               kiT2[hh * 64:(hh + 1) * 64, c0_:c1_], True, True, [R_qiT] + rk, [R_ps[b]])
                            rb, R_rb = r_ring.next()
                            ACT(rb[0:n, 0:wd], psf(b)[0:n, 0:wd], AF.Relu, [R_ps[b], R_sm], [R_rb],
                                scale=sm[0:n, 24 + h:25 + h])
                            eng = "dve"
                            if h == 0:
                                TS_("dve", score[0:n, c0_:c1_], rb[0:n, 0:wd], sm[0:n, 32:33], None, ALU.mult, None,
                                    [R_rb, R_sm], [R_score])
                            else:
                                STT(eng, score[0:n, c0_:c1_], rb[0:n, 0:wd], sm[0:n, 32 + h:33 + h],
                                    score[0:n, c0_:c1_], ALU.mult, ALU.add, [R_rb, R_sm, R_score], [R_score])
                            if h % 4 == 3:
                                yield
                    P.op("dve", lambda e: e.tensor_reduce(out=sm[0:n, 40:41], in_=score[0:n, 0:L], axis=AX.X,
                                                          op=ALU.max, apply_absolute_value=True), [R_score], [R_sm])
                    if adm_fill is not None:
                        (r0, r1, fc0, fc1) = adm_fill
                        MEMSET("dve", score[r0:r1, fc0:fc1], -1e30, [R_score])
                    TS_("dve", sm[0:n, 41:42], sm[0:n, 40:41], 1.0, None, ALU.add, None, [R_sm], [R_sm])
                    TS_("dve", bis[0:n, 0:NBIS + 2], pow2[0:n, 0:NBIS + 2], sm[0:n, 41:42], None, ALU.mult, None,
                        [R_pow2, R_sm], [R_bis])
                    MEMSET("dve", sm[0:n, 42:43], 0.0, [R_sm])
                    yield
                    for k in range(1, NBIS + 1):
                        TS_("dve", mask[0:n, 0:L], score[0:n, 0:L], sm[0:n, 42:43], 0.0, ALU.is_ge, ALU.add,
                            [R_score, R_sm], [R_mask, R_sm], accum_out=sm[0:n, 43:44])
                        TS_("dve", sm[0:n, 44:45], sm[0:n, 43:44], float(TOPK), 0.5, ALU.is_ge, ALU.subtract,
                            [R_sm], [R_sm])
                        STT("dve", sm[0:n, 42:43], sm[0:n, 44:45], bis[0:n, k - 1:k], sm[0:n, 42:43], ALU.mult, ALU.add,
                            [R_sm, R_bis], [R_sm])
                        yield
                    STT("dve", sm[0:n, 45:46], bis[0:n, NBIS:NBIS + 1], -1.0, sm[0:n, 42:43], ALU.mult, ALU.add,
                        [R_sm, R_bis], [R_sm])
                    TS_("dve", mask[0:n, 0:L], score[0:n, 0:L], sm[0:n, 45:46], None, ALU.is_ge, None,
                        [R_score, R_sm], [R_mask])
                    yield
                else:
                    MEMSET("pool", mask[0:n, 0:L], 1.0, [R_mask])
                    if adm_fill is not None:
                        (r0, r1, fc0, fc1) = adm_fill
                        MEMSET("pool", mask[r0:r1, fc0:fc1], 0.0, [R_mask])
                nkt = ktile_idx + 1
                for k0 in range(0, nkt, 8):
                    k1 = min(nkt, k0 + 8)
                    bt_ = nb(fr)
                    pt = r3(psb(bt_), 8)
                    for kt in range(k0, k1):
                        kc_, nk_ = key_tiles[kt]
                        TR(pt[0:nk_, kt - k0, 0:n], mask[0:n, kc_:kc_ + nk_], ident_b[0:n, 0:n], [R_mask, R_identb],
                           [R_ps[bt_]])
                    ACT(maskT[:, k0:k1, 0:n], pt[:, 0:k1 - k0, 0:n], AF.Identity, [R_ps[bt_]], [R_maskT],
                        scale=-NEG, bias=NEG)
                    yield
                B["ctx"][t] = ctx

            def attn(t, ctx, B, ktile_idx, key_tiles, bias_kind, yT_dst, R_yT):
                kT, vb = B["kT"], B["vb"]
                R_k = B["R_k"]
                tname, col0, n = tiles[t]
                sm, R_sm, maskT, R_maskT, qT, R_qT = ctx.sm, ctx.R_sm, ctx.maskT, ctx.R_maskT, ctx.qT, ctx.R_qT
                nkt = ktile_idx + 1
                near = {kt: bias_kind[kt] for kt in bias_kind}
                far_tiles = [kt for kt in range(nkt) if kt not in near]
                groups = [far_tiles[i:i + 4] for i in range(0, len(far_tiles), 4)]
                near_tiles = sorted(near.keys())
                if near_tiles:
                    groups.append(near_tiles)
                ya, R_ya = ya_ring.next()
                items = [(h, gi) for h in range(8) for gi in range(len(groups))]
                pend = []
                state = {}

                def stage1(h, gi):
                    g = h // 4
                    grp = groups[gi]
                    isnear = grp[0] in near
                    b = nb("at")
                    ps3 = r3(psf(b), 4)
                    for j, kt in enumerate(grp):
                        kc_, nk_ = key_tiles[kt]
                        o = ps3[0:nk_, j, 0:n]
                        MM(o, kT[:, g, kc_:kc_ + nk_], qT[:, h, 0:n], True, False, [R_k[kt], R_qT], [R_ps[b]])
                        MM(o, ident_b[0:nk_, 0:nk_], maskT[0:nk_, kt, 0:n], False, not isnear,
                           [R_identb, R_maskT], [R_ps[b]])
                        if isnear:
                            kind = near[kt]
                            if kind == BT_META:
                                MM(o, ident_f[:, 112:112 + nk_], bt[:, BT_PREV, h, 0:n], False, True,
                                   [R_identf, R_bt], [R_ps[b]])
                            else:
                                MM(o, ident_f[0:nk_, 0:nk_], bt[0:nk_, kind, h, 0:n], False, True,
                                   [R_identf, R_bt], [R_ps[b]])
                    ng = len(grp)
                    pm2, R_pm = pm_ring.next()
                    pm = r3(pm2, 4)
                    if isnear:
                        ACT(pm[:, 0:ng, 0:n], ps3[:, 0:ng, 0:n], AF.Exp, [R_ps[b]], [R_pm])
                    else:
                        ACT(pm[:, 0:ng, 0:n], ps3[:, 0:ng, 0:n], AF.Exp, [R_ps[b], R_rowb], [R_pm],
                            bias=rowb[:, RB_CB + h:RB_CB + h + 1])
                    return (h, gi, pm, R_pm)

                def stage2(h, gi, pm, R_pm):
                    g = h // 4
                    grp = groups[gi]
                    if gi == 0:
                        state["bo"] = nb("o")
                    bo = state["bo"]
                    O = psf(bo)[0:n, 0:129]
                    for j, kt in enumerate(grp):
                        kc_, nk_ = key_tiles[kt]
                        first = (gi == 0 and j == 0)
                        last = (gi == len(groups) - 1 and j == len(grp) - 1)
                        MM(O, pm[0:nk_, j, 0:n], vb[0:nk_, kt, g, :], first, last, [R_pm, R_k[kt]], [R_ps[bo]])
                    if gi == len(groups) - 1:
                        RECIP(sm[0:n, 48 + h:49 + h], psf(bo)[0:n, 128:129], [R_ps[bo]], [R_sm])
                        ACT(ya[0:n, h * 128:(h + 1) * 128], psf(bo)[0:n, 0:128], AF.Identity, [R_ps[bo], R_sm], [R_ya],
                            scale=sm[0:n, 48 + h:49 + h])

                SKEW = 3
                for it in items:
                    pend.append(stage1(*it))
                    if len(pend) > SKEW:
                        stage2(*pend.pop(0))
                    yield
                while pend:
                    stage2(*pend.pop(0))
                bt_ = nb("at")
                pt = r3(psb(bt_), 8)
                for kc in range(KC):
                    TR(pt[:, kc, 0:n], ya[0:n, kc * 128:(kc + 1) * 128], ident_b[0:n, 0:n], [R_ya, R_identb], [R_ps[bt_]])
                CP("act", yT_dst, pt[:, :, 0:n], [R_ps[bt_]], [R_yT])
                yield

            def key_bufs(LMAX, NKT, nmask, pre_alloc=None):
                B = {"NKT": NKT, "ctx": {}}
                if pre_alloc is None:
                    B["kT"] = r3(A.alloc(2 * LMAX, BF16), 2)
                    B["vb"] = A.alloc(NKT * 2 * 129, BF16).rearrange("p (t g d) -> p t g d", t=NKT, g=2)
                else:
                    B["kT"] = r3(pre_alloc[:, 0:2 * LMAX], 2)
                    B["vb"] = pre_alloc[:, 2 * LMAX:2 * LMAX + NKT * 2 * 129].rearrange("p (t g d) -> p t g d",
                                                                                      t=NKT, g=2)
                B["kiT2"] = A.alloc(LMAX, BF16)
                raws = [A.alloc(2 * LMAX, BF16) for _ in range(nmask)]
                B["score_raw"] = raws[0]

                class _SR:
                    def __init__(self, items):
                        self.items = items
                        self.i = 0

                    def next(self):
                        it = self.items[self.i % len(self.items)]
                        self.i += 1
                        return it
                B["score_ring"] = _SR([(r_.bitcast(F32)[:, 0:LMAX], Res("score%d" % i)) for i, r_ in enumerate(raws)])
                B["mask_ring"] = Ring(A, LMAX, nmask, BF16, "mask")
                B["maskT_ring"] = Ring(A, NKT * 128, nmask + 1 if nmask > 1 else 1, BF16, "maskT")
                B["R_k"] = [Res("kt%d" % i) for i in range(NKT)]
                MEMSET("pool", B["vb"][:, :, :, 128:129], 1.0, B["R_k"])
                return B

            def gate_gen(t, hn, R_hn, yT, R_yT):
                tname, col0, n = tiles[t]
                for half in range(2):
                    bb = 7
                    pbr = r3(psf(bb), 4)
                    for cc in range(4):
                        c = half * 4 + cc
                        for kc in range(KC):
                            MM(pbr[:, cc, 0:n], wbrA[:, kc, c * 128:(c + 1) * 128], yT[:, kc, 0:n], kc == 0,
                               kc == KC - 1, [R_wbrA, R_yT], [R_ps[bb]])
                    yield
                    bg = 4
                    pg = r3(psf(bg), 4)
                    for cc in range(4):
                        c = half * 4 + cc
                        for kc in range(KC):
                            MM(pg[:, cc, 0:n], wgA[:, kc, c * 128:(c + 1) * 128], hn[:, kc, 0:n], kc == 0,
                               kc == KC - 1, [R_wgA, R_hn], [R_ps[bg]])
                    yield
                    sg2, R_sg = sgA_ring.next()
                    sg = r3(sg2, 4)
                    ACT(sg[:, :, 0:n], pg[:, :, 0:n], AF.Sigmoid, [R_ps[bg]], [R_sg])
                    yield
                    tm2, R_tm = tmA_ring.next()
                    tm = r3(tm2, 4)
                    TT("dve", tm[:, :, 0:n], pbr[:, :, 0:n], sg[:, :, 0:n], ALU.mult, [R_ps[bb], R_sg], [R_tm])
                    yield
                    dst = mergedT[:, half * 4:(half + 1) * 4, col0:col0 + n]
                    TT("pool", dst, dst, tm[:, :, 0:n], ALU.add, [R_tm, R_mg[t]], [R_mg[t]])
                    yield

            def interleave(streams):
                gens = [s[0] for s in streams]
                est = [max(1, s[1]) for s in streams]
                prog = [0] * len(gens)
                alive = [True] * len(gens)
                while any(alive):
                    best = None
                    for i in range(len(gens)):
                        if alive[i] and (best is None or prog[i] / est[i] < prog[best] / est[best]):
                            best = i
                    try:
                        next(gens[best])
                        prog[best] += 1
                    except StopIteration:
                        alive[best] = False

            def run_all(g):
                for _ in g:
                    pass

            LS = PAST + TS
            assert 2 * LS + 33 * 2 * 129 <= XBYTES // 2
            Bs = key_bufs(LS, 33, 1, pre_alloc=regX)
            R_kc = Bs["R_k"]
            R_stg = Bs["score_ring"].items[0][1]
            stg = Bs["score_raw"][:, 0:4096]
            stg_i = stg.rearrange("p (t r d) -> p t r d", t=32, r=2)
            P.dma("pool", stg_i[:, :, 0, :], cache_ki.rearrange("(t p) d -> p t d", p=128), writes=[R_stg], sem_res=R_stg)
            P.dma_more("pool", stg_i[:, :, 1, :], cache_ki.rearrange("(t p) d -> p t d", p=128), R_stg)
            for t0 in range(0, 32, 8):
                bt_ = nb("tr")
                pt = r3(psb(bt_), 8)
                for j in range(8):
                    TR(pt[:, j, :], stg_i[:, t0 + j].rearrange("p r d -> p (r d)"), ident_b, [R_stg, R_identb],
                       [R_ps[bt_]])
                CP("act" if (t0 // 8) % 2 == 0 else "dve", Bs["kiT2"][:, t0 * 128:(t0 + 8) * 128], psb(bt_), [R_ps[bt_]],
                   [R_kc[i] for i in range(t0, t0 + 8)])
            stgk2 = A.alloc(8 * 256, BF16); R_stgk = Res("stgk")
            stg_k = r3(stgk2, 8)
            bank_pools["frk"] = [4, 7]
            cnt["frk"] = 0

            def cache_kv_gen():
                for t0 in range(0, 32, 8):
                    cvv = cache_v.rearrange("(t p) (g d) -> p t g d", p=128, g=2)
                    P.dma("pool", Bs["vb"][:, t0:t0 + 8, 0, 0:128], cvv[:, t0:t0 + 8, 0, :],
                          writes=[R_kc[i] for i in range(t0, t0 + 8)], sem_res=R_kc[t0])
                    P.dma_more("pool", Bs["vb"][:, t0:t0 + 8, 1, 0:128], cvv[:, t0:t0 + 8, 1, :], R_kc[t0])
                    for i in range(t0 + 1, t0 + 8):
                        R_kc[i].w = dict(R_kc[t0].w)
                yield
                for t0 in range(0, 32, 8):
                    P.dma("pool", stg_k, cache_k.rearrange("(t p) c -> p t c", p=128)[:, t0:t0 + 8, :], writes=[R_stgk])
                    yield
                    for g in range(2):
                        bt_ = nb("frk")
                        pt = r3(psb(bt_), 8)
                        for j in range(8):
                            TR(pt[:, j, :], stg_k[:, j, g * 128:(g + 1) * 128], ident_b, [R_stgk, R_identb], [R_ps[bt_]])
                        yield
                        CP("act", Bs["kT"][:, g, t0 * 128:(t0 + 8) * 128], psb(bt_), [R_ps[bt_]],
                           [R_kc[i] for i in range(t0, t0 + 8)])
                        yield
            key_tiles_s = [(i * 128, 128) for i in range(32)] + [(PAST, TS)]
            bk_s = {31: BT_PREV, 32: BT_SAME}
            interleave([(front(TS_IDX, Bs, 32, key_tiles_s, bk_s, None,
                               dict(k=k_s, v=v_s, ki=ki_s, kn="k_s", vn="v_s", kin="ki_s"), "fr"), 60),
                        (cache_kv_gen(), 18)])
            run_all(attn(TS_IDX, Bs["ctx"][TS_IDX], Bs, 32, key_tiles_s, bk_s,
                         yattT_s[:, :, :], R_yattT[TS_IDX]))
            print("pass A(sample) arena peak", A.peak)
            A.release(commonA_mark)
            P.barrier()

            Bp = key_bufs(TP, 17, 2)
            key_tiles_p = [(0, 16)] + [(16 + 128 * (j - 1), 128) for j in range(1, 17)]
            NP_ = NT - 1
            bank_pools["fr0"] = [2, 3]
            bank_pools["fr1"] = [7, 4]
            cnt["fr0"] = 0
            cnt["fr1"] = 0

            def bias_kind_p(t):
                if t == 0:
                    return {0: BT_SAME}
                if t == 1:
                    return {0: BT_META, 1: BT_SAME}
                return {t - 1: BT_PREV, t: BT_SAME}

            def front_p(t):
                tname, col0, n = tiles[t]
                adm = (0, 64, col0 + 64, col0 + 128) if t >= 1 else None
                return front(t, Bp, t, key_tiles_p, bias_kind_p(t), adm,
                             dict(k=k_p[col0:col0 + n, :], v=v_p[col0:col0 + n, :], ki=ki_p[col0:col0 + n, :],
                                  kn="k_p", vn="v_p", kin="ki_p"), "fr%d" % (t % 2))

            def front_steps(t):
                L = key_tiles_p[t][0] + key_tiles_p[t][1]
                if L <= TOPK:
                    return 20
                return 20 + 2 * ((L + 511) // 512) + NBIS + 5 + (t + 8) // 8

            def limited(g, k):
                for _ in range(k):
                    try:
                        next(g)
                    except StopIteration:
                        return
                    yield

            def attn_p(t):
                tname, col0, n = tiles[t]
                return attn(t, Bp["ctx"][t], Bp, t, key_tiles_p, bias_kind_p(t), yattT[:, :, col0:col0 + n], R_yattT[t])

            run_all(front_p(0))
            fcur = front_p(1)
            run_all(limited(fcur, front_steps(1) // 2))
            for t in range(NP_):
                nfar_ = (t + 1) - len(bias_kind_p(t))
                streams = [(attn_p(t), 8 * ((nfar_ + 3) // 4 + 1) + 1)]
                fnext = None
                if t + 1 < NP_:
                    streams.append((fcur, max(1, front_steps(t + 1) - front_steps(t + 1) // 2)))
                if t + 2 < NP_:
                    fnext = front_p(t + 2)
                    streams.append((limited(fnext, front_steps(t + 2) // 2), front_steps(t + 2) // 2))
                interleave(streams)
                if t + 1 < NP_:
                    run_all(fcur)
                fcur = fnext
                if t == NP_ - 2:
                    wbrS = r3(wA_raw[:, 0:KC * D], KC)
                    wgS = r3(wA_raw[:, KC * D:2 * KC * D], KC)
                    for j_, (dst_, w_, c0_) in enumerate(((wbrS, w_br_ssd, 0), (wgS, w_in, C_GS))):
                        if j_ == 0:
                            P.dma("pool", dst_, wsrc(w_, c0_, c0_ + D), writes=[R_wA], sem_res=R_wA)
                        else:
                            P.dma_more("pool", dst_, wsrc(w_, c0_, c0_ + D), R_wA)
            print("pass A arena peak", A.peak)
            A.release(afterX_mark)
            P.barrier()
            assert 2 * KC * D <= KC * NWA
            R_wbrS = R_wA
            R_wgS = R_wA
            sgM_ring = Ring(A, 512, 4, F32, "sgM")
            assert A.mark() <= wA_off, (A.mark(), wA_off)
            A.off = wA_end
            hnM = HN(V_N1W)
            wbrA = r3(A.alloc(KC * D, BF16), KC); R_wbrA = Res("wbrA")
            load_w(wbrA, R_wbrA, w_br_att, 0, D)
            wgA = r3(A.alloc(KC * D, BF16), KC); R_wgA = Res("wgA")
            load_w(wgA, R_wgA, w_in, C_GA, C_GA + D)
            WO_OFF = (A.nbytes - KC * D * 2) // 64 * 64
            wo = r3(arena_t[:, WO_OFF // 2: WO_OFF // 2 + KC * D], KC); R_wo = Res("wo")
            load_w(wo, R_wo, w_out, 0, D)
            tmM_ring = Ring(A, 512, 4, F32, "tmM")
            for par in range(2):
                bank_pools["m_mm%d" % par] = [0, 1, 2] if par == 0 else [4, 5, 6]
                bank_pools["m_tr%d" % par] = [3] if par == 0 else [7]
                cnt["m_mm%d" % par] = 0
                cnt["m_tr%d" % par] = 0

            def gateM(t, par):
                mmp = "m_mm%d" % par
                tname, col0, n = tiles[t]
                hn, R_hn = yield from hnM.tile_gen(t, pool="m_tr%d" % par)
                yield
                mg = mergedT[:, :, col0:col0 + n]

                def mm32(bank, w, R_w, src, R_src, half):
                    p4 = r3(psf(bank), 4)
                    for cc in range(4):
                        c = half * 4 + cc
                        for kc in range(KC):
                            MM(p4[:, cc, 0:n], w[:, kc, c * 128:(c + 1) * 128], src[:, kc, 0:n], kc == 0, kc == KC - 1,
                               [R_w, R_src], [R_ps[bank]])
                    return p4
                ba = nb(mmp); pa = mm32(ba, wbrS, R_wbrS, mg, R_mg[t], 0)
                bb = nb(mmp); pb_ = mm32(bb, wbrS, R_wbrS, mg, R_mg[t], 1)
                yield
                for half, (bbr, pbr) in enumerate(((ba, pa), (bb, pb_))):
                    bg = nb(mmp); pg = mm32(bg, wgS, R_wgS, hn, R_hn, half)
                    yield
                    sg2, R_sg = sgM_ring.next(); sg = r3(sg2, 4)
                    ACT(sg[:, :, 0:n], pg[:, :, 0:n], AF.Sigmoid, [R_ps[bg]], [R_sg])
                    yield
                    TT("dve", mergedT[:, half * 4:(half + 1) * 4, col0:col0 + n], pbr[:, :, 0:n], sg[:, :, 0:n], ALU.mult,
                       [R_ps[bbr], R_sg], [R_mg[t]])
                    yield
                yT = yattT_s if t == TS_IDX else yattT[:, :, col0:col0 + n]
                for half in range(2):
                    bbr = nb(mmp); pbr = mm32(bbr, wbrA, R_wbrA, yT, R_yattT[t], half)
                    bg = nb(mmp); pg = mm32(bg, wgA, R_wgA, hn, R_hn, half)
                    yield
                    sg2, R_sg = sgM_ring.next(); sg = r3(sg2, 4)
                    ACT(sg[:, :, 0:n], pg[:, :, 0:n], AF.Sigmoid, [R_ps[bg]], [R_sg])
                    yield
                    tm2, R_tm = tmM_ring.next(); tm = r3(tm2, 4)
                    TT("dve", tm[:, :, 0:n], pbr[:, :, 0:n], sg[:, :, 0:n], ALU.mult, [R_ps[bbr], R_sg], [R_tm])
                    yield
                    dst = mergedT[:, half * 4:(half + 1) * 4, col0:col0 + n]
                    TT("pool", dst, dst, tm[:, :, 0:n], ALU.add, [R_tm, R_mg[t]], [R_mg[t]])
                    yield

            def run_two_m(gens, lag):
                gens = list(gens)
                active = []
                nxt = 0
                while nxt < len(gens) or active:
                    if not active or (len(active) == 1 and nxt < len(gens) and active[0][1] >= lag):
                        if nxt < len(gens):
                            active.append([gens[nxt], 0])
                            nxt += 1
                    for a_ in list(active):
                        try:
                            next(a_[0])
                            a_[1] += 1
                        except StopIteration:
                            active.remove(a_)
            run_two_m([gateM(t, t % 2) for t in range(NT)], 5)
            hnM.pool = "tr"
            print("pass M arena peak", A.peak)
            assert A.peak_since_release <= WO_OFF, (A.peak_since_release, WO_OFF)
            A.release(persist_mark)
            P.barrier()

        if stage >= 4:
            h_all = r3(A.alloc(NT * D, F32), NT); R_h = [Res("h%d" % t) for t in range(NT)]
            hn2T = r3(A.alloc(KC * NCOL, BF16), KC); R_hn2 = [Res("hn2_%d" % t) for t in range(NT)]
            markO = A.mark()
            wgt_ring = Ring(A, KC * 256, 2, BF16, "wgt")
            wup_ring = Ring(A, KC * 256, 2, BF16, "wup")
            markO2 = A.mark()
            hnO = HN(V_N2W, nxt=2, nhn=0)
            pre_pairs = []
            for fp0 in (0, 2):
                wgt2_, R_wgt_ = wgt_ring.next()
                wup2_, R_wup_ = wup_ring.next()
                P.dma("pool", r3(wgt2_, KC)[:, :, 0:256], wsrc(w_gate, fp0 * 128, (fp0 + 2) * 128), writes=[R_wgt_])
                P.dma("pool", r3(wup2_, KC)[:, :, 0:256], wsrc(w_up, fp0 * 128, (fp0 + 2) * 128), writes=[R_wup_])
                pre_pairs.append((wgt2_, R_wgt_, wup2_, R_wup_))
            for par in range(2):
                bank_pools["o_mm%d" % par] = [0, 1] if par == 0 else [4, 5]
                bank_pools["o_tr%d" % par] = [2] if par == 0 else [6]
                cnt["o_mm%d" % par] = 0
                cnt["o_tr%d" % par] = 0

            def passO_gen(t, par):
                tname, col0, n = tiles[t]
                xt, R_xt = hnO.xt.next()
                P.dma("sp", xt[0:n, :], xin[col0:col0 + n, :], writes=[R_xt])
                banks = [nb("o_mm%d" % par), nb("o_mm%d" % par)]
                for q in range(2):
                    b = banks[q]
                    for kc in range(KC):
                        MM(psf(b)[0:n, :], mergedT[:, kc, col0:col0 + n], wo[:, kc, q * 512:(q + 1) * 512], kc == 0,
                           kc == KC - 1, [R_mg[t], R_wo], [R_ps[b]])
                yield
                for q in range(2):
                    b = banks[q]
                    TT("dve", h_all[0:n, t, q * 512:(q + 1) * 512], psf(b)[0:n, :], xt[0:n, q * 512:(q + 1) * 512],
                       ALU.add, [R_ps[b], R_xt], [R_h[t]])
                yield
                yield from hnO.tile_gen(t, src=h_all[:, t, :], R_src=R_h[t], dst=hn2T[:, :, col0:col0 + n],
                                        R_dst=R_hn2[t], pool="o_tr%d" % par)
                yield
            run_two_m([passO_gen(t, t % 2) for t in range(NT)], 4)
            print("pass O arena peak", A.peak)
            assert A.peak_since_release <= WO_OFF, (A.peak_since_release, WO_OFF)
            A.release(markO2)
            A.nbytes = ARENA_BYTES
            P.barrier()

            slices = [(0, 6), (6, 12), (12, 17), (17, 22)]
            NFMAX = 6
            actT = r3(A.alloc(NFMAX * NCOL, BF16), NFMAX); R_act = [Res("act%d" % c) for c in range(NFMAX)]
            wd_ring = Ring(A, NFMAX * D, 1, BF16, "wd")
            s_ring = Ring(A, 512, 2, F32, "silu")
            blocks = [(0, 512), (512, 1024), (1024, 1536), (1536, 2048), (2048, NCOL)]

            def tiles_in(c0, c1):
                return [t for t, (_, col0, n) in enumerate(tiles) if col0 < c1 and col0 + n > c0]
            for (f0, f1) in slices:
                nf = f1 - f0
                for fp in range(f0, f1, 2):
                    npair = min(2, f1 - fp)
                    if pre_pairs:
                        assert npair == 2
                        wgt2, R_wgt, wup2, R_wup = pre_pairs.pop(0)
                        wgt = r3(wgt2, KC)
                        wup = r3(wup2, KC)
                    else:
                        wgt2, R_wgt = wgt_ring.next()
                        wup2, R_wup = wup_ring.next()
                        wgt = r3(wgt2, KC)
                        wup = r3(wup2, KC)
                        P.dma("pool", wgt[:, :, 0:npair * 128], wsrc(w_gate, fp * 128, (fp + npair) * 128), writes=[R_wgt])
                        P.dma("pool", wup[:, :, 0:npair * 128], wsrc(w_up, fp * 128, (fp + npair) * 128), writes=[R_wup])
                    for ci in range(npair):
                        cl = fp + ci - f0
                        for (c0, c1) in blocks:
                            wdt = c1 - c0
                            rt = [R_hn2[t] for t in tiles_in(c0, c1)]
                            bg = nb("mm")
                            bu = nb("mm")
                            for kc in range(KC):
                                MM(psf(bg)[:, 0:wdt], wgt[:, kc, ci * 128:(ci + 1) * 128], hn2T[:, kc, c0:c1], kc == 0,
                                   kc == KC - 1, [R_wgt] + rt, [R_ps[bg]])
                            for kc in range(KC):
                                MM(psf(bu)[:, 0:wdt], wup[:, kc, ci * 128:(ci + 1) * 128], hn2T[:, kc, c0:c1], kc == 0,
                                   kc == KC - 1, [R_wup] + rt, [R_ps[bu]])
                            sl_, R_sl = s_ring.next()
                            ACT(sl_[:, 0:wdt], psf(bg)[:, 0:wdt], AF.Silu, [R_ps[bg]], [R_sl])
                            TT("dve", actT[:, cl, c0:c1], psf(bu)[:, 0:wdt], sl_[:, 0:wdt], ALU.mult,
                               [R_ps[bu], R_sl], [R_act[cl]])
                wd2, R_wd = wd_ring.next()
                wd = r3(wd2, NFMAX)
                P.dma("pool", wd[:, 0:nf, :], w_down.rearrange("(c p) d -> p c d", p=128)[:, f0:f1, :], writes=[R_wd])
                for t in range(NT):
                    tname, col0, n = tiles[t]
                    for q in range(2):
                        b = nb("o")
                        for cl in range(nf):
                            MM(psf(b)[0:n, :], actT[:, cl, col0:col0 + n], wd[:, cl, q * 512:(q + 1) * 512], cl == 0,
                               cl == nf - 1, [R_act[cl], R_wd], [R_ps[b]])
                        TT("dve", h_all[0:n, t, q * 512:(q + 1) * 512], h_all[0:n, t, q * 512:(q + 1) * 512],
                           psf(b)[0:n, :], ALU.add, [R_ps[b], R_h[t]], [R_h[t]])
            for t in range(1, NT):
                tname, col0, n = tiles[t]
                if t == TS_IDX:
                    P.dma("sp", y_s[:, :], h_all[0:n, t, :], reads=[R_h[t]], writes=[out_res["y_s"]], sem_res=R_h[t])
                else:
                    P.dma("sp", y_p[col0 - 16:col0 - 16 + n, :], h_all[0:n, t, :], reads=[R_h[t]],
                          writes=[out_res["y_p"]], sem_res=R_h[t])
            print("pass F arena peak", A.peak)

        P.final = [out_res[k] for k in out_names]
        P.emit(st)
    return nc


def _static_consts():
    c = {}
    c["ident"] = np.eye(128, dtype=np.float32)
    k = np.arange(128)[:, None]
    i = np.arange(128)[None, :]
    U = (k <= i).astype(np.float32)
    SL = (k > i).astype(np.float32)
    ones = np.ones((128, 128), np.float32)
    c["tri"] = np.ascontiguousarray(np.concatenate([U, SL, ones], axis=1))
    c["nm"] = np.where(i < k, np.float32(NEG), np.float32(0.0)).astype(np.float32)
    c["pow2"] = (2.0 ** -np.arange(32, dtype=np.float64)).astype(np.float32)[None, :]
    return c


def _bias_tables(rel_bias):
    ss = np.arange(128)[:, None]
    ii = np.arange(128)[None, :]
    tabs = []
    for off in (-128, 0):
        tabs.append(rel_bias[t5_bucket_np(ss + off - ii)])
    bt = np.stack(tabs, axis=1)
    bt = bt.transpose(0, 1, 3, 2)
    return np.ascontiguousarray(bt.reshape(128, -1)).astype(np.float32)


def make_in_maps(inputs, cores):
    g = lambda k: np.asarray(inputs[k], dtype=np.float32)
    x_prompt, x_sample = g("x_prompt"), g("x_sample")
    meta = g("meta_tokens")
    consts = _static_consts()
    rel_bias = g("rel_bias")
    bt = _bias_tables(rel_bias)
    vecs = np.zeros((128, 64), np.float32)
    vecs[:, 0:8] = g("norm1_w")[0].reshape(8, 128).T
    vecs[:, 8:16] = g("norm2_w")[0].reshape(8, 128).T
    vecs[:, 16:24] = g("ssd_norm_w")[0].reshape(8, 128).T
    vecs[:, 24] = g("q_norm_w")[0]
    rows = np.zeros((1, 512), np.float32)
    rows[0, 0:128] = g("k_norm_w")[0]
    rows[0, 128:192] = g("idx_k_norm_w")[0]
    rows[0, 192:208] = g("dt_bias")[0]
    rows[0, 208:224] = g("a_log")[0]
    rows[0, 224:240] = g("d_skip")[0]
    rows[0, 240:248] = rel_bias[15]
    convw = np.ascontiguousarray(g("conv_w")[0].reshape(4, 16, 128).transpose(2, 0, 1).reshape(128, 64))
    convb = np.ascontiguousarray(g("conv_b")[0].reshape(16, 128).T)
    shared = dict(
        w_in=g("w_in")[0], w_br_ssd=g("w_br_ssd")[0], w_br_att=g("w_br_att")[0], w_out=g("w_out")[0],
        w_gate=g("w_gate")[0], w_up=g("w_up")[0], w_down=g("w_down")[0],
        vecs=vecs, rows=rows, convw=convw, convb=convb, ident=consts["ident"], tri=consts["tri"], nm=consts["nm"],
        bt=bt, pow2=consts["pow2"])
    in_maps = []
    for b in cores:
        m = dict(shared)
        m["xin"] = np.ascontiguousarray(np.concatenate([meta, x_prompt[b], x_sample[b]], axis=0))
        m["cache_k"] = np.ascontiguousarray(g("cache_k")[0, b].reshape(PAST, 256))
        m["cache_v"] = np.ascontiguousarray(g("cache_v")[0, b].reshape(PAST, 256))
        m["cache_ki"] = np.ascontiguousarray(g("cache_kidx")[0, b])
        m["state_ssm"] = np.ascontiguousarray(g("state_ssm")[0, b].reshape(1024, 128))
        m["state_conv"] = np.ascontiguousarray(g("state_conv")[0, b])
        in_maps.append(m)
    return in_maps


_NC_CACHE = {}


def kernel(**inputs):
    if "nc" not in _NC_CACHE:
        _NC_CACHE["nc"] = build_program()
    nc = _NC_CACHE["nc"]
    cores = list(range(8))
    in_maps = make_in_maps(inputs, cores)
    res = run_bass_kernel_spmd(nc, in_maps, core_ids=cores)
    r = res.results
    st = lambda name: np.stack([np.asarray(r[b][name], dtype=np.float32) for b in cores], axis=0)
    y_prompt = st("y_p")
    y_sample = st("y_s")
    k_prompt = st("k_p").reshape(1, 8, TP, 2, 128)
    v_prompt = st("v_p").reshape(1, 8, TP, 2, 128)
    kidx_prompt = st("ki_p").reshape(1, 8, TP, 64)
    ssm_prompt = st("ssm_p").reshape(1, 8, 16, 64, 128)
    conv_prompt = st("conv_p").reshape(1, 8, 3, 2048)
    k_sample = st("k_s").reshape(1, 8, TS, 2, 128)
    v_sample = st("v_s").reshape(1, 8, TS, 2, 128)
    kidx_sample = st("ki_s").reshape(1, 8, TS, 64)
    ssm_sample = st("ssm_s").reshape(1, 8, 16, 64, 128)
    conv_sample = st("conv_s").reshape(1, 8, 3, 2048)
    return (y_prompt, y_sample, k_prompt, v_prompt, kidx_prompt, ssm_prompt, conv_prompt,
            k_sample, v_sample, kidx_sample, ssm_sample, conv_sample)
```

```python
import math
from contextlib import ExitStack

import numpy as np
import concourse.bass as bass
import concourse.mybir as mybir
from concourse.bass_utils import run_bass_kernel_spmd

F32 = mybir.dt.float32
BF16 = mybir.dt.bfloat16
ALU = mybir.AluOpType
AF = mybir.ActivationFunctionType
AX = mybir.AxisListType

ENGS = ("pe", "act", "dve", "pool", "sp")

D = 1024
SEQ = 2048
NMETA = 16
TP = NMETA + SEQ
TS = 64
PAST = 4096
NCOL = TP + TS
KC = 8
IN_DIM = 7256
C_Z, C_XBC, C_DT, C_Q, C_K, C_V, C_QI, C_KI, C_WI, C_GS, C_GA = (
    0, 1024, 3072, 3088, 4112, 4368, 4624, 5136, 5200, 5208, 6232)
DFF = 2816
NFF = 22
EPS = 1e-6
TOPK = 256
NBIS = 18
NEG = -30000.0
WI_SCALE = (8 ** -0.5) * (64 ** -0.5)


class Res:
    __slots__ = ("name", "w", "r", "dsem", "dcnt", "excl")

    def __init__(self, name, excl=False):
        self.name = name
        self.w = {}
        self.r = {}
        self.dsem = None
        self.dcnt = 0
        self.excl = excl


class Ins:
    __slots__ = ("eng", "fn", "waits", "signal", "ticket", "dma")

    def __init__(self, eng, fn):
        self.eng = eng
        self.fn = fn
        self.waits = []
        self.signal = False
        self.ticket = None
        self.dma = None


class Prog:
    def __init__(self, nc):
        self.nc = nc
        self.q = {e: [] for e in ENGS}
        self.dma_res = []
        self.last = {e: None for e in ENGS}
        self.pending = {e: [] for e in ENGS}
        self.final = []

    def _add(self, ins, waits):
        eng = ins.eng
        if self.pending[eng]:
            waits = waits + self.pending[eng]
            self.pending[eng] = []
        ins.waits = waits
        for ev in waits:
            if ev[0] == 'c':
                ev[1].signal = True
        self.q[eng].append(ins)

    def op(self, eng, fn, reads=(), writes=()):
        ins = Ins(eng, fn)
        waits = []
        for R in reads:
            for ev in R.w.values():
                if ev[0] == 'c' and ev[1].eng == eng and eng == "pe":
                    continue
                waits.append(ev)
            if R.excl:
                for ev in R.r.values():
                    if ev[0] == 'c' and ev[1].eng == eng:
                        continue
                    waits.append(ev)
        for R in writes:
            for ev in R.w.values():
                if ev[0] == 'c' and ev[1].eng == eng:
                    continue
                waits.append(ev)
            for ev in R.r.values():
                if ev[0] == 'c' and ev[1].eng == eng:
                    continue
                waits.append(ev)
        self._add(ins, waits)
        me = ('c', ins)
        for R in reads:
            R.r[eng] = me
        for R in writes:
            R.w = {eng: me}
            R.r = {}
        self.last[eng] = ins
        return ins

    def dma(self, eng, out, in_, reads=(), writes=(), sem_res=None, **kw):
        if sem_res is None:
            sem_res = writes[0] if writes else reads[0]
        if sem_res.dsem is None:
            sem_res.dsem = True
            self.dma_res.append(sem_res)

        def fn(e, out=out, in_=in_, kw=kw):
            return e.dma_start(out=out, in_=in_, **kw)
        ins = Ins(eng, fn)
        waits = []
        for R in reads:
            waits.extend(R.w.values())
        for R in writes:
            waits.extend(R.w.values())
            waits.extend(R.r.values())
        self._add(ins, waits)
        sem_res.dcnt += 16
        ins.dma = (sem_res, sem_res.dcnt)
        me = ('d', sem_res, sem_res.dcnt)
        key = ('d', id(sem_res))
        for R in reads:
            R.r[key] = me
        for R in writes:
            R.w = {key: me}
            R.r = {}
        return ins

    def dma_more(self, eng, out, in_, R, **kw):
        def fn(e, out=out, in_=in_, kw=kw):
            return e.dma_start(out=out, in_=in_, **kw)
        ins = Ins(eng, fn)
        self._add(ins, [])
        R.dcnt += 16
        ins.dma = (R, R.dcnt)
        R.w = {('d', id(R)): ('d', R, R.dcnt)}
        return ins

    def barrier(self):
        evs = []
        for e in ENGS:
            if self.last[e] is not None:
                evs.append(('c', self.last[e]))
        for R in self.dma_res:
            evs.append(('d', R, R.dcnt))
        for e in ENGS:
            self.pending[e] = self.pending[e] + [
                ev for ev in evs if not (ev[0] == 'c' and ev[1].eng == e)]

    def emit(self, stack):
        nc = self.nc
        esem = {e: stack.enter_context(nc.semaphore("s_" + e)) for e in ENGS}
        for i, R in enumerate(self.dma_res):
            R.dsem = stack.enter_context(nc.semaphore("d%d" % i))
        for e in ENGS:
            t = 0
            for ins in self.q[e]:
                if ins.signal:
                    t += 1
                    ins.ticket = t
        block = stack.enter_context(nc.Block())
        final = self.final

        def run(e, eo):
            seen = {}

            def wait(ev):
                if ev[0] == 'c':
                    sem, val, key = esem[ev[1].eng], ev[1].ticket, ev[1].eng
                else:
                    sem, val, key = ev[1].dsem, ev[2], id(ev[1])
                if seen.get(key, 0) >= val:
                    return
                seen[key] = val
                eo.wait_ge(sem, val)
            for ins in self.q[e]:
                for ev in ins.waits:
                    wait(ev)
                bi = ins.fn(eo)
                if ins.dma is not None:
                    bi.then_inc(ins.dma[0].dsem, 16)
                elif ins.signal:
                    bi.then_inc(esem[e], 1)
            if e == "sp":
                for R in final:
                    for ev in R.w.values():
                        wait(ev)

        @block.tensor
        def _(eo):
            run("pe", eo)

        @block.scalar
        def _(eo):
            run("act", eo)

        @block.vector
        def _(eo):
            run("dve", eo)

        @block.gpsimd
        def _(eo):
            run("pool", eo)

        @block.sync
        def _(eo):
            run("sp", eo)


class Arena:
    def __init__(self, base, nbytes):
        self.base = base
        self.nbytes = nbytes
        self.off = 0
        self.peak = 0
        self.peak_since_release = 0

    def alloc(self, n, dt):
        esz = 4 if dt == F32 else 2
        nb = (n * esz + 63) // 64 * 64
        assert self.off + nb <= self.nbytes, ("arena overflow", self.off, nb, self.nbytes)
        a = self.base[:, self.off // 2:(self.off + nb) // 2]
        self.off += nb
        self.peak = max(self.peak, self.off)
        self.peak_since_release = max(self.peak_since_release, self.off)
        if dt == F32:
            a = a.bitcast(F32)
        return a[:, 0:n]

    def mark(self):
        return self.off

    def release(self, m):
        self.off = m
        self.peak_since_release = m


class Ring:
    def __init__(self, arena, n, cnt, dt, name):
        self.bufs = [(arena.alloc(n, dt), Res("%s%d" % (name, i))) for i in range(cnt)]
        self.i = 0

    def next(self):
        b = self.bufs[self.i % len(self.bufs)]
        self.i += 1
        return b


def r3(ap, a):
    return ap.rearrange("p (a b) -> p a b", a=a)


def t5_bucket_np(rel):
    nb = 16
    max_exact = 8
    ret = np.where(rel > 0, nb, 0)
    n = np.abs(rel)
    nf = np.maximum(n, 1).astype(np.float32)
    large = max_exact + (np.log(nf / np.float32(max_exact)) / np.float32(math.log(128 / max_exact))
                         * np.float32(nb - max_exact)).astype(np.int32)
    large = np.minimum(large, nb - 1)
    return ret + np.where(n < max_exact, n, large)


def build_program(stage=99):
    nc = bass.Bass("TRN2", target_bir_lowering=False)
    dram = {}

    def IN(name, shape):
        dram[name] = nc.dram_tensor(name, list(shape), F32, kind="ExternalInput").ap()
        return dram[name]

    def OUT(name, shape):
        dram[name] = nc.dram_tensor(name, list(shape), F32, kind="ExternalOutput").ap()
        return dram[name]

    xin = IN("xin", [NCOL, D])
    cache_k = IN("cache_k", [PAST, 256])
    cache_v = IN("cache_v", [PAST, 256])
    cache_ki = IN("cache_ki", [PAST, 64])
    state_ssm = IN("state_ssm", [1024, 128])
    state_conv = IN("state_conv", [3, 2048])
    w_in = IN("w_in", [D, IN_DIM])
    w_br_ssd = IN("w_br_ssd", [D, D])
    w_br_att = IN("w_br_att", [D, D])
    w_out = IN("w_out", [D, D])
    w_gate = IN("w_gate", [D, DFF])
    w_up = IN("w_up", [D, DFF])
    w_down = IN("w_down", [DFF, D])
    vecs = IN("vecs", [128, 64])
    rows = IN("rows", [1, 512])
    convw = IN("convw", [128, 64])
    convb = IN("convb", [128, 16])
    ident_in = IN("ident", [128, 128])
    tri_in = IN("tri", [128, 3 * 128])
    nm_in = IN("nm", [128, 128])
    bt_in = IN("bt", [128, 2 * 8 * 128])
    pow2_in = IN("pow2", [1, 32])

    y_p = OUT("y_p", [SEQ, D])
    y_s = OUT("y_s", [TS, D])
    k_p = OUT("k_p", [TP, 256])
    v_p = OUT("v_p", [TP, 256])
    ki_p = OUT("ki_p", [TP, 64])
    ssm_p = OUT("ssm_p", [1024, 128])
    conv_p = OUT("conv_p", [3, 2048])
    k_s = OUT("k_s", [TS, 256])
    v_s = OUT("v_s", [TS, 256])
    ki_s = OUT("ki_s", [TS, 64])
    ssm_s = OUT("ssm_s", [1024, 128])
    conv_s = OUT("conv_s", [3, 2048])
    out_names = ("y_p", "y_s", "k_p", "v_p", "ki_p", "ssm_p", "conv_p", "k_s", "v_s", "ki_s", "ssm_s", "conv_s")
    out_res = {n: Res("o_" + n) for n in out_names}

    tiles = [("M", 0, 16)] + [("T%d" % j, 16 + 128 * (j - 1), 128) for j in range(1, 17)] + [("S", TP, TS)]
    NT = len(tiles)
    TS_IDX = NT - 1

    st = ExitStack()
    with st:
        P = Prog(nc)
        ARENA_BYTES = 212800
        arena_t = st.enter_context(nc.sbuf_tensor("arena", [128, ARENA_BYTES // 2], BF16))
        A = Arena(arena_t, ARENA_BYTES)
        psum_t = st.enter_context(nc.psum_tensor("psum", [128, 4096], F32))
        R_ps = [Res("ps%d" % b, excl=True) for b in range(8)]

        def psf(b, w=512, o=0):
            return psum_t[:, b * 512 + o: b * 512 + o + w]

        def psb(b, w=1024, o=0):
            return psum_t[:, b * 512:(b + 1) * 512].bitcast(BF16)[:, o:o + w]

        def MM(out, lhsT, rhs, start, stop, reads, writes):
            P.op("pe", lambda e: e.matmul(out, lhsT=lhsT, rhs=rhs, start=start, stop=stop), reads, writes)

        def TR(out, in_, ident, reads, writes):
            P.op("pe", lambda e: e.transpose(out, in_, ident), reads, writes)

        def ACT(out, in_, func, reads, writes, bias=None, scale=None, accum_out=None):
            kw = {}
            if bias is not None:
                kw["bias"] = bias
            if scale is not None:
                kw["scale"] = scale
            if accum_out is not None:
                kw["accum_out"] = accum_out
            P.op("act", lambda e: e.activation(out=out, in_=in_, func=func, **kw), reads, writes)

        def TS_(eng, out, in0, s1, s2, op0, op1, reads, writes, accum_out=None):
            if op1 is None:
                P.op(eng, lambda e: e.tensor_scalar(out=out, in0=in0, scalar1=s1, scalar2=None, op0=op0), reads, writes)
            elif accum_out is None:
                P.op(eng, lambda e: e.tensor_scalar(out=out, in0=in0, scalar1=s1, scalar2=s2, op0=op0, op1=op1),
                     reads, writes)
            else:
                P.op(eng, lambda e: e.tensor_scalar(out=out, in0=in0, scalar1=s1, scalar2=s2, op0=op0, op1=op1,
                                                    accum_out=accum_out), reads, writes)

        def TT(eng, out, in0, in1, op, reads, writes):
            P.op(eng, lambda e: e.tensor_tensor(out=out, in0=in0, in1=in1, op=op), reads, writes)

        def STT(eng, out, in0, scalar, in1, op0, op1, reads, writes):
            P.op(eng, lambda e: e.scalar_tensor_tensor(out=out, in0=in0, scalar=scalar, in1=in1, op0=op0, op1=op1),
                 reads, writes)

        def CP(eng, out, in_, reads, writes):
            if eng == "act":
                P.op("act", lambda e: e.activation(out=out, in_=in_, func=AF.Copy), reads, writes)
            else:
                P.op(eng, lambda e: e.tensor_copy(out=out, in_=in_), reads, writes)

        def MEMSET(eng, ap, val, writes):
            P.op(eng, lambda e: e.memset(ap, val), (), writes)

        def RECIP(out, in_, reads, writes):
            P.op("dve", lambda e: e.reciprocal(out=out, in_=in_), reads, writes)

        def wsrc(w, c0, c1):
            return w.rearrange("(kc p) c -> p kc c", p=128)[:, :, c0:c1]

        def load_w(dst3, R, w, c0, c1, step=1024):
            first = True
            for a in range(c0, c1, step):
                b = min(c1, a + step)
                if first:
                    P.dma("pool", dst3[:, :, a - c0:b - c0], wsrc(w, a, b), writes=[R], sem_res=R)
                    first = False
                else:
                    P.dma_more("pool", dst3[:, :, a - c0:b - c0], wsrc(w, a, b), R)

        ident_f = A.alloc(128, F32); R_identf = Res("identf")
        ident_b = A.alloc(128, BF16); R_identb = Res("identb")
        vec = A.alloc(64, F32); R_vec = Res("vec")
        rowb = A.alloc(512, F32); R_rowb = Res("rowb")
        mhalf = A.alloc(8, F32); R_mhalf = Res("mhalf")
        P.dma("sp", ident_f, ident_in, writes=[R_identf])
        P.dma("pool", ident_b, ident_in, writes=[R_identb])
        P.dma("sp", vec, vecs, writes=[R_vec])
        P.dma("sp", rowb, rows.partition_broadcast(128), writes=[R_rowb])
        MEMSET("pool", mhalf, -0.5, [R_mhalf])
        V_N1W, V_N2W, V_SNW, V_QNW = 0, 8, 16, 24
        RB_KNW, RB_KINW, RB_DTB, RB_ALOG, RB_DSK, RB_CB = 0, 128, 192, 208, 224, 240

        MG_BYTES = KC * NCOL * 2
        A.nbytes = ARENA_BYTES - MG_BYTES
        mergedT = r3(arena_t[:, (ARENA_BYTES - MG_BYTES) // 2: ARENA_BYTES // 2], KC)
        R_mg = [Res("mg%d" % t) for t in range(NT)]
        junk_act = A.alloc(D, BF16); R_junk_act = Res("junk_act")
        qnw_s = A.alloc(1, F32); R_qnws = Res("qnws")
        TS_("dve", qnw_s, vec[:, V_QNW:V_QNW + 1], 128.0 ** -0.5, None, ALU.mult, None, [R_vec], [R_qnws])
        yattT_s = r3(A.alloc(KC * TS, BF16), KC)
        persist_mark = A.mark()

        mm_banks = [0, 1, 2, 7]
        tr_banks = [3, 4]
        o_banks = [5, 6]
        cnt = {"mm": 0, "tr": 0, "o": 0, "at": 0, "fr": 0}
        bank_pools = {"mm": mm_banks, "tr": tr_banks, "o": o_banks, "at": [0, 1]}

        def nb(kind):
            pool = bank_pools[kind]
            b = pool[cnt[kind] % len(pool)]
            cnt[kind] += 1
            return b

        def rsqrt_small(out, in_, scale, n, w, reads, writes):
            TS_("pool", out, in_, scale, EPS, ALU.mult, ALU.add, reads, writes)
            TT("pool", out, out, mhalf[0:n, 0:w], ALU.pow, list(writes) + [R_mhalf], writes)

        class HN:
            def __init__(self, vcol, nxt=2, nhn=2):
                self.xt = Ring(A, D, nxt, F32, "xt") if nxt else None
                self.xn = Ring(A, D, 1, BF16, "xn")
                self.hn = Ring(A, KC * 128, nhn, BF16, "hn") if nhn else None
                self.sm = Ring(A, 4, 2, F32, "smh")
                self.vcol = vcol
                self.pool = "tr"

            def tile_gen(self, t, src=None, R_src=None, dst=None, R_dst=None, pool=None):
                pool = self.pool if pool is None else pool
                tname, col0, n = tiles[t]
                if src is None:
                    xt, R_xt = self.xt.next()
                    P.dma("sp", xt[0:n, :], xin[col0:col0 + n, :], writes=[R_xt])
                else:
                    xt, R_xt = src, R_src
                sm, R_sm = self.sm.next()
                if dst is None:
                    hn2, R_hn = self.hn.next()
                    hn = r3(hn2, KC)
                else:
                    hn, R_hn = dst, R_dst
                ACT(junk_act[0:n, :], xt[0:n, :], AF.Square, [R_xt], [R_junk_act, R_sm], accum_out=sm[0:n, 0:1])
                yield
                TS_("pool", sm[0:n, 1:2], sm[0:n, 0:1], 1.0 / D, EPS, ALU.mult, ALU.add, [R_sm], [R_sm])
                yield
                TT("pool", sm[0:n, 1:2], sm[0:n, 1:2], mhalf[0:n, 0:1], ALU.pow, [R_sm, R_mhalf], [R_sm])
                yield
                xn, R_xn = self.xn.next()
                if getattr(self, "mul_on_act", False):
                    ACT(xn[0:n, :], xt[0:n, :], AF.Identity, [R_xt, R_sm], [R_xn], scale=sm[0:n, 1:2])
                else:
                    TS_("dve", xn[0:n, :], xt[0:n, :], sm[0:n, 1:2], None, ALU.mult, None, [R_xt, R_sm], [R_xn])
                b = nb(pool)
                pt = r3(psb(b), 8)
                for kc in range(KC):
                    TR(pt[:, kc, 0:n], xn[0:n, kc * 128:(kc + 1) * 128], ident_b[0:n, 0:n], [R_xn, R_identb], [R_ps[b]])
                yield
                TT("dve", hn[:, :, 0:n], pt[:, :, 0:n],
                   vec[:, self.vcol:self.vcol + 8].unsqueeze(2).to_broadcast([128, 8, n]), ALU.mult,
                   [R_ps[b], R_vec], [R_hn])
                return hn, R_hn

            def tile(self, t, src=None, R_src=None, dst=None, R_dst=None):
                g = self.tile_gen(t, src, R_src, dst, R_dst)
                try:
                    while True:
                        next(g)
                except StopIteration as e:
                    return e.value

        def gate_branch(t, hn, R_hn, yT, R_yT, wbr, R_wbr, wg, R_wg, first, sg_ring, tmpm_ring):
            tname, col0, n = tiles[t]
            bbs = [nb("mm"), nb("mm")]
            bgs = [nb("mm"), nb("mm")]
            for half in range(2):
                pbr = r3(psf(bbs[half]), 4)
                for cc in range(4):
                    c = half * 4 + cc
                    for kc in range(KC):
                        MM(pbr[:, cc, 0:n], wbr[:, kc, c * 128:(c + 1) * 128], yT[:, kc, 0:n], kc == 0, kc == KC - 1,
                           [R_wbr, R_yT], [R_ps[bbs[half]]])
            for half in range(2):
                pg = r3(psf(bgs[half]), 4)
                for cc in range(4):
                    c = half * 4 + cc
                    for kc in range(KC):
                        MM(pg[:, cc, 0:n], wg[:, kc, c * 128:(c + 1) * 128], hn[:, kc, 0:n], kc == 0, kc == KC - 1,
                           [R_wg, R_hn], [R_ps[bgs[half]]])
            for half in range(2):
                pbr = r3(psf(bbs[half]), 4)
                pg = r3(psf(bgs[half]), 4)
                sg2, R_sg = sg_ring.next()
                sg = r3(sg2, 4)
                ACT(sg[:, :, 0:n], pg[:, :, 0:n], AF.Sigmoid, [R_ps[bgs[half]]], [R_sg])
                dst = mergedT[:, half * 4:(half + 1) * 4, col0:col0 + n]
                if first:
                    TT("dve", dst, pbr[:, :, 0:n], sg[:, :, 0:n], ALU.mult, [R_ps[bbs[half]], R_sg], [R_mg[t]])
                else:
                    tm2, R_tm = tmpm_ring.next()
                    tm = r3(tm2, 4)
                    TT("dve", tm[:, :, 0:n], pbr[:, :, 0:n], sg[:, :, 0:n], ALU.mult, [R_ps[bbs[half]], R_sg], [R_tm])
                    TT("pool", dst, dst, tm[:, :, 0:n], ALU.add, [R_tm, R_mg[t]], [R_mg[t]])

        if stage >= 2:
            hnS = HN(V_N1W)
            wS = r3(A.alloc(KC * 3088, BF16), KC); R_wS = Res("wS")
            R_wSz = Res("wSz")
            load_w(wS[:, :, C_XBC:3088], R_wS, w_in, C_XBC, 3088)
            load_w(wS[:, :, 0:C_XBC], R_wSz, w_in, 0, C_XBC)
            cw = A.alloc(64, F32); R_cw = Res("cw")
            P.dma("sp", cw, convw, writes=[R_cw])
            cb = A.alloc(16, F32); R_cb = Res("cb")
            P.dma("sp", cb, convb, writes=[R_cb])
            tri = r3(A.alloc(3 * 128, F32), 3); R_tri = Res("tri")
            P.dma("sp", tri, tri_in.rearrange("p (a b) -> p a b", a=3), writes=[R_tri])
            U_f, SL_f, ONES_f = tri[:, 0, :], tri[:, 1, :], tri[:, 2, :]
            NM = r3(A.alloc(4 * 128, BF16), 4); R_NM = Res("NM")
            for h in range(4):
                if h == 0:
                    P.dma("pool", NM[:, h, :], nm_in, writes=[R_NM], sem_res=R_NM)
                else:
                    P.dma_more("pool", NM[:, h, :], nm_in, R_NM)
            dconv = r3(A.alloc(64 * 128, BF16), 64); R_dconv = Res("dconv"); R_dconv2 = Res("dconv2")
            for i in range(64):
                if i < 32:
                    TS_("dve", dconv[:, i, :], ident_f, cw[:, i:i + 1], None, ALU.mult, None, [R_identf, R_cw], [R_dconv])
                else:
                    ACT(dconv[:, i, :], ident_f, AF.Copy, [R_identf, R_cw], [R_dconv2], scale=cw[:, i:i + 1])
            aneg = A.alloc(16, F32); R_aneg = Res("aneg")
            ACT(aneg, rowb[:, RB_ALOG:RB_ALOG + 16], AF.Exp, [R_rowb], [R_aneg])
            TS_("dve", aneg, aneg, -1.0, None, ALU.mult, None, [R_aneg], [R_aneg])
            ST = A.alloc(1024, F32); R_ST = Res("ST")
            Sb = A.alloc(1024, BF16); R_Sb = Res("Sb")
            hist = r3(A.alloc(16 * 3, BF16), 16); R_hist = Res("hist")
            Rm = A.alloc(2048, F32); R_Rm = Res("Rm")
            cvT = A.alloc(48, F32); R_cvT = Res("cvT")
            cvio = A.alloc(128, F32); R_cvio = Res("cvio")
            sso = r3(Rm[:, 0:1024], 8)
            stin = r3(Rm[:, 1024:2048], 8)
            xraw_ring = Ring(A, 16 * 131, 2, BF16, "xraw")
            xact_ring = Ring(A, 16 * 128, 2, BF16, "xact")
            xtok_ring = Ring(A, 1024, 2, BF16, "xtok")
            btok_ring = Ring(A, 512, 2, BF16, "btok")
            xdt_ring = Ring(A, 1024, 2, BF16, "xdt")
            xD_ring = Ring(A, 1024, 2, BF16, "xD")
            xdec_ring = Ring(A, 1024, 2, BF16, "xdec")
            Lt_ring = Ring(A, 2048, 2, BF16, "Lt")
            t1_ring = Ring(A, 1024, 2, F32, "t1")
            yn_ring = Ring(A, 1024, 2, BF16, "yn")
            szh_ring = Ring(A, 512, 2, F32, "szh")
            smS_ring = Ring(A, 160, 2, F32, "smS")
            etb_ring = Ring(A, 16, 2, F32, "etb")
            for par in range(2):
                bank_pools["s_mm%d" % par] = [0, 1] if par == 0 else [4, 5]
                bank_pools["s_tr%d" % par] = [2] if par == 0 else [6]
                bank_pools["s_o%d" % par] = [3] if par == 0 else [7]
                for k_ in ("s_mm", "s_tr", "s_o"):
                    cnt["%s%d" % (k_, par)] = 0
            flags = {"hist": -1, "state": -1}

            def ssd_gen(t, first, last, seq, par, prev_t, S):
                mmp, trp, op_ = "s_mm%d" % par, "s_tr%d" % par, "s_o%d" % par
                tname, col0, n = tiles[t]
                hn, R_hn = yield from hnS.tile_gen(t, pool=trp)
                yield
                sm, R_sm = smS_ring.next()
                xraw2, R_xraw = xraw_ring.next(); xraw = r3(xraw2, 16)
                xact2, R_xact = xact_ring.next(); xact = r3(xact2, 16)
                xtok, R_xtok = xtok_ring.next()
                btok, R_btok = btok_ring.next()
                xdt, R_xdt = xdt_ring.next()
                xD, R_xD = xD_ring.next()
                xdec, R_xdec = xdec_ring.next()
                Lt, R_Lt = Lt_ring.next()
                t1, R_t1 = t1_ring.next()
                yn, R_yn = yn_ring.next()
                etb, R_etb = etb_ring.next()
                if first and seq == "p":
                    MEMSET("pool", xraw[:, :, 0:3], 0.0, [R_xraw])
                elif first and seq == "s":
                    P.dma("sp", S.cvio[0:48, :], state_conv.rearrange("t (c f) -> (t c) f", f=128), writes=[S.R_cvio])
                    bt = nb(op_)
                    TR(psf(bt)[:, 0:48], S.cvio[0:48, :], ident_f[0:48, 0:48], [S.R_cvio, R_identf], [R_ps[bt]])
                    CP("dve", xraw[:, :, 0:3], psf(bt)[:, 0:48].rearrange("p (t c) -> p c t", t=3), [R_ps[bt]], [R_xraw])
                else:
                    while S.flags["hist"] < prev_t:
                        yield
                    CP("pool", xraw[:, :, 0:3], S.hist[:, :, :], [S.R_hist], [R_xraw])
                for hf in range(2):
                    banks = [nb(mmp), nb(mmp)]
                    for cc in range(8):
                        c = hf * 8 + cc
                        b = banks[cc // 4]
                        for kc in range(KC):
                            MM(r3(psf(b), 4)[:, cc % 4, 0:n], wS[:, kc, C_XBC + c * 128: C_XBC + (c + 1) * 128],
                               hn[:, kc, 0:n], kc == 0, kc == KC - 1, [R_wS, R_hn], [R_ps[b]])
                    yield
                    for q2 in range(2):
                        q = hf * 2 + q2
                        CP("act" if q2 == 0 else "dve", xraw[:, q * 4:(q + 1) * 4, 3:3 + n],
                           r3(psf(banks[q2]), 4)[:, :, 0:n], [R_ps[banks[q2]]], [R_xraw])
                        if last:
                            CP("dve", S.cvT.rearrange("p (t c) -> p c t", t=3)[:, q * 4:(q + 1) * 4, :],
                               r3(psf(banks[q2]), 4)[:, :, n - 3:n], [R_ps[banks[q2]]], [S.R_cvT])
                    yield
                CP("pool", S.hist[:, :, :], xraw[:, :, n:n + 3], [R_xraw], [S.R_hist])
                S.flags["hist"] = t
                if last:
                    bt = nb(op_)
                    TR(psf(bt)[0:48, 0:128], S.cvT[:, 0:48], ident_f, [S.R_cvT, R_identf], [R_ps[bt]])
                    CP("act", S.cvio[0:48, :], psf(bt)[0:48, 0:128], [R_ps[bt]], [S.R_cvio])
                    oname = "conv_p" if seq == "p" else "conv_s"
                    P.dma("sp", dram[oname].rearrange("t (c f) -> (t c) f", f=128), S.cvio[0:48, :], reads=[S.R_cvio],
                          writes=[out_res[oname]], sem_res=S.R_cvio)
                yield
                for hf in range(2):
                    banks = [nb(mmp), nb(mmp)]
                    for cc in range(8):
                        c = hf * 8 + cc
                        b = banks[cc // 4]
                        o = r3(psf(b), 4)[:, cc % 4, 0:n]
                        for i in range(4):
                            MM(o, dconv[:, i * 16 + c, :], xraw[:, c, i:i + n], i == 0, i == 3,
                               [R_dconv if i < 2 else R_dconv2, R_xraw], [R_ps[b]])
                    yield
                    for cc in range(8):
                        c = hf * 8 + cc
                        b = banks[cc // 4]
                        o = r3(psf(b), 4)[:, cc % 4, 0:n]
                        ACT(xact[:, c, 0:n], o, AF.Silu, [R_ps[b], R_cb], [R_xact], bias=cb[:, c:c + 1])
                    yield
                bx = nb(trp)
                for c in range(8):
                    TR(psb(bx)[0:n, c * 128:(c + 1) * 128], xact[:, c, 0:n], ident_b, [R_xact, R_identb], [R_ps[bx]])
                yield
                CP("act", xtok[0:n, :], psb(bx)[0:n, :], [R_ps[bx]], [R_xtok])
                yield
                bB = nb(trp)
                for c in range(4):
                    TR(psb(bB)[0:n, c * 128:(c + 1) * 128], xact[:, 8 + c, 0:n], ident_b, [R_xact, R_identb], [R_ps[bB]])
                yield
                CP("dve", btok[0:n, :], psb(bB)[0:n, 0:512], [R_ps[bB]], [R_btok])
                yield
                bd = nb(op_)
                for kc in range(KC):
                    MM(psf(bd)[0:n, 0:16], hn[:, kc, 0:n], wS[:, kc, C_DT:C_DT + 16], kc == 0, kc == KC - 1,
                       [R_hn, R_wS], [R_ps[bd]])
                yield
                TT("dve", sm[0:n, 0:16], psf(bd)[0:n, 0:16], rowb[0:n, RB_DTB:RB_DTB + 16], ALU.add,
                   [R_ps[bd], R_rowb], [R_sm])
                yield
                STT("dve", sm[0:n, 16:32], sm[0:n, 0:16], -1.0, sm[0:n, 0:16], ALU.mult, ALU.max, [R_sm], [R_sm])
                yield
                ACT(sm[0:n, 16:32], sm[0:n, 16:32], AF.Exp, [R_sm], [R_sm], scale=-1.0)
                yield
                ACT(sm[0:n, 16:32], sm[0:n, 16:32], AF.Ln, [R_sm], [R_sm], bias=1.0)
                yield
                STT("dve", sm[0:n, 32:48], sm[0:n, 0:16], 0.0, sm[0:n, 16:32], ALU.max, ALU.add, [R_sm], [R_sm])
                yield
                TT("dve", sm[0:n, 48:64], sm[0:n, 32:48], aneg[0:n, :], ALU.mult, [R_sm, R_aneg], [R_sm])
                yield
                dt_ = sm[0:n, 32:48]
                dtA = sm[0:n, 48:64]
                MM(psf(bd)[0:n, 16:32], U_f[0:n, 0:n], dtA, True, True, [R_tri, R_sm], [R_ps[bd]])
                MM(psf(bd)[:, 32:48], ONES_f[0:n, :], dtA, True, True, [R_tri, R_sm], [R_ps[bd]])
                yield
                CP("dve", sm[0:n, 64:80], psf(bd)[0:n, 16:32], [R_ps[bd]], [R_sm])
                ACT(sm[0:n, 80:96], psf(bd)[0:n, 16:32], AF.Exp, [R_ps[bd]], [R_sm])
                yield
                TT("dve", sm[0:n, 96:112], psf(bd)[0:n, 32:48], sm[0:n, 64:80], ALU.subtract, [R_ps[bd], R_sm], [R_sm])
                ACT(etb, psf(bd)[:, 32:48], AF.Exp, [R_ps[bd]], [R_etb])
                yield
                ACT(sm[0:n, 96:112], sm[0:n, 96:112], AF.Exp, [R_sm], [R_sm])
                yield
                ea = sm[0:n, 80:96]
                dec = sm[0:n, 96:112]
                xtok3 = r3(xtok, 16)
                TT("dve", r3(xdt, 16)[0:n], xtok3[0:n], dt_.unsqueeze(2).to_broadcast([n, 16, 64]), ALU.mult,
                   [R_xtok, R_sm], [R_xdt])
                TT("pool", r3(xD, 16)[0:n], xtok3[0:n], rowb[0:n, RB_DSK:RB_DSK + 16].unsqueeze(2).to_broadcast([n, 16, 64]),
                   ALU.mult, [R_xtok, R_rowb], [R_xD])
                yield
                TT("pool", r3(xdec, 16)[0:n], r3(xdt, 16)[0:n], dec.unsqueeze(2).to_broadcast([n, 16, 64]), ALU.mult,
                   [R_xdt, R_sm], [R_xdec])
                yield
                if not first:
                    while S.flags["state"] < prev_t:
                        yield
                byo = [nb(mmp), nb(mmp)]
                for g in range(4):
                    b = byo[g // 2]
                    MM(psf(b)[0:n, (g % 2) * 256:(g % 2 + 1) * 256], xact[:, 12 + g, 0:n], S.Sb[:, g * 256:(g + 1) * 256],
                       True, True, [R_xact, S.R_Sb], [R_ps[b]])
                yield
                for q in range(2):
                    TT("dve", r3(t1, 16)[0:n, q * 8:(q + 1) * 8, :], r3(psf(byo[q]), 8)[0:n],
                       ea[:, q * 8:(q + 1) * 8].unsqueeze(2).to_broadcast([n, 8, 64]), ALU.mult,
                       [R_ps[byo[q]], R_sm], [R_t1])
                yield
                bs = [nb(mmp), nb(mmp)]
                for g in range(4):
                    b = bs[g // 2]
                    MM(psf(b)[:, (g % 2) * 256:(g % 2 + 1) * 256], btok[0:n, g * 128:(g + 1) * 128],
                       xdec[0:n, g * 256:(g + 1) * 256], True, True, [R_btok, R_xdec], [R_ps[b]])
                TT("pool", r3(S.ST, 16), r3(S.ST, 16), etb.unsqueeze(2).to_broadcast([128, 16, 64]), ALU.mult,
                   [S.R_ST, R_etb], [S.R_ST])
                yield
                for q in range(2):
                    TT("dve", S.ST[:, q * 512:(q + 1) * 512], S.ST[:, q * 512:(q + 1) * 512], psf(bs[q]), ALU.add,
                       [S.R_ST, R_ps[bs[q]]], [S.R_ST])
                yield
                CP("pool", S.Sb, S.ST, [S.R_ST], [S.R_Sb])
                S.flags["state"] = t
                yield
                if last:
                    for q in range(2):
                        b = nb(mmp)
                        for cc in range(4):
                            c = q * 4 + cc
                            TR(psf(b)[:, cc * 128:(cc + 1) * 128], S.ST[:, c * 128:(c + 1) * 128], ident_f,
                               [S.R_ST, R_identf], [R_ps[b]])
                        CP("act", S.sso[:, q * 4:(q + 1) * 4, :], r3(psf(b), 4), [R_ps[b]], [S.R_sso])
                    oname = "ssm_p" if seq == "p" else "ssm_s"
                    P.dma("sp", dram[oname].rearrange("(c p) n -> p c n", p=128), S.sso, reads=[S.R_sso],
                          writes=[out_res[oname]], sem_res=S.R_sso)
                    yield
                Rm3 = r3(Rm, 16)
                TT("dve", Rm3[0:n, :, 0:n], U_f[0:n, 0:n].unsqueeze(1).to_broadcast([n, 16, n]),
                   dtA.unsqueeze(2).to_broadcast([n, 16, n]), ALU.mult, [R_tri, R_sm], [R_Rm])
                Lt3 = r3(Lt, 16)
                for q in range(4):
                    b = nb(mmp)
                    o = r3(psf(b), 4)[0:n, :, 0:n]
                    if n == 128:
                        MM(o, SL_f[0:n, 0:n], Rm3[0:n, q * 4:(q + 1) * 4, 0:n], True, False, [R_tri, R_Rm], [R_ps[b]])
                        MM(o, ident_b[0:n, 0:n], NM[0:n, :, 0:n], False, True, [R_identb, R_NM], [R_ps[b]])
                    else:
                        for r_ in range(4):
                            MM(o[:, r_, :], SL_f[0:n, 0:n], Rm3[0:n, q * 4 + r_, 0:n], True, False, [R_tri, R_Rm],
                               [R_ps[b]])
                            MM(o[:, r_, :], ident_b[0:n, 0:n], NM[0:n, r_, 0:n], False, True, [R_identb, R_NM],
                               [R_ps[b]])
                    ACT(Lt3[0:n, q * 4:(q + 1) * 4, 0:n], o, AF.Exp, [R_ps[b]], [R_Lt])
                yield
                bc = nb(op_)
                pcb = r3(psf(bc), 4)
                for g in range(4):
                    MM(pcb[0:n, g, 0:n], xact[:, 8 + g, 0:n], xact[:, 12 + g, 0:n], True, True, [R_xact], [R_ps[bc]])
                yield
                Lt4 = Lt.rearrange("p (g r i) -> p g r i", g=4, r=4)
                TT("dve", Lt4[0:n, :, :, 0:n], Lt4[0:n, :, :, 0:n],
                   pcb[0:n, :, 0:n].unsqueeze(2).to_broadcast([n, 4, 4, n]), ALU.mult, [R_Lt, R_ps[bc]], [R_Lt])
                yield
                byd = [nb(mmp), nb(mmp)]
                for h in range(16):
                    b = byd[h // 8]
                    o = psf(b)[0:n, (h % 8) * 64:(h % 8 + 1) * 64]
                    MM(o, Lt3[0:n, h, 0:n], xdt[0:n, h * 64:(h + 1) * 64], True, False, [R_Lt, R_xdt], [R_ps[b]])
                    MM(o, ident_b[0:n, 0:n], xD[0:n, h * 64:(h + 1) * 64], False, True, [R_identb, R_xD], [R_ps[b]])
                yield
                for q in range(2):
                    TT("dve", t1[0:n, q * 512:(q + 1) * 512], psf(byd[q])[0:n, :], t1[0:n, q * 512:(q + 1) * 512], ALU.add,
                       [R_ps[byd[q]], R_t1], [R_t1])
                yield
                for q in range(2):
                    bz = nb(mmp)
                    for kc in range(KC):
                        MM(psf(bz)[0:n, :], hn[:, kc, 0:n], wS[:, kc, C_Z + q * 512:C_Z + (q + 1) * 512], kc == 0,
                           kc == KC - 1, [R_hn, R_wSz], [R_ps[bz]])
                    szh, R_szh = szh_ring.next()
                    ACT(szh[0:n, :], psf(bz)[0:n, :], AF.Silu, [R_ps[bz]], [R_szh])
                    yield
                    TT("pool", t1[0:n, q * 512:(q + 1) * 512], t1[0:n, q * 512:(q + 1) * 512], szh[0:n, :], ALU.mult,
                       [R_t1, R_szh], [R_t1])
                    yield
                ACT(junk_act[0:n, :], t1[0:n, :], AF.Square, [R_t1], [R_junk_act, R_sm], accum_out=sm[0:n, 112:113])
                yield
                TS_("pool", sm[0:n, 113:114], sm[0:n, 112:113], 1.0 / 1024, EPS, ALU.mult, ALU.add, [R_sm], [R_sm])
                yield
                TT("pool", sm[0:n, 113:114], sm[0:n, 113:114], mhalf[0:n, 0:1], ALU.pow, [R_sm, R_mhalf], [R_sm])
                yield
                TS_("dve", yn[0:n, :], t1[0:n, :], sm[0:n, 113:114], None, ALU.mult, None, [R_t1, R_sm], [R_yn])
                yield
                bt = nb(trp)
                pt = r3(psb(bt), 8)
                for kc in range(KC):
                    TR(pt[:, kc, 0:n], yn[0:n, kc * 128:(kc + 1) * 128], ident_b[0:n, 0:n], [R_yn, R_identb], [R_ps[bt]])
                yield
                TT("dve", mergedT[:, :, col0:col0 + n], pt[:, :, 0:n],
                   vec[:, V_SNW:V_SNW + 8].unsqueeze(2).to_broadcast([128, 8, n]),
                   ALU.mult, [R_ps[bt], R_vec], [R_mg[t]])
                yield

            def run_two(gens, lag):
                gens = list(gens)
                active = []
                nxt = 0
                while nxt < len(gens) or active:
                    if not active or (len(active) == 1 and nxt < len(gens) and active[0][1] >= lag):
                        if nxt < len(gens):
                            active.append([gens[nxt], 0])
                            nxt += 1
                    for a_ in list(active):
                        try:
                            next(a_[0])
                            a_[1] += 1
                        except StopIteration:
                            active.remove(a_)

            class SeqState:
                pass
            Sp = SeqState()
            Sp.ST, Sp.R_ST, Sp.Sb, Sp.R_Sb, Sp.hist, Sp.R_hist = ST, R_ST, Sb, R_Sb, hist, R_hist
            Sp.cvT, Sp.R_cvT, Sp.cvio, Sp.R_cvio, Sp.sso, Sp.R_sso = cvT, R_cvT, cvio, R_cvio, sso, R_Rm
            Sp.flags = {"hist": -1, "state": -1}
            Ss = SeqState()
            Ss.ST = A.alloc(1024, F32); Ss.R_ST = Res("STs")
            Ss.Sb = A.alloc(1024, BF16); Ss.R_Sb = Res("Sbs")
            Ss.hist = r3(A.alloc(16 * 3, BF16), 16); Ss.R_hist = Res("hists")
            Ss.cvT = A.alloc(48, F32); Ss.R_cvT = Res("cvTs")
            Ss.cvio = A.alloc(128, F32); Ss.R_cvio = Res("cvios")
            Ss.sso = r3(A.alloc(1024, F32), 8); Ss.R_sso = Res("ssos")
            Ss.flags = {"hist": -1, "state": -1}
            P.dma("sp", stin, state_ssm.rearrange("(c p) n -> p c n", p=128), reads=[], writes=[R_Rm])
            for q in range(2):
                b = nb("mm")
                for cc in range(4):
                    c = q * 4 + cc
                    TR(psf(b)[:, cc * 128:(cc + 1) * 128], stin[:, c, :], ident_f, [R_Rm, R_identf], [R_ps[b]])
                CP("dve", Ss.ST[:, q * 512:(q + 1) * 512], psf(b), [R_ps[b]], [Ss.R_ST])
            CP("pool", Ss.Sb, Ss.ST, [Ss.R_ST], [Ss.R_Sb])
            MEMSET("dve", ST, 0.0, [R_ST])
            MEMSET("pool", Sb, 0.0, [R_Sb])
            gens = [ssd_gen(t, t == 0, t == NT - 2, "p", t % 2, t - 1, Sp) for t in range(0, NT - 1)]
            gens.append(ssd_gen(TS_IDX, True, True, "s", TS_IDX % 2, None, Ss))
            run_two(gens, 24)
            hnS.pool = "tr"
            print("pass S1 arena peak", A.peak)
            A.release(persist_mark)
            P.barrier()

        if stage >= 3:
            XBYTES = 33792
            regX = A.alloc(XBYTES // 2, BF16)
            afterX_mark = A.mark()
            yattT = r3(regX[:, 0:KC * TP], KC)
            hnA = HN(V_N1W, nxt=1, nhn=2)
            hnA.mul_on_act = True
            R_yattT = [Res("yat%d" % t) for t in range(NT)]
            NWA = C_GS - C_Q
            wA_off = A.mark()
            wA_raw = A.alloc(KC * NWA, BF16)
            wA_end = A.mark()
            wA = r3(wA_raw, KC); R_wA = Res("wA")
            WQ, WK, WV, WQI, WKI, WWI = 0, C_K - C_Q, C_V - C_Q, C_QI - C_Q, C_KI - C_Q, C_WI - C_Q
            bt2 = A.alloc(2 * 8 * 128, F32); R_bt = Res("bt")
            P.dma("sp", bt2, bt_in, writes=[R_bt])
            bt = bt2.rearrange("p (w h i) -> p w h i", w=2, h=8)
            BT_PREV, BT_SAME, BT_META = 0, 1, 2
            qb_ring = Ring(A, D, 1, BF16, "qb")
            qT_ring = Ring(A, 8 * 128, 3, BF16, "qT")
            qiT_ring = Ring(A, 4 * 128, 2, BF16, "qiT")
            ko_ring = Ring(A, 256, 1, F32, "ko")
            vo_ring = Ring(A, 256, 1, F32, "vo")
            kio_ring = Ring(A, 64, 2, F32, "kio")
            kb_ring = Ring(A, 256, 2, BF16, "kb")
            ki2_ring = Ring(A, 128, 2, BF16, "ki2")
            r_ring = Ring(A, 512, 2, F32, "rbuf")
            pm_ring = Ring(A, 512, 4, BF16, "pm")
            ya_ring = Ring(A, D, 1, BF16, "ya")
            smA_ring = Ring(A, 96, 3, F32, "smA")
            bis_ring = Ring(A, 64, 2, F32, "bis")
            pow2 = A.alloc(32, F32); R_pow2 = Res("pow2")
            P.dma("sp", pow2, pow2_in.partition_broadcast(128), writes=[R_pow2])
            commonA_mark = A.mark()
            bank_pools["at"] = [0, 1]
            bank_pools["fr"] = [2, 3]
            hnA.pool = "fr"

            class TileCtx:
                pass

            def front(t, B, ktile_idx, key_tiles, bias_kind, adm_fill, outs, fr):
                kT, vb, kiT2 = B["kT"], B["vb"], B["kiT2"]
                R_k = B["R_k"]
                score, R_score = B["score_ring"].next()
                bis, R_bis = bis_ring.next()
                hn, R_hn = yield from hnA.tile_gen(t, pool=fr)
                yield
                mask, R_mask = B["mask_ring"].next()
                maskT2, R_maskT = B["maskT_ring"].next()
                maskT = r3(maskT2, B["NKT"])
                tname, col0, n = tiles[t]
                kc0, nk_self = key_tiles[ktile_idx]
                L = kc0 + nk_self
                sm, R_sm = smA_ring.next()
                ctx = TileCtx()
                ctx.sm, ctx.R_sm, ctx.maskT, ctx.R_maskT = sm, R_sm, maskT, R_maskT
                qb, R_qb = qb_ring.next()
                qbanks = [nb(fr), nb(fr)]
                for half in range(2):
                    b = qbanks[half]
                    for kc in range(KC):
                        MM(psf(b)[0:n, :], hn[:, kc, 0:n], wA[:, kc, WQ + half * 512: WQ + (half + 1) * 512],
                           kc == 0, kc == KC - 1, [R_hn, R_wA], [R_ps[b]])
                yield
                for half in range(2):
                    b = qbanks[half]
                    for hh in range(4):
                        h = half * 4 + hh
                        ACT(junk_act[0:n, 0:128], psf(b)[0:n, hh * 128:(hh + 1) * 128], AF.Square, [R_ps[b]],
                            [R_junk_act, R_sm], accum_out=sm[0:n, h:h + 1])
                yield
                TS_("pool", sm[0:n, 8:16], sm[0:n, 0:8], 1.0 / 128, EPS, ALU.mult, ALU.add, [R_sm], [R_sm])
                yield
                TT("pool", sm[0:n, 8:16], sm[0:n, 8:16], mhalf[0:n, 0:8], ALU.pow, [R_sm, R_mhalf], [R_sm])
                yield
                for half in range(2):
                    b = qbanks[half]
                    TT("dve", r3(qb[0:n, half * 512:(half + 1) * 512], 4), r3(psf(b)[0:n, :], 4),
                       sm[0:n, 8 + half * 4:12 + half * 4].unsqueeze(2).to_broadcast([n, 4, 128]), ALU.mult,
                       [R_ps[b], R_sm], [R_qb])
                qT2, R_qT = qT_ring.next()
                qT = r3(qT2, 8)
                ctx.qT, ctx.R_qT = qT, R_qT
                btq = nb(fr)
                ptq = r3(psb(btq), 8)
                for h in range(8):
                    TR(ptq[:, h, 0:n], qb[0:n, h * 128:(h + 1) * 128], ident_b[0:n, 0:n], [R_qb, R_identb], [R_ps[btq]])
                bkv = nb(fr)
                for kc in range(KC):
                    MM(psf(bkv)[0:n, :], hn[:, kc, 0:n], wA[:, kc, WK:WK + 512], kc == 0, kc == KC - 1,
                       [R_hn, R_wA], [R_ps[bkv]])
                yield
                ACT(qT[:, :, 0:n], ptq[:, :, 0:n], AF.Identity, [R_ps[btq], R_qnws], [R_qT], scale=qnw_s[:, 0:1])
                bki = nb(fr)
                for kc in range(KC):
                    MM(psf(bki)[0:n, 0:72], hn[:, kc, 0:n], wA[:, kc, WKI:WKI + 72], kc == 0, kc == KC - 1,
                       [R_hn, R_wA], [R_ps[bki]])
                yield
                for g in range(2):
                    ACT(junk_act[0:n, 0:128], psf(bkv)[0:n, g * 128:(g + 1) * 128], AF.Square, [R_ps[bkv]],
                        [R_junk_act, R_sm], accum_out=sm[0:n, 16 + g:17 + g])
                ACT(junk_act[0:n, 0:64], psf(bki)[0:n, 0:64], AF.Square, [R_ps[bki]], [R_junk_act, R_sm],
                    accum_out=sm[0:n, 20:21])
                ACT(sm[0:n, 24:32], psf(bki)[0:n, 64:72], AF.Abs, [R_ps[bki]], [R_sm], scale=WI_SCALE)
                ACT(sm[0:n, 32:40], psf(bki)[0:n, 64:72], AF.Sign, [R_ps[bki]], [R_sm])
                yield
                TS_("pool", sm[0:n, 18:20], sm[0:n, 16:18], 1.0 / 128, EPS, ALU.mult, ALU.add, [R_sm], [R_sm])
                TS_("pool", sm[0:n, 21:22], sm[0:n, 20:21], 1.0 / 64, EPS, ALU.mult, ALU.add, [R_sm], [R_sm])
                yield
                TT("pool", sm[0:n, 18:22], sm[0:n, 18:22], mhalf[0:n, 0:4], ALU.pow, [R_sm, R_mhalf], [R_sm])
                yield
                ko, R_ko = ko_ring.next()
                vo, R_vo = vo_ring.next()
                b = bkv
                for g in range(2):
                    STT("dve", ko[0:n, g * 128:(g + 1) * 128], psf(b)[0:n, g * 128:(g + 1) * 128], sm[0:n, 18 + g:19 + g],
                        rowb[0:n, RB_KNW:RB_KNW + 128], ALU.mult, ALU.mult, [R_ps[b], R_sm, R_rowb], [R_ko])
                CP("act", vo[0:n, :], psf(b)[0:n, 256:512], [R_ps[b]], [R_vo])
                CP("act", vb[0:n, ktile_idx, :, 0:128], r3(psf(b)[0:n, 256:512], 2), [R_ps[b]], [R_k[ktile_idx]])
                P.dma("sp", outs["k"], ko[0:n, :], reads=[R_ko], writes=[out_res[outs["kn"]]], sem_res=R_ko)
                P.dma("sp", outs["v"], vo[0:n, :], reads=[R_vo], writes=[out_res[outs["vn"]]], sem_res=R_vo)
                b = bki
                kio, R_kio = kio_ring.next()
                STT("dve", kio[0:n, :], psf(b)[0:n, 0:64], sm[0:n, 21:22], rowb[0:n, RB_KINW:RB_KINW + 64],
                    ALU.mult, ALU.mult, [R_ps[b], R_sm, R_rowb], [R_kio])
                P.dma("sp", outs["ki"], kio[0:n, :], reads=[R_kio], writes=[out_res[outs["kin"]]], sem_res=R_kio)
                yield
                kb, R_kb = kb_ring.next()
                CP("pool", kb[0:n, :], ko[0:n, :], [R_ko], [R_kb])
                ki2, R_ki2 = ki2_ring.next()
                CP("pool", r3(ki2[0:n, :], 2), kio[0:n, :].unsqueeze(1).to_broadcast([n, 2, 64]), [R_kio], [R_ki2])
                yield
                bt_ = nb(fr)
                pt = r3(psb(bt_), 8)
                for g in range(2):
                    TR(pt[:, g, 0:n], kb[0:n, g * 128:(g + 1) * 128], ident_b[0:n, 0:n], [R_kb, R_identb], [R_ps[bt_]])
                bt2_ = nb(fr)
                TR(psb(bt2_)[:, 0:n], ki2[0:n, :], ident_b[0:n, 0:n], [R_ki2, R_identb], [R_ps[bt2_]])
                yield
                CP("act", kT[:, :, kc0:kc0 + n], pt[:, 0:2, 0:n], [R_ps[bt_]], [R_k[ktile_idx]])
                CP("act", kiT2[:, kc0:kc0 + n], psb(bt2_)[:, 0:n], [R_ps[bt2_]], [R_k[ktile_idx]])
                yield
                if L > TOPK:
                    qiT2, R_qiT = qiT_ring.next()
                    qiT = r3(qiT2, 4)
                    b = nb(fr)
                    pq = r3(psf(b), 4)
                    for m in range(4):
                        for kc in range(KC):
                            MM(pq[:, m, 0:n], wA[:, kc, WQI + m * 128: WQI + (m + 1) * 128], hn[:, kc, 0:n],
                               kc == 0, kc == KC - 1, [R_hn, R_wA], [R_ps[b]])
                    CP("act", qiT[:, :, 0:n], pq[:, :, 0:n], [R_ps[b]], [R_qiT])
                    for c0_ in range(0, L, 512):
                        c1_ = min(L, c0_ + 512)
                        wd = c1_ - c0_
                        rk = [R_k[i] for i, (kc_, nk_) in enumerate(key_tiles[:ktile_idx + 1])
                              if kc_ < c1_ and kc_ + nk_ > c0_]
                        for h in range(8):
                            m, hh = h // 2, h % 2
                            b = nb(fr)
                            MM(psf(b)[0:n, 0:wd], qiT[hh * 64:(hh + 1) * 64, m, 0:n],
                               kiT2[hh * 64:(hh + 1) * 64, c0_:c1_], True, True, [R_qiT] + rk, [R_ps[b]])
                            rb, R_rb = r_ring.next()
                            ACT(rb[0:n, 0:wd], psf(b)[0:n, 0:wd], AF.Relu, [R_ps[b], R_sm], [R_rb],
                                scale=sm[0:n, 24 + h:25 + h])
                            eng = "dve"
                            if h == 0:
                                TS_("dve", score[0:n, c0_:c1_], rb[0:n, 0:wd], sm[0:n, 32:33], None, ALU.mult, None,
                                    [R_rb, R_sm], [R_score])
                            else:
                                STT(eng, score[0:n, c0_:c1_], rb[0:n, 0:wd], sm[0:n, 32 + h:33 + h],
                                    score[0:n, c0_:c1_], ALU.mult, ALU.add, [R_rb, R_sm, R_score], [R_score])
                            if h % 4 == 3:
                                yield
                    P.op("dve", lambda e: e.tensor_reduce(out=sm[0:n, 40:41], in_=score[0:n, 0:L], axis=AX.X,
                                                          op=ALU.max, apply_absolute_value=True), [R_score], [R_sm])
                    if adm_fill is not None:
                        (r0, r1, fc0, fc1) = adm_fill
                        MEMSET("dve", score[r0:r1, fc0:fc1], -1e30, [R_score])
                    TS_("dve", sm[0:n, 41:42], sm[0:n, 40:41], 1.0, None, ALU.add, None, [R_sm], [R_sm])
                    TS_("dve", bis[0:n, 0:NBIS + 2], pow2[0:n, 0:NBIS + 2], sm[0:n, 41:42], None, ALU.mult, None,
                        [R_pow2, R_sm], [R_bis])
                    MEMSET("dve", sm[0:n, 42:43], 0.0, [R_sm])
                    yield
                    for k in range(1, NBIS + 1):
                        TS_("dve", mask[0:n, 0:L], score[0:n, 0:L], sm[0:n, 42:43], 0.0, ALU.is_ge, ALU.add,
                            [R_score, R_sm], [R_mask, R_sm], accum_out=sm[0:n, 43:44])
                        TS_("dve", sm[0:n, 44:45], sm[0:n, 43:44], float(TOPK), 0.5, ALU.is_ge, ALU.subtract,
                            [R_sm], [R_sm])
                        STT("dve", sm[0:n, 42:43], sm[0:n, 44:45], bis[0:n, k - 1:k], sm[0:n, 42:43], ALU.mult, ALU.add,
                            [R_sm, R_bis], [R_sm])
                        yield
                    STT("dve", sm[0:n, 45:46], bis[0:n, NBIS:NBIS + 1], -1.0, sm[0:n, 42:43], ALU.mult, ALU.add,
                        [R_sm, R_bis], [R_sm])
                    TS_("dve", mask[0:n, 0:L], score[0:n, 0:L], sm[0:n, 45:46], None, ALU.is_ge, None,
                        [R_score, R_sm], [R_mask])
                    yield
                else:
                    MEMSET("pool", mask[0:n, 0:L], 1.0, [R_mask])
                    if adm_fill is not None:
                        (r0, r1, fc0, fc1) = adm_fill
                        MEMSET("pool", mask[r0:r1, fc0:fc1], 0.0, [R_mask])
                nkt = ktile_idx + 1
                for k0 in range(0, nkt, 8):
                    k1 = min(nkt, k0 + 8)
                    bt_ = nb(fr)
                    pt = r3(psb(bt_), 8)
                    for kt in range(k0, k1):
                        kc_, nk_ = key_tiles[kt]
                        TR(pt[0:nk_, kt - k0, 0:n], mask[0:n, kc_:kc_ + nk_], ident_b[0:n, 0:n], [R_mask, R_identb],
                           [R_ps[bt_]])
                    ACT(maskT[:, k0:k1, 0:n], pt[:, 0:k1 - k0, 0:n], AF.Identity, [R_ps[bt_]], [R_maskT],
                        scale=-NEG, bias=NEG)
                    yield
                B["ctx"][t] = ctx

            def attn(t, ctx, B, ktile_idx, key_tiles, bias_kind, yT_dst, R_yT):
                kT, vb = B["kT"], B["vb"]
                R_k = B["R_k"]
                tname, col0, n = tiles[t]
                sm, R_sm, maskT, R_maskT, qT, R_qT = ctx.sm, ctx.R_sm, ctx.maskT, ctx.R_maskT, ctx.qT, ctx.R_qT
                nkt = ktile_idx + 1
                near = {kt: bias_kind[kt] for kt in bias_kind}
                far_tiles = [kt for kt in range(nkt) if kt not in near]
                groups = [far_tiles[i:i + 4] for i in range(0, len(far_tiles), 4)]
                near_tiles = sorted(near.keys())
                if near_tiles:
                    groups.append(near_tiles)
                ya, R_ya = ya_ring.next()
                items = [(h, gi) for h in range(8) for gi in range(len(groups))]
                pend = []
                state = {}

                def stage1(h, gi):
                    g = h // 4
                    grp = groups[gi]
                    isnear = grp[0] in near
                    b = nb("at")
                    ps3 = r3(psf(b), 4)
                    for j, kt in enumerate(grp):
                        kc_, nk_ = key_tiles[kt]
                        o = ps3[0:nk_, j, 0:n]
                        MM(o, kT[:, g, kc_:kc_ + nk_], qT[:, h, 0:n], True, False, [R_k[kt], R_qT], [R_ps[b]])
                        MM(o, ident_b[0:nk_, 0:nk_], maskT[0:nk_, kt, 0:n], False, not isnear,
                           [R_identb, R_maskT], [R_ps[b]])
                        if isnear:
                            kind = near[kt]
                            if kind == BT_META:
                                MM(o, ident_f[:, 112:112 + nk_], bt[:, BT_PREV, h, 0:n], False, True,
                                   [R_identf, R_bt], [R_ps[b]])
                            else:
                                MM(o, ident_f[0:nk_, 0:nk_], bt[0:nk_, kind, h, 0:n], False, True,
                                   [R_identf, R_bt], [R_ps[b]])
                    ng = len(grp)
                    pm2, R_pm = pm_ring.next()
                    pm = r3(pm2, 4)
                    if isnear:
                        ACT(pm[:, 0:ng, 0:n], ps3[:, 0:ng, 0:n], AF.Exp, [R_ps[b]], [R_pm])
                    else:
                        ACT(pm[:, 0:ng, 0:n], ps3[:, 0:ng, 0:n], AF.Exp, [R_ps[b], R_rowb], [R_pm],
                            bias=rowb[:, RB_CB + h:RB_CB + h + 1])
                    return (h, gi, pm, R_pm)

                def stage2(h, gi, pm, R_pm):
                    g = h // 4
                    grp = groups[gi]
                    if gi == 0:
                        state["bo"] = nb("o")
                    bo = state["bo"]
                    O = psf(bo)[0:n, 0:129]
                    for j, kt in enumerate(grp):
                        kc_, nk_ = key_tiles[kt]
                        first = (gi == 0 and j == 0)
                        last = (gi == len(groups) - 1 and j == len(grp) - 1)
                        MM(O, pm[0:nk_, j, 0:n], vb[0:nk_, kt, g, :], first, last, [R_pm, R_k[kt]], [R_ps[bo]])
                    if gi == len(groups) - 1:
                        RECIP(sm[0:n, 48 + h:49 + h], psf(bo)[0:n, 128:129], [R_ps[bo]], [R_sm])
                        ACT(ya[0:n, h * 128:(h + 1) * 128], psf(bo)[0:n, 0:128], AF.Identity, [R_ps[bo], R_sm], [R_ya],
                            scale=sm[0:n, 48 + h:49 + h])

                SKEW = 3
                for it in items:
                    pend.append(stage1(*it))
                    if len(pend) > SKEW:
                        stage2(*pend.pop(0))
                    yield
                while pend:
                    stage2(*pend.pop(0))
                bt_ = nb("at")
                pt = r3(psb(bt_), 8)
                for kc in range(KC):
                    TR(pt[:, kc, 0:n], ya[0:n, kc * 128:(kc + 1) * 128], ident_b[0:n, 0:n], [R_ya, R_identb], [R_ps[bt_]])
                CP("act", yT_dst, pt[:, :, 0:n], [R_ps[bt_]], [R_yT])
                yield

            def key_bufs(LMAX, NKT, nmask, pre_alloc=None):
                B = {"NKT": NKT, "ctx": {}}
                if pre_alloc is None:
                    B["kT"] = r3(A.alloc(2 * LMAX, BF16), 2)
                    B["vb"] = A.alloc(NKT * 2 * 129, BF16).rearrange("p (t g d) -> p t g d", t=NKT, g=2)
                else:
                    B["kT"] = r3(pre_alloc[:, 0:2 * LMAX], 2)
                    B["vb"] = pre_alloc[:, 2 * LMAX:2 * LMAX + NKT * 2 * 129].rearrange("p (t g d) -> p t g d",
                                                                                      t=NKT, g=2)
                B["kiT2"] = A.alloc(LMAX, BF16)
                B["score_off"] = A.mark()
                raws = [A.alloc(2 * LMAX, BF16) for _ in range(nmask)]
                B["score_end"] = A.mark()
                B["score_raw"] = raws[0]

                class _SR:
                    def __init__(self, items):
                        self.items = items
                        self.i = 0

                    def next(self):
                        it = self.items[self.i % len(self.items)]
                        self.i += 1
                        return it
                B["score_ring"] = _SR([(r_.bitcast(F32)[:, 0:LMAX], Res("score%d" % i)) for i, r_ in enumerate(raws)])
                B["mask_ring"] = Ring(A, LMAX, nmask, BF16, "mask")
                B["maskT_ring"] = Ring(A, NKT * 128, nmask + 1 if nmask > 1 else 1, BF16, "maskT")
                B["R_k"] = [Res("kt%d" % i) for i in range(NKT)]
                MEMSET("pool", B["vb"][:, :, :, 128:129], 1.0, B["R_k"])
                return B

            def gate_gen(t, hn, R_hn, yT, R_yT):
                tname, col0, n = tiles[t]
                for half in range(2):
                    bb = 7
                    pbr = r3(psf(bb), 4)
                    for cc in range(4):
                        c = half * 4 + cc
                        for kc in range(KC):
                            MM(pbr[:, cc, 0:n], wbrA[:, kc, c * 128:(c + 1) * 128], yT[:, kc, 0:n], kc == 0,
                               kc == KC - 1, [R_wbrA, R_yT], [R_ps[bb]])
                    yield
                    bg = 4
                    pg = r3(psf(bg), 4)
                    for cc in range(4):
                        c = half * 4 + cc
                        for kc in range(KC):
                            MM(pg[:, cc, 0:n], wgA[:, kc, c * 128:(c + 1) * 128], hn[:, kc, 0:n], kc == 0,
                               kc == KC - 1, [R_wgA, R_hn], [R_ps[bg]])
                    yield
                    sg2, R_sg = sgA_ring.next()
                    sg = r3(sg2, 4)
                    ACT(sg[:, :, 0:n], pg[:, :, 0:n], AF.Sigmoid, [R_ps[bg]], [R_sg])
                    yield
                    tm2, R_tm = tmA_ring.next()
                    tm = r3(tm2, 4)
                    TT("dve", tm[:, :, 0:n], pbr[:, :, 0:n], sg[:, :, 0:n], ALU.mult, [R_ps[bb], R_sg], [R_tm])
                    yield
                    dst = mergedT[:, half * 4:(half + 1) * 4, col0:col0 + n]
                    TT("pool", dst, dst, tm[:, :, 0:n], ALU.add, [R_tm, R_mg[t]], [R_mg[t]])
                    yield

            def interleave(streams):
                gens = [s[0] for s in streams]
                est = [max(1, s[1]) for s in streams]
                prog = [0] * len(gens)
                alive = [True] * len(gens)
                while any(alive):
                    best = None
                    for i in range(len(gens)):
                        if alive[i] and (best is None or prog[i] / est[i] < prog[best] / est[best]):
                            best = i
                    try:
                        next(gens[best])
                        prog[best] += 1
                    except StopIteration:
                        alive[best] = False

            def run_all(g):
                for _ in g:
                    pass

            LS = PAST + TS
            assert 2 * LS + 33 * 2 * 129 <= XBYTES // 2
            Bs = key_bufs(LS, 33, 1, pre_alloc=regX)
            R_kc = Bs["R_k"]
            R_stg = Bs["score_ring"].items[0][1]
            stg = Bs["score_raw"][:, 0:4096]
            stg_i = stg.rearrange("p (t r d) -> p t r d", t=32, r=2)
            P.dma("pool", stg_i[:, :, 0, :], cache_ki.rearrange("(t p) d -> p t d", p=128), writes=[R_stg], sem_res=R_stg)
            P.dma_more("pool", stg_i[:, :, 1, :], cache_ki.rearrange("(t p) d -> p t d", p=128), R_stg)
            load_w(wA, R_wA, w_in, C_Q, C_GS)
            for t0 in range(0, 32, 8):
                bt_ = nb("tr")
                pt = r3(psb(bt_), 8)
                for j in range(8):
                    TR(pt[:, j, :], stg_i[:, t0 + j].rearrange("p r d -> p (r d)"), ident_b, [R_stg, R_identb],
                       [R_ps[bt_]])
                CP("act" if (t0 // 8) % 2 == 0 else "dve", Bs["kiT2"][:, t0 * 128:(t0 + 8) * 128], psb(bt_), [R_ps[bt_]],
                   [R_kc[i] for i in range(t0, t0 + 8)])
            stgk2 = A.alloc(8 * 256, BF16); R_stgk = Res("stgk")
            stg_k = r3(stgk2, 8)
            bank_pools["frk"] = [4, 7]
            cnt["frk"] = 0

            def cache_kv_gen():
                for t0 in range(0, 32, 8):
                    cvv = cache_v.rearrange("(t p) (g d) -> p t g d", p=128, g=2)
                    P.dma("pool", Bs["vb"][:, t0:t0 + 8, 0, 0:128], cvv[:, t0:t0 + 8, 0, :],
                          writes=[R_kc[i] for i in range(t0, t0 + 8)], sem_res=R_kc[t0])
                    P.dma_more("pool", Bs["vb"][:, t0:t0 + 8, 1, 0:128], cvv[:, t0:t0 + 8, 1, :], R_kc[t0])
                    for i in range(t0 + 1, t0 + 8):
                        R_kc[i].w = dict(R_kc[t0].w)
                yield
                for t0 in range(0, 32, 8):
                    P.dma("pool", stg_k, cache_k.rearrange("(t p) c -> p t c", p=128)[:, t0:t0 + 8, :], writes=[R_stgk])
                    yield
                    for g in range(2):
                        bt_ = nb("frk")
                        pt = r3(psb(bt_), 8)
                        for j in range(8):
                            TR(pt[:, j, :], stg_k[:, j, g * 128:(g + 1) * 128], ident_b, [R_stgk, R_identb], [R_ps[bt_]])
                        yield
                        CP("act", Bs["kT"][:, g, t0 * 128:(t0 + 8) * 128], psb(bt_), [R_ps[bt_]],
                           [R_kc[i] for i in range(t0, t0 + 8)])
                        yield
            key_tiles_s = [(i * 128, 128) for i in range(32)] + [(PAST, TS)]
            bk_s = {31: BT_PREV, 32: BT_SAME}
            interleave([(front(TS_IDX, Bs, 32, key_tiles_s, bk_s, None,
                               dict(k=k_s, v=v_s, ki=ki_s, kn="k_s", vn="v_s", kin="ki_s"), "fr"), 60),
                        (cache_kv_gen(), 18)])
            run_all(attn(TS_IDX, Bs["ctx"][TS_IDX], Bs, 32, key_tiles_s, bk_s,
                         yattT_s[:, :, :], R_yattT[TS_IDX]))
            print("pass A(sample) arena peak", A.peak)
            A.release(commonA_mark)
            P.barrier()

            Bp = key_bufs(TP, 17, 2)
            key_tiles_p = [(0, 16)] + [(16 + 128 * (j - 1), 128) for j in range(1, 17)]
            NP_ = NT - 1
            bank_pools["fr0"] = [2, 3]
            bank_pools["fr1"] = [7, 4]
            cnt["fr0"] = 0
            cnt["fr1"] = 0

            def bias_kind_p(t):
                if t == 0:
                    return {0: BT_SAME}
                if t == 1:
                    return {0: BT_META, 1: BT_SAME}
                return {t - 1: BT_PREV, t: BT_SAME}

            def front_p(t):
                tname, col0, n = tiles[t]
                adm = (0, 64, col0 + 64, col0 + 128) if t >= 1 else None
                return front(t, Bp, t, key_tiles_p, bias_kind_p(t), adm,
                             dict(k=k_p[col0:col0 + n, :], v=v_p[col0:col0 + n, :], ki=ki_p[col0:col0 + n, :],
                                  kn="k_p", vn="v_p", kin="ki_p"), "fr%d" % (t % 2))

            def front_steps(t):
                L = key_tiles_p[t][0] + key_tiles_p[t][1]
                if L <= TOPK:
                    return 20
                return 20 + 2 * ((L + 511) // 512) + NBIS + 5 + (t + 8) // 8

            def limited(g, k):
                for _ in range(k):
                    try:
                        next(g)
                    except StopIteration:
                        return
                    yield

            def attn_p(t):
                tname, col0, n = tiles[t]
                return attn(t, Bp["ctx"][t], Bp, t, key_tiles_p, bias_kind_p(t), yattT[:, :, col0:col0 + n], R_yattT[t])

            run_all(front_p(0))
            fcur = front_p(1)
            run_all(limited(fcur, front_steps(1) // 2))
            for t in range(NP_):
                nfar_ = (t + 1) - len(bias_kind_p(t))
                streams = [(attn_p(t), 8 * ((nfar_ + 3) // 4 + 1) + 1)]
                fnext = None
                if t + 1 < NP_:
                    streams.append((fcur, max(1, front_steps(t + 1) - front_steps(t + 1) // 2)))
                if t + 2 < NP_:
                    fnext = front_p(t + 2)
                    streams.append((limited(fnext, front_steps(t + 2) // 2), front_steps(t + 2) // 2))
                interleave(streams)
                if t + 1 < NP_:
                    run_all(fcur)
                fcur = fnext
                if t == NP_ - 2:
                    wbrS = r3(wA_raw[:, 0:KC * D], KC)
                    wgS = r3(wA_raw[:, KC * D:2 * KC * D], KC)
                    for j_, (dst_, w_, c0_) in enumerate(((wbrS, w_br_ssd, 0), (wgS, w_in, C_GS))):
                        if j_ == 0:
                            P.dma("pool", dst_, wsrc(w_, c0_, c0_ + D), writes=[R_wA], sem_res=R_wA)
                        else:
                            P.dma_more("pool", dst_, wsrc(w_, c0_, c0_ + D), R_wA)
                    assert Bp["score_end"] - Bp["score_off"] >= KC * D * 2
                    wbrA = r3(arena_t[:, Bp["score_off"] // 2: Bp["score_off"] // 2 + KC * D], KC)
                    R_scs = [it[1] for it in Bp["score_ring"].items]
                    P.dma("pool", wbrA, wsrc(w_br_att, 0, D), writes=R_scs, sem_res=R_scs[0])
            print("pass A arena peak", A.peak)
            A.release(afterX_mark)
            P.barrier()
            assert 2 * KC * D <= KC * NWA
            R_wbrS = R_wA
            R_wgS = R_wA
            sgM_ring = Ring(A, 512, 4, F32, "sgM")
            assert A.mark() <= wA_off, (A.mark(), wA_off)
            A.off = wA_end
            hnM = HN(V_N1W)
            R_wbrA = Res("wbrA")
            wgA = r3(A.alloc(KC * D, BF16), KC); R_wgA = Res("wgA")
            load_w(wgA, R_wgA, w_in, C_GA, C_GA + D)
            WO_OFF = (A.nbytes - KC * D * 2) // 64 * 64
            wo = r3(arena_t[:, WO_OFF // 2: WO_OFF // 2 + KC * D], KC); R_wo = Res("wo")
            load_w(wo, R_wo, w_out, 0, D)
            tmM_ring = Ring(A, 512, 4, F32, "tmM")
            for par in range(2):
                bank_pools["m_mm%d" % par] = [0, 1, 2] if par == 0 else [4, 5, 6]
                bank_pools["m_tr%d" % par] = [3] if par == 0 else [7]
                cnt["m_mm%d" % par] = 0
                cnt["m_tr%d" % par] = 0

            def gateM(t, par):
                mmp = "m_mm%d" % par
                tname, col0, n = tiles[t]
                hn, R_hn = yield from hnM.tile_gen(t, pool="m_tr%d" % par)
                yield
                mg = mergedT[:, :, col0:col0 + n]

                def mm32(bank, w, R_w, src, R_src, half):
                    p4 = r3(psf(bank), 4)
                    for cc in range(4):
                        c = half * 4 + cc
                        for kc in range(KC):
                            MM(p4[:, cc, 0:n], w[:, kc, c * 128:(c + 1) * 128], src[:, kc, 0:n], kc == 0, kc == KC - 1,
                               [R_w, R_src], [R_ps[bank]])
                    return p4
                ba = nb(mmp); pa = mm32(ba, wbrS, R_wbrS, mg, R_mg[t], 0)
                bb = nb(mmp); pb_ = mm32(bb, wbrS, R_wbrS, mg, R_mg[t], 1)
                yield
                for half, (bbr, pbr) in enumerate(((ba, pa), (bb, pb_))):
                    bg = nb(mmp); pg = mm32(bg, wgS, R_wgS, hn, R_hn, half)
                    yield
                    sg2, R_sg = sgM_ring.next(); sg = r3(sg2, 4)
                    ACT(sg[:, :, 0:n], pg[:, :, 0:n], AF.Sigmoid, [R_ps[bg]], [R_sg])
                    yield
                    TT("dve", mergedT[:, half * 4:(half + 1) * 4, col0:col0 + n], pbr[:, :, 0:n], sg[:, :, 0:n], ALU.mult,
                       [R_ps[bbr], R_sg], [R_mg[t]])
                    yield
                yT = yattT_s if t == TS_IDX else yattT[:, :, col0:col0 + n]
                for half in range(2):
                    bbr = nb(mmp); pbr = mm32(bbr, wbrA, R_wbrA, yT, R_yattT[t], half)
                    bg = nb(mmp); pg = mm32(bg, wgA, R_wgA, hn, R_hn, half)
                    yield
                    sg2, R_sg = sgM_ring.next(); sg = r3(sg2, 4)
                    ACT(sg[:, :, 0:n], pg[:, :, 0:n], AF.Sigmoid, [R_ps[bg]], [R_sg])
                    yield
                    tm2, R_tm = tmM_ring.next(); tm = r3(tm2, 4)
                    TT("dve", tm[:, :, 0:n], pbr[:, :, 0:n], sg[:, :, 0:n], ALU.mult, [R_ps[bbr], R_sg], [R_tm])
                    yield
                    dst = mergedT[:, half * 4:(half + 1) * 4, col0:col0 + n]
                    TT("pool", dst, dst, tm[:, :, 0:n], ALU.add, [R_tm, R_mg[t]], [R_mg[t]])
                    yield

            def run_two_m(gens, lag):
                gens = list(gens)
                active = []
                nxt = 0
                while nxt < len(gens) or active:
                    if not active or (len(active) == 1 and nxt < len(gens) and active[0][1] >= lag):
                        if nxt < len(gens):
                            active.append([gens[nxt], 0])
                            nxt += 1
                    for a_ in list(active):
                        try:
                            next(a_[0])
                            a_[1] += 1
                        except StopIteration:
                            active.remove(a_)
            run_two_m([gateM(t, t % 2) for t in range(NT)], 5)
            hnM.pool = "tr"
            print("pass M arena peak", A.peak)
            assert A.peak_since_release <= Bp["score_off"], (A.peak_since_release, Bp["score_off"])
            assert Bp["score_end"] <= WO_OFF
            A.release(persist_mark)
            P.barrier()

        if stage >= 4:
            h_all = r3(A.alloc(NT * D, F32), NT); R_h = [Res("h%d" % t) for t in range(NT)]
            hn2T = r3(A.alloc(KC * NCOL, BF16), KC); R_hn2 = [Res("hn2_%d" % t) for t in range(NT)]
            markO = A.mark()
            wgt_ring = Ring(A, KC * 256, 2, BF16, "wgt")
            wup_ring = Ring(A, KC * 256, 2, BF16, "wup")
            markO2 = A.mark()
            hnO = HN(V_N2W, nxt=2, nhn=0)
            pre_pairs = []
            for fp0 in (0, 2):
                wgt2_, R_wgt_ = wgt_ring.next()
                wup2_, R_wup_ = wup_ring.next()
                P.dma("pool", r3(wgt2_, KC)[:, :, 0:256], wsrc(w_gate, fp0 * 128, (fp0 + 2) * 128), writes=[R_wgt_])
                P.dma("pool", r3(wup2_, KC)[:, :, 0:256], wsrc(w_up, fp0 * 128, (fp0 + 2) * 128), writes=[R_wup_])
                pre_pairs.append((wgt2_, R_wgt_, wup2_, R_wup_))
            for par in range(2):
                bank_pools["o_mm%d" % par] = [0, 1] if par == 0 else [4, 5]
                bank_pools["o_tr%d" % par] = [2] if par == 0 else [6]
                cnt["o_mm%d" % par] = 0
                cnt["o_tr%d" % par] = 0

            def passO_gen(t, par):
                tname, col0, n = tiles[t]
                xt, R_xt = hnO.xt.next()
                P.dma("sp", xt[0:n, :], xin[col0:col0 + n, :], writes=[R_xt])
                banks = [nb("o_mm%d" % par), nb("o_mm%d" % par)]
                for q in range(2):
                    b = banks[q]
                    for kc in range(KC):
                        MM(psf(b)[0:n, :], mergedT[:, kc, col0:col0 + n], wo[:, kc, q * 512:(q + 1) * 512], kc == 0,
                           kc == KC - 1, [R_mg[t], R_wo], [R_ps[b]])
                yield
                for q in range(2):
                    b = banks[q]
                    TT("dve", h_all[0:n, t, q * 512:(q + 1) * 512], psf(b)[0:n, :], xt[0:n, q * 512:(q + 1) * 512],
                       ALU.add, [R_ps[b], R_xt], [R_h[t]])
                yield
                yield from hnO.tile_gen(t, src=h_all[:, t, :], R_src=R_h[t], dst=hn2T[:, :, col0:col0 + n],
                                        R_dst=R_hn2[t], pool="o_tr%d" % par)
                yield
            run_two_m([passO_gen(t, t % 2) for t in range(NT)], 4)
            print("pass O arena peak", A.peak)
            assert A.peak_since_release <= WO_OFF, (A.peak_since_release, WO_OFF)
            A.release(markO2)
            A.nbytes = ARENA_BYTES
            P.barrier()

            slices = [(0, 8), (8, 15), (15, 22)]
            NFMAX = 8
            actT = r3(A.alloc(NFMAX * NCOL, BF16), NFMAX); R_act = [Res("act%d" % c) for c in range(NFMAX)]
            wd_ring = Ring(A, NFMAX * D, 1, BF16, "wd")
            s_ring = Ring(A, 512, 2, F32, "silu")
            blocks = [(0, 512), (512, 1024), (1024, 1536), (1536, 2048), (2048, NCOL)]

            def tiles_in(c0, c1):
                return [t for t, (_, col0, n) in enumerate(tiles) if col0 < c1 and col0 + n > c0]
            for (f0, f1) in slices:
                nf = f1 - f0
                for fp in range(f0, f1, 2):
                    npair = min(2, f1 - fp)
                    if pre_pairs:
                        assert npair == 2
                        wgt2, R_wgt, wup2, R_wup = pre_pairs.pop(0)
                        wgt = r3(wgt2, KC)
                        wup = r3(wup2, KC)
                    else:
                        wgt2, R_wgt = wgt_ring.next()
                        wup2, R_wup = wup_ring.next()
                        wgt = r3(wgt2, KC)
                        wup = r3(wup2, KC)
                        P.dma("pool", wgt[:, :, 0:npair * 128], wsrc(w_gate, fp * 128, (fp + npair) * 128), writes=[R_wgt])
                        P.dma("pool", wup[:, :, 0:npair * 128], wsrc(w_up, fp * 128, (fp + npair) * 128), writes=[R_wup])
                    for ci in range(npair):
                        cl = fp + ci - f0
                        for (c0, c1) in blocks:
                            wdt = c1 - c0
                            rt = [R_hn2[t] for t in tiles_in(c0, c1)]
                            bg = nb("mm")
                            bu = nb("mm")
                            for kc in range(KC):
                                MM(psf(bg)[:, 0:wdt], wgt[:, kc, ci * 128:(ci + 1) * 128], hn2T[:, kc, c0:c1], kc == 0,
                                   kc == KC - 1, [R_wgt] + rt, [R_ps[bg]])
                            for kc in range(KC):
                                MM(psf(bu)[:, 0:wdt], wup[:, kc, ci * 128:(ci + 1) * 128], hn2T[:, kc, c0:c1], kc == 0,
                                   kc == KC - 1, [R_wup] + rt, [R_ps[bu]])
                            sl_, R_sl = s_ring.next()
                            ACT(sl_[:, 0:wdt], psf(bg)[:, 0:wdt], AF.Silu, [R_ps[bg]], [R_sl])
                            TT("dve", actT[:, cl, c0:c1], psf(bu)[:, 0:wdt], sl_[:, 0:wdt], ALU.mult,
                               [R_ps[bu], R_sl], [R_act[cl]])
                wd2, R_wd = wd_ring.next()
                wd = r3(wd2, NFMAX)
                P.dma("pool", wd[:, 0:nf, :], w_down.rearrange("(c p) d -> p c d", p=128)[:, f0:f1, :], writes=[R_wd])
                for t in range(NT):
                    tname, col0, n = tiles[t]
                    for q in range(2):
                        b = nb("o")
                        for cl in range(nf):
                            MM(psf(b)[0:n, :], actT[:, cl, col0:col0 + n], wd[:, cl, q * 512:(q + 1) * 512], cl == 0,
                               cl == nf - 1, [R_act[cl], R_wd], [R_ps[b]])
                        TT("dve", h_all[0:n, t, q * 512:(q + 1) * 512], h_all[0:n, t, q * 512:(q + 1) * 512],
                           psf(b)[0:n, :], ALU.add, [R_ps[b], R_h[t]], [R_h[t]])
            for t in range(1, NT):
                tname, col0, n = tiles[t]
                if t == TS_IDX:
                    P.dma("sp", y_s[:, :], h_all[0:n, t, :], reads=[R_h[t]], writes=[out_res["y_s"]], sem_res=R_h[t])
                else:
                    P.dma("sp", y_p[col0 - 16:col0 - 16 + n, :], h_all[0:n, t, :], reads=[R_h[t]],
                          writes=[out_res["y_p"]], sem_res=R_h[t])
            print("pass F arena peak", A.peak)

        P.final = [out_res[k] for k in out_names]
        P.emit(st)
    return nc


def _static_consts():
    c = {}
    c["ident"] = np.eye(128, dtype=np.float32)
    k = np.arange(128)[:, None]
    i = np.arange(128)[None, :]
    U = (k <= i).astype(np.float32)
    SL = (k > i).astype(np.float32)
    ones = np.ones((128, 128), np.float32)
    c["tri"] = np.ascontiguousarray(np.concatenate([U, SL, ones], axis=1))
    c["nm"] = np.where(i < k, np.float32(NEG), np.float32(0.0)).astype(np.float32)
    c["pow2"] = (2.0 ** -np.arange(32, dtype=np.float64)).astype(np.float32)[None, :]
    return c


def _bias_tables(rel_bias):
    ss = np.arange(128)[:, None]
    ii = np.arange(128)[None, :]
    tabs = []
    for off in (-128, 0):
        tabs.append(rel_bias[t5_bucket_np(ss + off - ii)])
    bt = np.stack(tabs, axis=1)
    bt = bt.transpose(0, 1, 3, 2)
    return np.ascontiguousarray(bt.reshape(128, -1)).astype(np.float32)


def make_in_maps(inputs, cores):
    g = lambda k: np.asarray(inputs[k], dtype=np.float32)
    x_prompt, x_sample = g("x_prompt"), g("x_sample")
    meta = g("meta_tokens")
    consts = _static_consts()
    rel_bias = g("rel_bias")
    bt = _bias_tables(rel_bias)
    vecs = np.zeros((128, 64), np.float32)
    vecs[:, 0:8] = g("norm1_w")[0].reshape(8, 128).T
    vecs[:, 8:16] = g("norm2_w")[0].reshape(8, 128).T
    vecs[:, 16:24] = g("ssd_norm_w")[0].reshape(8, 128).T
    vecs[:, 24] = g("q_norm_w")[0]
    rows = np.zeros((1, 512), np.float32)
    rows[0, 0:128] = g("k_norm_w")[0]
    rows[0, 128:192] = g("idx_k_norm_w")[0]
    rows[0, 192:208] = g("dt_bias")[0]
    rows[0, 208:224] = g("a_log")[0]
    rows[0, 224:240] = g("d_skip")[0]
    rows[0, 240:248] = rel_bias[15]
    convw = np.ascontiguousarray(g("conv_w")[0].reshape(4, 16, 128).transpose(2, 0, 1).reshape(128, 64))
    convb = np.ascontiguousarray(g("conv_b")[0].reshape(16, 128).T)
    shared = dict(
        w_in=g("w_in")[0], w_br_ssd=g("w_br_ssd")[0], w_br_att=g("w_br_att")[0], w_out=g("w_out")[0],
        w_gate=g("w_gate")[0], w_up=g("w_up")[0], w_down=g("w_down")[0],
        vecs=vecs, rows=rows, convw=convw, convb=convb, ident=consts["ident"], tri=consts["tri"], nm=consts["nm"],
        bt=bt, pow2=consts["pow2"])
    in_maps = []
    for b in cores:
        m = dict(shared)
        m["xin"] = np.ascontiguousarray(np.concatenate([meta, x_prompt[b], x_sample[b]], axis=0))
        m["cache_k"] = np.ascontiguousarray(g("cache_k")[0, b].reshape(PAST, 256))
        m["cache_v"] = np.ascontiguousarray(g("cache_v")[0, b].reshape(PAST, 256))
        m["cache_ki"] = np.ascontiguousarray(g("cache_kidx")[0, b])
        m["state_ssm"] = np.ascontiguousarray(g("state_ssm")[0, b].reshape(1024, 128))
        m["state_conv"] = np.ascontiguousarray(g("state_conv")[0, b])
        in_maps.append(m)
    return in_maps


_NC_CACHE = {}


def kernel(**inputs):
    if "nc" not in _NC_CACHE:
        _NC_CACHE["nc"] = build_program()
    nc = _NC_CACHE["nc"]
    cores = list(range(8))
    in_maps = make_in_maps(inputs, cores)
    res = run_bass_kernel_spmd(nc, in_maps, core_ids=cores)
    r = res.results
    st = lambda name: np.stack([np.asarray(r[b][name], dtype=np.float32) for b in cores], axis=0)
    y_prompt = st("y_p")
    y_sample = st("y_s")
    k_prompt = st("k_p").reshape(1, 8, TP, 2, 128)
    v_prompt = st("v_p").reshape(1, 8, TP, 2, 128)
    kidx_prompt = st("ki_p").reshape(1, 8, TP, 64)
    ssm_prompt = st("ssm_p").reshape(1, 8, 16, 64, 128)
    conv_prompt = st("conv_p").reshape(1, 8, 3, 2048)
    k_sample = st("k_s").reshape(1, 8, TS, 2, 128)
    v_sample = st("v_s").reshape(1, 8, TS, 2, 128)
    kidx_sample = st("ki_s").reshape(1, 8, TS, 64)
    ssm_sample = st("ssm_s").reshape(1, 8, 16, 64, 128)
    conv_sample = st("conv_s").reshape(1, 8, 3, 2048)
    return (y_prompt, y_sample, k_prompt, v_prompt, kidx_prompt, ssm_prompt, conv_prompt,
            k_sample, v_sample, kidx_sample, ssm_sample, conv_sample)
```
